# Optimizing a Trainium2 kernel written in Bass

```python
import jax, jax.numpy as jnp
from jax import lax
import numpy as np

D_MODEL = 1024
BATCH = 8
SEQ = 2048
DEPTH = 1
DEC_BATCH = 128
DEC_SEQ = 8
PAST_LEN = 16384
PAGE_SIZE = 128

D_CONV = 512
CONV_WIDTH = 31
HG_HEADS = 4
HG_DK = 128
HG_DV = 128
D_HGRN = HG_HEADS * HG_DK
MEM_HEADS = 4
MEM_HEAD_DIM = 128
D_MEM = MEM_HEADS * MEM_HEAD_DIM
N_MEM = 256
N_BRANCH = 3
D_FF = -(-(8 * D_MODEL) // (3 * 256)) * 256
CHUNK = 32
EPS = 1e-6
D_IN = 2 * D_CONV + 4 * D_HGRN + D_MEM + N_BRANCH * D_MODEL

kernel_name = 'gated_conformer_hgrn2_memory_decoder_step'


def _rmsnorm(x, gain):
    xf = x.astype(jnp.float32)
    y = xf * lax.rsqrt(jnp.mean(xf * xf, axis=-1, keepdims=True) + EPS)
    return (y * gain.astype(jnp.float32)).astype(x.dtype)


def _layernorm(x, gain, bias):
    xf = x.astype(jnp.float32)
    mu = jnp.mean(xf, axis=-1, keepdims=True)
    var = jnp.mean(jnp.square(xf - mu), axis=-1, keepdims=True)
    y = (xf - mu) * lax.rsqrt(var + EPS)
    return (y * gain.astype(jnp.float32) + bias.astype(jnp.float32)).astype(x.dtype)


def _causal_dwconv(u_ext, conv_w, conv_b):
    out = lax.conv_general_dilated(
        u_ext, conv_w[:, None, :].astype(u_ext.dtype), window_strides=(1,), padding='VALID',
        dimension_numbers=('NWC', 'WIO', 'NWC'), feature_group_count=u_ext.shape[-1])
    return out + conv_b.astype(out.dtype)


def _hgrn2_chunkwise(q, k, v, logf, s0):
    B, L = q.shape[0], q.shape[1]
    n_chunks = -(-L // CHUNK)
    pad = n_chunks * CHUNK - L

    def prep(t):
        t = jnp.pad(t, ((0, 0), (0, pad), (0, 0), (0, 0)))
        return t.reshape((B, n_chunks, CHUNK) + t.shape[2:])

    q, k, v, logf = prep(q), prep(k), prep(v), prep(logf)
    b = jnp.cumsum(logf, axis=2)
    b_last = b[:, :, -1:]
    q_dec = q * jnp.exp(b)
    k_inv = k * jnp.exp(-b)
    k_end = k * jnp.exp(b_last - b)
    causal = jnp.tril(jnp.ones((CHUNK, CHUNK), dtype=bool))
    scores = jnp.where(causal, jnp.einsum('bnthd,bnshd->bnhts', q_dec, k_inv), 0.0)
    o_intra = jnp.einsum('bnhts,bnshv->bnthv', scores, v)
    chunk_kv = jnp.einsum('bnshd,bnshv->bnhdv', k_end, v)
    decay = jnp.exp(b[:, :, -1])

    def step(S, xs):
        qd, dec, kv = xs
        o = jnp.einsum('bthd,bhdv->bthv', qd, S)
        S = dec[..., None] * S + kv
        return S, o

    xs = (jnp.moveaxis(q_dec, 1, 0), jnp.moveaxis(decay, 1, 0), jnp.moveaxis(chunk_kv, 1, 0))
    s_final, o_inter = lax.scan(step, s0, xs)
    o = o_intra + jnp.moveaxis(o_inter, 0, 1)
    o = o.reshape((B, n_chunks * CHUNK) + o.shape[3:])[:, :L]
    return o, s_final


def _memory_kv(mem, mem_norm_g, w_mem_kv):
    m = _rmsnorm(mem, mem_norm_g)
    kv = jnp.einsum('bmd,de->bme', m, w_mem_kv)
    k, v = jnp.split(kv, 2, axis=-1)
    shp = (mem.shape[0], mem.shape[1], MEM_HEADS, MEM_HEAD_DIM)
    return k.reshape(shp), v.reshape(shp)


def _mixer(h, conv_prefix, hg_state, mem_k, mem_v, lb, w_in, conv_w, conv_b, conv_ln_g, conv_ln_b,
           w_conv_out, hg_norm_g, w_hg_out, w_mem_out, w_out):
    B, L, _ = h.shape
    z = jnp.einsum('bld,de->ble', h, w_in)
    sizes = (D_CONV, D_CONV, D_HGRN, D_HGRN, D_HGRN, D_HGRN, D_MEM)
    idx = [int(i) for i in np.cumsum(sizes)]
    ca, cb, hq, hf, hi, hgate, mq, gl = jnp.split(z, idx, axis=-1)

    u = ca * jax.nn.sigmoid(cb)
    u_ext = jnp.concatenate([conv_prefix.astype(u.dtype), u], axis=1)
    c = _causal_dwconv(u_ext, conv_w, conv_b)
    c = jax.nn.silu(_layernorm(c, conv_ln_g, conv_ln_b))
    p_conv = jnp.einsum('blc,cd->bld', c, w_conv_out)
    new_conv = u_ext[:, u_ext.shape[1] - (CONV_WIDTH - 1):]

    fx = hf.astype(jnp.float32).reshape(B, L, HG_HEADS, HG_DK)
    lbh = lb.reshape(HG_HEADS, HG_DK)
    logf = jnp.log(lbh + (1.0 - lbh) * jax.nn.sigmoid(fx))
    k = (1.0 - lbh) * jax.nn.sigmoid(-fx)
    q = jax.nn.silu(hq.astype(jnp.float32)).reshape(B, L, HG_HEADS, HG_DK)
    v = hi.astype(jnp.float32).reshape(B, L, HG_HEADS, HG_DV)
    o, new_hg = _hgrn2_chunkwise(q, k, v, logf, hg_state.astype(jnp.float32))
    o = _rmsnorm(o, hg_norm_g).astype(h.dtype).reshape(B, L, D_HGRN) * jax.nn.silu(hgate)
    p_hg = jnp.einsum('ble,ed->bld', o, w_hg_out)

    qm = mq.reshape(B, L, MEM_HEADS, MEM_HEAD_DIM)
    s = jnp.einsum('blhd,bmhd->bhlm', qm, mem_k).astype(jnp.float32) * (MEM_HEAD_DIM ** -0.5)
    p = jax.nn.softmax(s, axis=-1).astype(mem_v.dtype)
    om = jnp.einsum('bhlm,bmhd->blhd', p, mem_v).reshape(B, L, D_MEM)
    p_mem = jnp.einsum('ble,ed->bld', om, w_mem_out)

    g = jax.nn.sigmoid(gl.reshape(B, L, N_BRANCH, D_MODEL))
    merged = g[:, :, 0] * p_conv + g[:, :, 1] * p_hg + g[:, :, 2] * p_mem
    out = jnp.einsum('bld,de->ble', merged, w_out)
    return out, new_conv, new_hg.astype(hg_state.dtype)


def _layer(x, conv_prefix, hg_state, mem_k, mem_v, lb, norm_pre_mix, norm_post_mix, norm_pre_ffn,
           norm_post_ffn, w_in, conv_w, conv_b, conv_ln_g, conv_ln_b, w_conv_out, hg_norm_g, w_hg_out,
           w_mem_out, w_out, w_ffn_gate, w_ffn_up, w_ffn_down):
    h = _rmsnorm(x, norm_pre_mix)
    m, new_conv, new_hg = _mixer(h, conv_prefix, hg_state, mem_k, mem_v, lb, w_in, conv_w, conv_b,
                                 conv_ln_g, conv_ln_b, w_conv_out, hg_norm_g, w_hg_out, w_mem_out, w_out)
    x = x + _rmsnorm(m, norm_post_mix)
    h = _rmsnorm(x, norm_pre_ffn)
    f = jax.nn.silu(jnp.einsum('bld,df->blf', h, w_ffn_gate)) * jnp.einsum('bld,df->blf', h, w_ffn_up)
    f = jnp.einsum('blf,fd->bld', f, w_ffn_down)
    x = x + _rmsnorm(f, norm_post_ffn)
    return x, new_conv, new_hg


def setup_inputs(seed: int = 0) -> dict:
    key = jax.random.key(seed)
    ks = iter(jax.random.split(key, 32))

    def nrm(shape, scale):
        return jax.random.normal(next(ks), shape, jnp.float32) * scale

    def gain(shape):
        return 1.0 + nrm(shape, 0.02)

    return {
        'x_prompt': nrm((BATCH, SEQ, D_MODEL), 1.0),
        'x_sample': nrm((DEC_BATCH, DEC_SEQ, D_MODEL), 1.0),
        'mem_prompt': nrm((BATCH, N_MEM, D_MODEL), 1.0),
        'state_conv': nrm((DEPTH, DEC_BATCH, CONV_WIDTH - 1, D_CONV), 0.5),
        'state_hgrn': nrm((DEPTH, DEC_BATCH, HG_HEADS, HG_DK, HG_DV), 0.3),
        'cache_mem_k': nrm((DEPTH, DEC_BATCH, N_MEM, MEM_HEADS, MEM_HEAD_DIM), 1.0),
        'cache_mem_v': nrm((DEPTH, DEC_BATCH, N_MEM, MEM_HEADS, MEM_HEAD_DIM), 1.0),
        'norm_pre_mix': gain((DEPTH, D_MODEL)),
        'norm_post_mix': gain((DEPTH, D_MODEL)),
        'norm_pre_ffn': gain((DEPTH, D_MODEL)),
        'norm_post_ffn': gain((DEPTH, D_MODEL)),
        'w_in': nrm((DEPTH, D_MODEL, D_IN), D_MODEL ** -0.5),
        'conv_w': nrm((DEPTH, CONV_WIDTH, D_CONV), CONV_WIDTH ** -0.5),
        'conv_b': nrm((DEPTH, D_CONV), 0.02),
        'conv_ln_g': gain((DEPTH, D_CONV)),
        'conv_ln_b': nrm((DEPTH, D_CONV), 0.02),
        'w_conv_out': nrm((DEPTH, D_CONV, D_MODEL), D_CONV ** -0.5),
        'hg_lb_logits': nrm((DEPTH + 1, D_HGRN), 0.5),
        'hg_norm_g': gain((DEPTH, HG_DV)),
        'w_hg_out': nrm((DEPTH, D_HGRN, D_MODEL), D_HGRN ** -0.5),
        'mem_norm_g': gain((DEPTH, D_MODEL)),
        'w_mem_kv': nrm((DEPTH, D_MODEL, 2 * D_MEM), D_MODEL ** -0.5),
        'w_mem_out': nrm((DEPTH, D_MEM, D_MODEL), D_MEM ** -0.5),
        'w_out': nrm((DEPTH, D_MODEL, D_MODEL), D_MODEL ** -0.5),
        'w_ffn_gate': nrm((DEPTH, D_MODEL, D_FF), D_MODEL ** -0.5),
        'w_ffn_up': nrm((DEPTH, D_MODEL, D_FF), D_MODEL ** -0.5),
        'w_ffn_down': nrm((DEPTH, D_FF, D_MODEL), D_FF ** -0.5),
    }


def reference(x_prompt, x_sample, mem_prompt, state_conv, state_hgrn, cache_mem_k, cache_mem_v,
              norm_pre_mix, norm_post_mix, norm_pre_ffn, norm_post_ffn, w_in, conv_w, conv_b,
              conv_ln_g, conv_ln_b, w_conv_out, hg_lb_logits, hg_norm_g, w_hg_out, mem_norm_g,
              w_mem_kv, w_mem_out, w_out, w_ffn_gate, w_ffn_up, w_ffn_down):
    lb_all = jnp.cumsum(jax.nn.softmax(hg_lb_logits.astype(jnp.float32), axis=0), axis=0)
    xp, xs = x_prompt, x_sample
    conv_p, hg_p, mk_p, mv_p, conv_s, hg_s = [], [], [], [], [], []
    for l in range(DEPTH):
        shared = (norm_pre_mix[l], norm_post_mix[l], norm_pre_ffn[l], norm_post_ffn[l], w_in[l],
                  conv_w[l], conv_b[l], conv_ln_g[l], conv_ln_b[l], w_conv_out[l], hg_norm_g[l],
                  w_hg_out[l], w_mem_out[l], w_out[l], w_ffn_gate[l], w_ffn_up[l], w_ffn_down[l])
        lb = lb_all[l]
        mk, mv = _memory_kv(mem_prompt, mem_norm_g[l], w_mem_kv[l])
        prefix0 = jnp.zeros((xp.shape[0], CONV_WIDTH - 1, D_CONV), xp.dtype)
        s00 = jnp.zeros((xp.shape[0], HG_HEADS, HG_DK, HG_DV), xp.dtype)
        xp, nc_p, nh_p = _layer(xp, prefix0, s00, mk, mv, lb, *shared)
        xs, nc_s, nh_s = _layer(xs, state_conv[l], state_hgrn[l], cache_mem_k[l], cache_mem_v[l], lb, *shared)
        conv_p.append(nc_p)
        hg_p.append(nh_p)
        mk_p.append(mk)
        mv_p.append(mv)
        conv_s.append(nc_s)
        hg_s.append(nh_s)
    return (xp, xs, jnp.stack(conv_p), jnp.stack(hg_p), jnp.stack(mk_p), jnp.stack(mv_p),
            jnp.stack(conv_s), jnp.stack(hg_s))
```

```python
import os
import math
from collections import deque
from contextlib import ExitStack

import numpy as np
import ml_dtypes

import concourse.bass as bass
import concourse.mybir as mybir
from concourse.bass_utils import run_bass_kernel_spmd

F32 = mybir.dt.float32
BF16 = mybir.dt.bfloat16
AF = mybir.ActivationFunctionType
ALU = mybir.AluOpType
AX = mybir.AxisListType

ENGS = ("pe", "act", "dve", "pool", "sp")
EPS = 1e-6
NCORES = 8
T_ALL = 2176
D_IN = 6656


class Op:
    __slots__ = ("eng", "fn", "deps", "is_dma", "sem_key", "needs_inc", "inc_val", "idx", "clock")

    def __init__(self, eng, fn, is_dma, sem_key):
        self.eng = eng
        self.fn = fn
        self.deps = []
        self.is_dma = is_dma
        self.sem_key = sem_key
        self.needs_inc = False
        self.inc_val = None
        self.clock = None


class Prog:
    def __init__(self, nc):
        self.nc = nc
        self.ops = []
        self.last_write = {}
        self.readers = {}

    def op(self, eng, fn, reads=(), writes=(), dma_key=None):
        o = Op(eng, fn, dma_key is not None, (eng, dma_key) if dma_key is not None else None)
        o.idx = len(self.ops)
        deps = {}
        for r in reads:
            w = self.last_write.get(r)
            if w is not None:
                deps[w.idx] = w
        for r in writes:
            w = self.last_write.get(r)
            if w is not None:
                deps[w.idx] = w
            for rd in self.readers.get(r, ()):
                deps[rd.idx] = rd
        o.deps = list(deps.values())
        for r in reads:
            self.readers.setdefault(r, []).append(o)
        for r in writes:
            self.last_write[r] = o
            self.readers[r] = []
        self.ops.append(o)
        return o

    def emit(self, stack):
        nc = self.nc
        for o in self.ops:
            o.deps = [d for d in o.deps if not (d.eng == "pe" and o.eng == "pe" and not d.is_dma and not o.is_dma)]
            for d in o.deps:
                d.needs_inc = True
        sems = {e: stack.enter_context(nc.semaphore("s_" + e)) for e in ENGS}
        dma_sems = {}
        counts = {e: 0 for e in ENGS}
        dma_counts = {}
        for o in self.ops:
            if o.is_dma:
                if o.sem_key not in dma_sems:
                    dma_sems[o.sem_key] = stack.enter_context(nc.semaphore("d%d" % len(dma_sems)))
                    dma_counts[o.sem_key] = 0
                dma_counts[o.sem_key] += 16
                o.inc_val = (dma_sems[o.sem_key], dma_counts[o.sem_key])
            elif o.needs_inc:
                counts[o.eng] += 1
                o.inc_val = (sems[o.eng], counts[o.eng])
        self.n_dma_sems = len(dma_sems)
        eng_clock = {e: {} for e in ENGS}
        waits = {}
        for o in self.ops:
            ck = eng_clock[o.eng]
            wm = {}
            for d in sorted(o.deps, key=lambda d: d.idx):
                sem, val = d.inc_val
                if ck.get(sem, 0) >= val:
                    continue
                if wm.get(sem, 0) < val:
                    wm[sem] = val
                for s, v in d.clock.items():
                    if ck.get(s, 0) < v:
                        ck[s] = v
            waits[o.idx] = list(wm.items())
            oc = dict(ck)
            if o.inc_val is not None:
                s, v = o.inc_val
                if oc.get(s, 0) < v:
                    oc[s] = v
            o.clock = oc
        final = [(dma_sems[k], dma_counts[k]) for k in dma_sems]
        final += [(sems[e], counts[e]) for e in ENGS if counts[e] > 0]
        self.n_waits = sum(len(w) for w in waits.values())
        by_eng = {e: [o for o in self.ops if o.eng == e] for e in ENGS}
        with nc.Block() as block:
            def mk(ename):
                def body(eng):
                    for o in by_eng[ename]:
                        for s, v in waits[o.idx]:
                            eng.wait_ge(s, v)
                        ins = o.fn(eng)
                        if o.inc_val is not None:
                            ins.then_inc(o.inc_val[0], 16 if o.is_dma else 1)
                    if ename == "sp":
                        for s, v in final:
                            eng.wait_ge(s, v)
                return body
            block.tensor(mk("pe"))
            block.scalar(mk("act"))
            block.vector(mk("dve"))
            block.gpsimd(mk("pool"))
            block.sync(mk("sp"))


class Tile:
    def __init__(self, ap_f32, keys):
        self.f = ap_f32
        self.b = ap_f32.bitcast(BF16)
        self.keys = list(keys)

    def half(self, i):
        return Half(self.f[:, 256 * i:256 * (i + 1)], [self.keys[i]])


class Half:
    def __init__(self, ap_f32, keys):
        self.f = ap_f32
        self.b = ap_f32.bitcast(BF16)
        self.keys = list(keys)


class FreeList:
    def __init__(self, items):
        self.q = deque(items)

    def alloc(self):
        if not self.q:
            raise RuntimeError("scratch pool exhausted")
        return self.q.popleft()

    def free(self, *ts):
        for t in ts:
            self.q.append(t)


def build_program(debug=(), stop_after=None):
    nc = bass.Bass("TRN2", target_bir_lowering=False)

    def din(name, shape, dt=F32):
        return nc.dram_tensor(name, list(shape), dt, kind="ExternalInput").ap()

    def dout(name, shape, dt=F32):
        return nc.dram_tensor(name, list(shape), dt, kind="ExternalOutput").ap()

    x_all = din("x_all", [T_ALL, 1024])
    mem = din("mem", [256, 1024])
    sconv = din("sconv", [16, 30, 512])
    shg = din("shg", [16, 4, 128, 128])
    ck = din("ck", [16, 256, 512])
    cv = din("cv", [16, 256, 512])
    w_in = din("w_in", [1024, D_IN])
    w_conv_out = din("w_conv_out", [512, 1024])
    w_hg_out = din("w_hg_out", [512, 1024])
    w_mem_out = din("w_mem_out", [512, 1024])
    w_mem_kv = din("w_mem_kv", [1024, 1024])
    w_out = din("w_out", [1024, 1024])
    w_gate = din("w_gate", [1024, 2816])
    w_up = din("w_up", [1024, 2816])
    w_down = din("w_down", [2816, 1024])
    vec36 = din("vec36", [36, 512])
    hgn = din("hgn", [128, 1])
    gains = din("gains", [5, 1024])
    c_identb = din("c_identb", [128, 128], BF16)
    c_identf = din("c_identf", [128, 128])
    c_startp = din("c_startp", [128, 512])
    c_starts = din("c_starts", [128, 128])
    c_causp = din("c_causp", [128, 128])
    c_causs = din("c_causs", [128, 128])
    c_crowp = din("c_crowp", [128, 4])
    c_crows = din("c_crows", [128, 16])
    c_seqm = din("c_seqm", [128, 128], BF16)

    y = dout("y", [T_ALL, 1024])
    o_ncp = dout("o_ncp", [30, 512])
    o_nhp = dout("o_nhp", [4, 128, 128])
    o_mk = dout("o_mk", [256, 512])
    o_mv = dout("o_mv", [256, 512])
    o_ncs = dout("o_ncs", [16, 30, 512])
    o_nhs = dout("o_nhs", [16, 4, 128, 128])
    x1s = nc.dram_tensor("x1s", [T_ALL, 1024], F32, kind="Internal").ap()

    dbg_outs = {}
    st = ExitStack()
    with st:
        def sb(name, shape, dt):
            return st.enter_context(nc.sbuf_tensor(name, list(shape), dt))

        def ps(name, shape, dt):
            return st.enter_context(nc.psum_tensor(name, list(shape), dt))

        P = Prog(nc)

        NSLOT = 33
        AR = sb("arena", [128, NSLOT * 2048], BF16)
        NPG = 17
        PGT = sb("pgt", [128, NPG, 512], F32)
        PXT = sb("pxt", [128, 4, 1024], F32)
        gA = sb("gA", [128, 1024], F32)
        UB = sb("ubuf", [128, 4, 608], BF16)
        identb = sb("identb", [128, 128], BF16)
        identf = sb("identf", [128, 128], F32)
        ones32 = sb("ones32", [128, 128], F32)
        mhalf = sb("mhalf", [128, 1], F32)
        epst = sb("epst", [128, 1], F32)
        startp = sb("startp", [128, 512], F32)
        starts = sb("starts", [128, 128], F32)
        causp = sb("causp", [128, 128], F32)
        causs = sb("causs", [128, 128], F32)
        crowp = sb("crowp", [128, 4], F32)
        crows = sb("crows", [128, 16], F32)
        seqm = sb("seqm", [128, 128], BF16)
        vecT = sb("vecT", [128, 4, 36], F32)
        lbv = sb("lbv", [128, 4], F32)
        omlv = sb("omlv", [128, 4], F32)
        hgnv = sb("hgnv", [128, 1], F32)
        KTp = sb("KTp", [128, 4, 256], BF16)
        Vbp = sb("Vbp", [128, 2, 512], BF16)
        S32 = sb("S32", [128, 4, 128], F32)
        Sbf = sb("Sbf", [128, 4, 128], BF16)
        stat = sb("stat", [128, 64], F32)
        decs = sb("decs", [128, 4, 16], F32)

        Q = [ps("Q%d" % i, [128, 1024], F32) for i in range(4)]

        def bank(q, h):
            return Q[q][:, 512 * h:512 * (h + 1)]

        def bk(q, h):
            return [("ps", q, h)]

        pg_tiles = [Tile(PGT[:, i, :], [("g", i, 0), ("g", i, 1)]) for i in range(NPG)]
        px_as_pg = []
        for b_ in range(4):
            for hh in range(2):
                px_as_pg.append(Tile(PXT[:, b_, 512 * hh:512 * (hh + 1)], [("x", b_, 2 * hh), ("x", b_, 2 * hh + 1)]))

        def pxkeys(b_):
            return [("x", b_, i) for i in range(4)]

        def halves_of(tiles):
            out = []
            for t in tiles:
                out.append(t.half(0))
                out.append(t.half(1))
            return out

        def slot(s0, n_el):
            return AR[:, s0 * 2048:s0 * 2048 + n_el]

        def akeys(s0, n_el):
            return [("a", s) for s in range(s0, s0 + (n_el * 2 + 4095) // 4096)]

        def wview(s0, kc, cols):
            return slot(s0, kc * cols).rearrange("p (k e) -> p k e", k=kc)

        def DMA(eng, out, in_, reads, writes, key):
            return P.op(eng, lambda e: e.dma_start(out=out, in_=in_), reads=reads, writes=writes, dma_key=key)

        def ACT(out, in_, func, reads, writes, bias=None, scale=None, accum=None):
            kw = {}
            if bias is not None:
                kw["bias"] = bias
            if scale is not None:
                kw["scale"] = scale
            if accum is not None:
                kw["accum_out"] = accum
            return P.op("act", lambda e: e.activation(out=out, in_=in_, func=func, **kw), reads=reads, writes=writes)

        def ACOPY(out, in_, reads, writes):
            return P.op("act", lambda e: e.copy(out, in_), reads=reads, writes=writes)

        def VCOPY(out, in_, reads, writes, eng="dve"):
            return P.op(eng, lambda e: e.tensor_copy(out, in_), reads=reads, writes=writes)

        def TT(out, a, b, op, reads, writes, eng="dve"):
            return P.op(eng, lambda e: e.tensor_tensor(out, a, b, op=op), reads=reads, writes=writes)

        def TS(out, a, s1, s2, op0, op1, reads, writes, eng="dve"):
            if op1 is None:
                return P.op(eng, lambda e: e.tensor_scalar(out, a, s1, None, op0=op0), reads=reads, writes=writes)
            return P.op(eng, lambda e: e.tensor_scalar(out, a, s1, s2, op0=op0, op1=op1), reads=reads, writes=writes)

        def STT(out, a, s, b, op0, op1, reads, writes):
            return P.op("dve", lambda e: e.scalar_tensor_tensor(out, a, s, b, op0=op0, op1=op1), reads=reads, writes=writes)

        def RECIP(out, in_, reads, writes):
            return P.op("dve", lambda e: e.reciprocal(out, in_), reads=reads, writes=writes)

        mm_count = [0]
        marks = []

        def mark(name):
            marks.append((name, mm_count[0]))

        def MM(out_ap, pairs, reads, writes, start=True, stop=True):
            pairs = list(pairs)
            mm_count[0] += len(pairs)

            def fn(e):
                n = len(pairs)
                ins = None
                for i, (l, r) in enumerate(pairs):
                    ins = e.matmul(out_ap, lhsT=l, rhs=r, start=(start and i == 0), stop=(stop and i == n - 1))
                return ins
            return P.op("pe", fn, reads=reads, writes=writes)

        def TRS(items, ident, reads, writes):
            items = list(items)
            mm_count[0] += len(items)

            def fn(e):
                ins = None
                for (o_, i_) in items:
                    ins = e.transpose(o_, i_, ident)
                return ins
            return P.op("pe", fn, reads=reads, writes=writes)

        def load_w(s0, src, kc, extra_writes=()):
            cols = src.shape[1]
            v = wview(s0, kc, cols)
            keys = akeys(s0, kc * cols)
            DMA("pool", v, src.rearrange("(k p) e -> p k e", p=128), [], keys + list(extra_writes), ("a", s0))
            return v, keys

        all_hm_keys = [("hT", c) for c in range(9)] + [("mg", c) for c in range(9)]
        WDEF = {
            "K_k": (8, w_mem_kv[:, 0:512], 8), "K_v": (10, w_mem_kv[:, 512:1024], 8),
            "A_ca": (0, w_in[:, 0:512], 8), "A_cb": (2, w_in[:, 512:1024], 8), "A_co": (12, w_conv_out, 4),
            "A_g0a": (14, w_in[:, 3584:4096], 8), "A_g0b": (16, w_in[:, 4096:4608], 8),
            "B_q": (0, w_in[:, 1024:1536], 8), "B_f": (2, w_in[:, 1536:2048], 8), "B_i": (4, w_in[:, 2048:2560], 8),
            "B_hgt": (6, w_in[:, 2560:3072], 8), "B_ho": (8, w_hg_out, 4),
            "B_g1a": (10, w_in[:, 4608:5120], 8), "B_g1b": (12, w_in[:, 5120:5632], 8),
            "C_mq": (0, w_in[:, 3072:3584], 8), "C_mo": (2, w_mem_out, 4),
            "C_g2a": (4, w_in[:, 5632:6144], 8), "C_g2b": (6, w_in[:, 6144:6656], 8),
            "D_o1": (0, w_out[:, 0:512], 8), "D_o2": (2, w_out[:, 512:1024], 8),
        }
        _gslots = [4, 8, 12, 16, 20, 22]
        _uslots = [6, 10, 14, 18, 0, 2]
        for i in range(6):
            cw = 512 if i < 5 else 256
            WDEF["E_g%d" % i] = (_gslots[i], w_gate[:, 512 * i:512 * i + cw], 8)
            WDEF["E_u%d" % i] = (_uslots[i], w_up[:, 512 * i:512 * i + cw], 8)
        _dslots = [3] + list(range(23, 33))
        for i in range(11):
            WDEF["E_d%d" % i] = (_dslots[i], w_down[256 * i:256 * (i + 1), :], 2)

        def wget(name):
            s0, src, kc = WDEF[name]
            cols = src.shape[1]
            return wview(s0, kc, cols), akeys(s0, kc * cols)

        def wload(name):
            s0, src, kc = WDEF[name]
            cols = src.shape[1]
            nsl = (kc * cols * 2 + 4095) // 4096
            extra = []
            if name.startswith("E_"):
                if s0 + nsl > 18 and s0 < 23:
                    extra += [("hT", c) for c in range(9)]
                if s0 + nsl > 23:
                    extra += [("mg", c) for c in range(9)]
            return load_w(s0, src, kc, extra_writes=extra)

        stat_i = [0]

        def stat_col(n=1):
            i = stat_i[0]
            if i + n > 64:
                i = 0
            stat_i[0] = i + n
            return stat[:, i:i + n], [("st", j) for j in range(i, i + n)]

        def dump(name, ap, shape, reads, dt=F32):
            if name not in debug:
                return
            t = dout("dbg_" + name, shape, dt)
            dbg_outs[name] = t
            DMA("sp", t, ap, reads, [], ("dbg", name))

        def pow_mhalf(ap, keys, W):
            ACT(ap, ap, AF.Sqrt, keys, keys)
            RECIP(ap, ap, keys, keys)

        def rstd_from_ss(ss_ap, ss_keys, n, eps):
            r_ap, r_keys = stat_col()
            ACT(r_ap, ss_ap, AF.Sqrt, ss_keys + ["epst"], r_keys, scale=1.0 / n, bias=epst[:, 0:1])
            RECIP(r_ap, r_ap, r_keys, r_keys)
            return r_ap, r_keys

        def sumsq_1024(src, src_keys, pool, junk_ps=None):
            if junk_ps is not None:
                a_ap, a_keys = stat_col()
                ACT(Q[junk_ps][:, :], src, AF.Square, src_keys, bk(junk_ps, 0) + bk(junk_ps, 1) + a_keys, accum=a_ap)
                return a_ap, a_keys
            jt = pool.alloc()
            a_ap, a_keys = stat_col()
            b_ap, b_keys = stat_col()
            ACT(jt.f, src[:, 0:512], AF.Square, src_keys, jt.keys + a_keys, accum=a_ap)
            ACT(jt.f, src[:, 512:1024], AF.Square, src_keys, jt.keys + b_keys, accum=b_ap)
            TT(a_ap, a_ap, b_ap, ALU.add, a_keys + b_keys, a_keys)
            pool.free(jt)
            return a_ap, a_keys

        for dst, src, key in ((identb, c_identb, "identb"), (identf, c_identf, "identf"), (startp, c_startp, "startp"),
                              (starts, c_starts, "starts"), (causp, c_causp, "causp"), (causs, c_causs, "causs"),
                              (crowp, c_crowp, "crowp"), (crows, c_crows, "crows"), (seqm, c_seqm, "seqm"),
                              (hgnv, hgn, "hgnv")):
            DMA("sp", dst[:], src, [], [key], ("c", key))
        P.op("dve", lambda e: e.memset(ones32[:], 1.0), writes=["ones32"])
        P.op("dve", lambda e: e.memset(mhalf[:], -0.5), writes=["mhalf"])
        P.op("dve", lambda e: e.memset(epst[:], EPS), writes=["epst"])
        P.op("dve", lambda e: e.memset(UB[:], 0.0), writes=["ub"])
        P.op("dve", lambda e: e.memset(S32[:], 0.0), writes=[("S32", h) for h in range(4)])
        P.op("dve", lambda e: e.memset(Sbf[:], 0.0), writes=[("Sbf", h) for h in range(4)])

        DMA("sp", PXT[0:36, 0, 0:512], vec36, [], pxkeys(0), ("x", 0))
        TRS([(bank(0, 0)[:, 36 * j:36 * (j + 1)], PXT[0:36, 0, 128 * j:128 * (j + 1)]) for j in range(4)],
            identf[0:36, 0:36], pxkeys(0) + ["identf"], bk(0, 0))
        ACOPY(vecT[:].rearrange("p j k -> p (j k)"), bank(0, 0)[:, 0:144], bk(0, 0), ["vecT"])
        TT(lbv[:], vecT[:, :, 34], vecT[:, :, 35], ALU.subtract, ["vecT"], ["lbv"])
        ACT(lbv[:], lbv[:], AF.Sigmoid, ["lbv"], ["lbv"])
        TS(omlv[:], lbv[:], -1.0, 1.0, ALU.mult, ALU.add, ["lbv"], ["omlv"])

        def load_gain(dst, key, row):
            DMA("sp", dst, gains[row:row + 1, :].partition_broadcast(128), [], key if isinstance(key, list) else [key], ("gain", str(key)))

        def norm_front(src_rows, pxb, gain_ap, gain_keys, pool, src_keys=(), dve_sumsq=False, preloaded=False, junk_ps=None):
            xt = PXT[:, pxb, :]
            if not preloaded:
                DMA("sp", xt, src_rows, list(src_keys), pxkeys(pxb), ("x", pxb))
            jt2 = pool.alloc()
            if dve_sumsq:
                ss_ap, ss_keys = stat_col()
                P.op("dve", (lambda o_, a_, acc_: (lambda e: e.scalar_tensor_tensor(o_, a_, 1.0, a_, op0=ALU.mult, op1=ALU.mult,
                                                                               accum_out=acc_)))(jt2.f, xt[:, 0:512], ss_ap),
                     reads=pxkeys(pxb), writes=jt2.keys + ss_keys)
                s2_ap, s2_keys = stat_col()
                P.op("dve", (lambda o_, a_, acc_: (lambda e: e.scalar_tensor_tensor(o_, a_, 1.0, a_, op0=ALU.mult, op1=ALU.mult,
                                                                               accum_out=acc_)))(jt2.f, xt[:, 512:1024], s2_ap),
                     reads=pxkeys(pxb), writes=jt2.keys + s2_keys)
                TT(ss_ap, ss_ap, s2_ap, ALU.add, ss_keys + s2_keys, ss_keys)
            else:
                ss_ap, ss_keys = sumsq_1024(xt, pxkeys(pxb), pool, junk_ps=junk_ps)
            r_ap, r_keys = rstd_from_ss(ss_ap, ss_keys, 1024, EPS)
            hb = jt2.b
            STT(hb, xt, r_ap, gain_ap, ALU.mult, ALU.mult, pxkeys(pxb) + r_keys + gain_keys, jt2.keys)
            return jt2

        def norm_back(jt2, dstT, dcol, dkeys, pool, tb):
            hb = jt2.b
            pb = bank(*tb).bitcast(BF16)
            TRS([(pb[:, 128 * k:128 * (k + 1)], hb[:, 128 * k:128 * (k + 1)]) for k in range(8)], identb[:],
                jt2.keys + ["identb"], bk(*tb))
            ACOPY(dstT[:, :, dcol:dcol + 128], pb.rearrange("p (k t) -> p k t", k=8), bk(*tb), dkeys)
            pool.free(jt2)

        def norm_tile_to_T(src_rows, pxb, dstT, dcol, dkeys, gain_ap, gain_keys, pool, tb, src_keys=(), junk_ps=None):
            jt2 = norm_front(src_rows, pxb, gain_ap, gain_keys, pool, src_keys, junk_ps=junk_ps)
            norm_back(jt2, dstT, dcol, dkeys, pool, tb)

        gB = PGT[:, 15:17, :].rearrange("p a b -> p (a b)")
        gB_keys = pg_tiles[15].keys + pg_tiles[16].keys
        tmpE = PGT[:, 9:11, :].rearrange("p a b -> p (a b)")
        tmpE_keys = pg_tiles[9].keys + pg_tiles[10].keys
        for nm in ("A_ca", "A_cb", "A_co", "A_g0a", "A_g0b", "K_k", "K_v"):
            wload(nm)
        load_gain(gB, gB_keys, 4)
        diag = [slot(4 + 2 * j, 31 * 128).rearrange("p (k c) -> p k c", k=31) for j in range(4)]
        dkeys = [akeys(4 + 2 * j, 31 * 128) for j in range(4)]

        def build_diag():
            for j in range(4):
                TT(diag[j], identf[:].unsqueeze(1).to_broadcast([128, 31, 128]),
                   vecT[:, j, 0:31].unsqueeze(2).to_broadcast([128, 31, 128]), ALU.mult, ["identf", "vecT"], dkeys[j])

        def phaseK():
            wK, wK_keys = wget("K_k")
            wV, wV_keys = wget("K_v")
            memT = wview(6, 8, 256)
            memT_keys = akeys(6, 2048)
            pgK = FreeList(pg_tiles[0:9])
            kfr = [norm_front(mem[128 * mt:128 * (mt + 1), :], 1 + mt, gB, gB_keys, pgK, junk_ps=2) for mt in range(2)]
            for mt in range(2):
                norm_back(kfr[mt], memT, 128 * mt, memT_keys, pgK, (3, 1))
            for mt in range(2):
                for which, wv, wkeys, odram in ((0, wK, wK_keys, o_mk), (1, wV, wV_keys, o_mv)):
                    bq = which
                    MM(bank(0, bq), [(memT[:, k, 128 * mt:128 * (mt + 1)], wv[:, k, :]) for k in range(8)],
                       memT_keys + wkeys, bk(0, bq))
                    t = pg_tiles[2 * mt + which]
                    ACOPY(t.f, bank(0, bq), bk(0, bq), t.keys)
                    DMA("sp", odram[128 * mt:128 * (mt + 1), :], t.f, t.keys, [], ("o", 2 * mt + which))
                    if which == 1:
                        VCOPY(Vbp[:, mt, :], t.f, t.keys, [("Vbp", mt)])
            for h in range(4):
                bq = h % 2
                MM(bank(1, bq)[:, 0:256], [(wK[:, k, 128 * h:128 * (h + 1)], memT[:, k, :]) for k in range(8)],
                   memT_keys + wK_keys, bk(1, bq))
                ACOPY(KTp[:, h, :], bank(1, bq)[:, 0:256], bk(1, bq), [("KTp", h)])
            dump("KTp", KTp[:], [128, 4, 256], [("KTp", h) for h in range(4)], BF16)

        hT = slot(18, 8 * 1152).rearrange("p (k t) -> p k t", k=8)
        MG = AR[:, 23 * 2048:33 * 2048].bitcast(F32)[:, 0:8 * 1152].rearrange("p (k t) -> p k t", k=8)
        halves = [
            dict(tiles=list(range(0, 8)), groups=[(0, 512, "p", [0, 1, 2, 3]), (512, 512, "p", [4, 5, 6, 7])]),
            dict(tiles=list(range(8, 16)) + [16],
                 groups=[(0, 512, "p", [8, 9, 10, 11]), (512, 512, "p", [12, 13, 14, 15]), (1024, 128, "s", [16])]),
        ]

        def lcol(hi, ti):
            return 128 * ti if hi == 0 else 128 * (ti - 8)

        def hkeys(c0, W):
            return [("hT", c) for c in range(c0 // 128, (c0 + W) // 128)]

        def mkeys(c0, W):
            return [("mg", c) for c in range(c0 // 128, (c0 + W) // 128)]

        def gate_pre(pool, w_ga, k_ga, w_gb, k_gb, c0, W, n_pre, banks, dm0=0):
            hk = hkeys(c0, W)
            out = []
            for dm in range(dm0, n_pre):
                bg = banks[dm % len(banks)]
                wg, kg = (w_ga, k_ga) if dm < 4 else (w_gb, k_gb)
                dmo = 128 * (dm % 4)
                MM(bank(*bg)[:, 0:W], [(wg[:, k, dmo:dmo + 128], hT[:, k, c0:c0 + W]) for k in range(8)], kg + hk, bk(*bg))
                sg_t = pool.alloc()
                ACT(sg_t.f[:, 0:W], bank(*bg)[:, 0:W], AF.Sigmoid, bk(*bg), sg_t.keys)
                out.append(sg_t)
            return out

        def gate_merge(pool, w_pr, k_pr, in_list, w_ga, k_ga, w_gb, k_gb, c0, W, first, pre=()):
            hk = hkeys(c0, W)
            in_keys = []
            for (_, ks) in in_list:
                in_keys += ks
            pre = list(pre)
            for dm in range(8):
                bp, bg = ((0, 0), (0, 1)) if dm % 2 == 0 else ((1, 0), (1, 1))
                MM(bank(*bp)[:, 0:W], [(w_pr[:, j, 128 * dm:128 * (dm + 1)], ap) for j, (ap, _) in enumerate(in_list)],
                   k_pr + in_keys, bk(*bp))
                if dm < len(pre):
                    sg_t = pre[dm]
                else:
                    wg, kg = (w_ga, k_ga) if dm < 4 else (w_gb, k_gb)
                    dmo = 128 * (dm % 4)
                    MM(bank(*bg)[:, 0:W], [(wg[:, k, dmo:dmo + 128], hT[:, k, c0:c0 + W]) for k in range(8)], kg + hk, bk(*bg))
                    sg_t = pool.alloc()
                    ACT(sg_t.f[:, 0:W], bank(*bg)[:, 0:W], AF.Sigmoid, bk(*bg), sg_t.keys)
                if first:
                    TT(MG[:, dm, c0:c0 + W], bank(*bp)[:, 0:W], sg_t.f[:, 0:W], ALU.mult, bk(*bp) + sg_t.keys, mkeys(c0, W))
                else:
                    TT(sg_t.f[:, 0:W], bank(*bp)[:, 0:W], sg_t.f[:, 0:W], ALU.mult, bk(*bp) + sg_t.keys, sg_t.keys)
                    TT(MG[:, dm, c0:c0 + W], MG[:, dm, c0:c0 + W], sg_t.f[:, 0:W], ALU.add, mkeys(c0, W) + sg_t.keys,
                       mkeys(c0, W))
                pool.free(sg_t)

        stopped = False
        for hi, half in enumerate(halves):
            mark("h%d_0" % hi)
            if hi == 0:
                load_gain(gA[:], "gA", 0)
                pg0 = FreeList(pg_tiles)
                p0t = half["tiles"]

                def p0_front(n_):
                    ti_ = p0t[n_]
                    return norm_front(x_all[128 * ti_:128 * (ti_ + 1), :], n_ % 4, gA[:], ["gA"], pg0, junk_ps=2)

                fr_ = p0_front(0)
                for n_, ti in enumerate(p0t):
                    lc = lcol(hi, ti)
                    nx_ = p0_front(n_ + 1) if n_ + 1 < len(p0t) else None
                    norm_back(fr_, hT, lc, [("hT", lc // 128)], pg0, (3, 1))
                    fr_ = nx_
            if hi == 0:
                phaseK()
                build_diag()
                dump("hT0", hT[:, :, 0:1024], [128, 8, 1024], [("hT", c) for c in range(8)], BF16)
            if stop_after in ("p0", "p0@%d" % hi):
                stopped = True
                break

            mark("h%d_A" % hi)
            w_ca, k_ca = wget("A_ca")
            w_cb, k_cb = wget("A_cb")
            w_co, k_co = wget("A_co")
            w_g0a, k_g0a = wget("A_g0a")
            w_g0b, k_g0b = wget("A_g0b")
            pgA = FreeList(pg_tiles + px_as_pg[2:])
            ubs = UB[:].rearrange("p j (s t) -> p j s t", s=16)
            for (c0, W, kind, tiles) in half["groups"]:
                hk = hkeys(c0, W)
                if kind == "p":
                    if not (hi == 0 and c0 == 0):
                        VCOPY(UB[:, :, 0:30], UB[:, :, 512:542], ["ub"], ["ub"])
                else:
                    xt = PXT[:, 0, :]
                    t_keys = pxkeys(0)
                    for sg in range(4):
                        DMA("sp", xt[0:120, 0:512], sconv[4 * sg:4 * sg + 4].rearrange("s r c -> (s r) c"), [], t_keys, ("x", 0))
                        for s_ in range(4):
                            DMA("sp", o_ncs[4 * sg + s_, 0:22, :], xt[30 * s_ + 8:30 * s_ + 30, 0:512], t_keys, [], ("xo", 0))
                        TRS([(bank(3, 0)[:, 120 * j:120 * (j + 1)], xt[0:120, 128 * j:128 * (j + 1)]) for j in range(4)],
                            identf[0:120, 0:120], t_keys + ["identf"], bk(3, 0))
                        ACOPY(ubs[:, :, 4 * sg:4 * sg + 4, 0:30],
                              bank(3, 0)[:, 0:480].rearrange("p (j s r) -> p j s r", j=4, s=4), bk(3, 0), ["ub"])
                for j in range(4):
                    ba, bb = ((0, 0), (0, 1)) if j % 2 == 0 else ((2, 0), (2, 1))
                    MM(bank(*ba)[:, 0:W], [(w_ca[:, k, 128 * j:128 * (j + 1)], hT[:, k, c0:c0 + W]) for k in range(8)],
                       k_ca + hk, bk(*ba))
                    MM(bank(*bb)[:, 0:W], [(w_cb[:, k, 128 * j:128 * (j + 1)], hT[:, k, c0:c0 + W]) for k in range(8)],
                       k_cb + hk, bk(*bb))
                    sg_t = pgA.alloc()
                    ACT(sg_t.f[:, 0:W], bank(*bb)[:, 0:W], AF.Sigmoid, bk(*bb), sg_t.keys)
                    if kind == "p":
                        TT(UB[:, j, 30:30 + W], bank(*ba)[:, 0:W], sg_t.f[:, 0:W], ALU.mult, bk(*ba) + sg_t.keys, ["ub"])
                    else:
                        TT(ubs[:, j, :, 30:38], bank(*ba)[:, 0:W].rearrange("p (s t) -> p s t", s=16),
                           sg_t.f[:, 0:W].rearrange("p (s t) -> p s t", s=16), ALU.mult, bk(*ba) + sg_t.keys, ["ub"])
                    pgA.free(sg_t)
                if hi == 0 and c0 == 0:
                    dump("u0", UB[:, :, 0:542], [128, 4, 542], ["ub"], BF16)
                for ti in tiles:
                    if ti not in (15, 16):
                        continue
                    lc = lcol(hi, ti)
                    MM(bank(1, 0), [(hT[:, k, lc:lc + 128], w_ca[:, k, :]) for k in range(8)], k_ca + [("hT", lc // 128)], bk(1, 0))
                    MM(bank(1, 1), [(hT[:, k, lc:lc + 128], w_cb[:, k, :]) for k in range(8)], k_cb + [("hT", lc // 128)], bk(1, 1))
                    ut = pgA.alloc()
                    ACT(ut.f, bank(1, 1), AF.Sigmoid, bk(1, 1), ut.keys)
                    TT(ut.f, bank(1, 0), ut.f, ALU.mult, bk(1, 0) + ut.keys, ut.keys)
                    if ti == 15:
                        DMA("sp", o_ncp, ut.f[98:128, :], ut.keys, [], ("uo", 0))
                    else:
                        for s_ in range(16):
                            DMA("sp", o_ncs[s_, 22:30, :], ut.f[8 * s_:8 * s_ + 8, :], ut.keys, [], ("uo", 0))
                    pgA.free(ut)
                is_last = (c0, W, kind, tiles) == half["groups"][-1]
                if is_last:
                    wload("B_q")
                    wload("B_f")
                c32 = []
                for j in range(4):
                    cb_ = [(1, 0), (1, 1), (3, 0), (3, 1)][j]
                    if kind == "p":
                        pairs = [(diag[j][:, k, :], UB[:, j, k:k + W]) for k in range(31)]
                        oap = bank(*cb_)[:, 0:W]
                    else:
                        pairs = [(diag[j][:, k, :], ubs[:, j, :, k:k + 8]) for k in range(31)]
                        oap = bank(*cb_)[:, 0:W].rearrange("p (s t) -> p s t", s=16)
                    MM(oap, pairs, dkeys[j] + ["ub"], bk(*cb_))
                    ct = pgA.alloc()
                    cq = pgA.alloc()
                    ACT(ct.f[:, 0:W], bank(*cb_)[:, 0:W], AF.Identity, bk(*cb_) + ["vecT"], ct.keys, bias=vecT[:, j, 31:32])
                    ACT(cq.f[:, 0:W], bank(*cb_)[:, 0:W], AF.Square, bk(*cb_) + ["vecT"], cq.keys, bias=vecT[:, j, 31:32])
                    MM(bank(2, 0)[:, 0:W], [(ones32[:], ct.f[:, 0:W])], ct.keys + ["ones32"], bk(2, 0), start=(j == 0), stop=(j == 3))
                    MM(bank(2, 1)[:, 0:W], [(ones32[:], cq.f[:, 0:W])], cq.keys + ["ones32"], bk(2, 1), start=(j == 0), stop=(j == 3))
                    pgA.free(cq)
                    c32.append(ct)
                if hi == 0 and c0 == 0:
                    dump("c0", c32[0].f, [128, 512], c32[0].keys)
                preA = gate_pre(pgA, w_g0a, k_g0a, w_g0b, k_g0b, c0, W, 4, [(0, 0), (0, 1), (1, 0), (1, 1)])
                mean = pgA.alloc()
                rstd = pgA.alloc()
                TS(mean.f[:, 0:W], bank(2, 0)[:, 0:W], 1.0 / 512, None, ALU.mult, None, bk(2, 0), mean.keys)
                TT(rstd.f[:, 0:W], mean.f[:, 0:W], mean.f[:, 0:W], ALU.mult, mean.keys, rstd.keys)
                STT(rstd.f[:, 0:W], bank(2, 1)[:, 0:W], 1.0 / 512, rstd.f[:, 0:W], ALU.mult, ALU.subtract,
                    bk(2, 1) + rstd.keys, rstd.keys)
                ACT(rstd.f[:, 0:W], rstd.f[:, 0:W], AF.Sqrt, rstd.keys + ["epst"], rstd.keys, bias=epst[:, 0:1])
                RECIP(rstd.f[:, 0:W], rstd.f[:, 0:W], rstd.keys, rstd.keys)
                preA = preA + gate_pre(pgA, w_g0a, k_g0a, w_g0b, k_g0b, c0, W, 8, [(0, 0), (0, 1), (1, 0), (1, 1)], dm0=4)
                cT = [pgA.alloc(), pgA.alloc()]
                cT_list = []
                for j in range(4):
                    ct = c32[j]
                    cv_ = cT[j // 2].b[:, 512 * (j % 2):512 * (j % 2) + W]
                    ckey = [cT[j // 2].keys[j % 2]]
                    TT(ct.f[:, 0:W], ct.f[:, 0:W], mean.f[:, 0:W], ALU.subtract, ct.keys + mean.keys, ct.keys)
                    TT(ct.f[:, 0:W], ct.f[:, 0:W], rstd.f[:, 0:W], ALU.mult, ct.keys + rstd.keys, ct.keys)
                    ACT(cv_, ct.f[:, 0:W], AF.Silu, ct.keys + ["vecT"], ckey, scale=vecT[:, j, 32:33], bias=vecT[:, j, 33:34])
                    cT_list.append((cv_, ckey))
                    pgA.free(ct)
                pgA.free(mean, rstd)
                if hi == 0 and c0 == 0:
                    dump("cl0", cT[0].b[:, 0:512], [128, 512], cT[0].keys, BF16)
                gate_merge(pgA, w_co, k_co, cT_list, w_g0a, k_g0a, w_g0b, k_g0b, c0, W, True, pre=preA)
                pgA.free(*cT)
            for nm in ("B_i", "B_hgt", "B_ho", "B_g1a", "B_g1b"):
                wload(nm)
            if hi == 0:
                dump("mgA", MG[:, :, 0:1024], [128, 8, 1024], [("mg", c) for c in range(8)])
            if stop_after in ("pA", "pA@%d" % hi):
                stopped = True
                break

            if hi == 0 and "h0A" in os.environ.get("KDBG", ""):
                continue
            mark("h%d_B" % hi)
            w_q, k_q = wget("B_q")
            w_f, k_f = wget("B_f")
            w_i, k_i = wget("B_i")
            w_hgt, k_hgt = wget("B_hgt")
            w_ho, k_ho = wget("B_ho")
            w_g1a, k_g1a = wget("B_g1a")
            w_g1b, k_g1b = wget("B_g1b")
            S0b = [slot(14 + i, 2048).rearrange("p (s v) -> p s v", s=16) for i in range(2)]
            S0b_keys = [akeys(14 + i, 2048) for i in range(2)]
            S0f = [AR[:, 16 * 2048:18 * 2048].bitcast(F32).rearrange("p (s v) -> p s v", s=16),
                   PXT[:, 2:4, :].rearrange("p a b -> p (a b)").rearrange("p (s v) -> p s v", s=16)]
            S0f_keys = [akeys(16, 4096), pxkeys(2) + pxkeys(3)]

            def load_states(b_):
                src = shg[4 * b_:4 * b_ + 4].rearrange("s h d v -> d (s h) v")
                DMA("pool", S0b[b_ % 2], src, [], S0b_keys[b_ % 2], ("s0b", b_ % 2))
                DMA("sp", S0f[b_ % 2], src, [], S0f_keys[b_ % 2], ("s0f", b_ % 2))
            all_t = pg_tiles + px_as_pg
            pgB = FreeList(all_t[0:12])
            hpB = FreeList(halves_of(all_t[12:]))
            oT = [bank(2, 0), bank(2, 1), bank(3, 0), bank(3, 1)]
            oTk = [bk(2, 0), bk(2, 1), bk(3, 0), bk(3, 1)]
            sq_i = [0]
            st_i = [0]

            sq_banks = [(1, 0), (0, 0), (0, 1)]

            def psq_bank():
                b_ = sq_banks[sq_i[0] % 3]
                sq_i[0] += 1
                return b_

            for (c0, W, kind, tiles) in half["groups"]:
                if kind == "s":
                    pgB = FreeList(all_t[0:12])
                    hpB = FreeList(halves_of(pg_tiles[12:17] + px_as_pg[0:4]))
                    load_states(0)
                    load_states(1)
                hk = hkeys(c0, W)
                CH = 32 if kind == "p" else 8
                smask = startp if kind == "p" else starts
                caus = causp if kind == "p" else causs
                kinvT, kendT, qdecT = [], [], []
                fss, qss = [], []
                pb4 = [(0, 0), (0, 1), (1, 0), (1, 1)]
                for h in range(4):
                    hc = slice(128 * h, 128 * (h + 1))
                    b1, b2 = pb4[(2 * h) % 4], pb4[(2 * h + 1) % 4]
                    MM(bank(*b1)[:, 0:W], [(w_f[:, k, hc], hT[:, k, c0:c0 + W]) for k in range(8)], k_f + hk, bk(*b1))
                    fs = pgB.alloc()
                    ACT(fs.f[:, 0:W], bank(*b1)[:, 0:W], AF.Sigmoid, bk(*b1), fs.keys)
                    MM(bank(*b2)[:, 0:W], [(w_q[:, k, hc], hT[:, k, c0:c0 + W]) for k in range(8)], k_q + hk, bk(*b2))
                    qs = pgB.alloc()
                    ACT(qs.f[:, 0:W], bank(*b2)[:, 0:W], AF.Silu, bk(*b2), qs.keys)
                    fss.append(fs)
                    qss.append(qs)
                vbs = []
                for tt in range(len(tiles)):
                    gc = c0 + 128 * tt
                    bv = pb4[tt % 4]
                    MM(bank(*bv), [(hT[:, k, gc:gc + 128], w_i[:, k, :]) for k in range(8)], k_i + [("hT", gc // 128)], bk(*bv))
                    vb = hpB.alloc()
                    ACOPY(vb.b, bank(*bv), bk(*bv), vb.keys)
                    vbs.append(vb)
                for h in range(4):
                    fs, qs = fss[h], qss[h]
                    TS(fs.f[:, 0:W], fs.f[:, 0:W], omlv[:, h:h + 1], lbv[:, h:h + 1], ALU.mult, ALU.add,
                       fs.keys + ["omlv", "lbv"], fs.keys)
                    Pt = pgB.alloc()
                    P.op("dve", (lambda o_, m_, f_: (lambda e: e.tensor_tensor_scan(o_, m_, f_, 1.0, op0=ALU.max, op1=ALU.mult)))(
                        Pt.f[:, 0:W], smask[:, 0:W], fs.f[:, 0:W]), reads=fs.keys + ["startp", "starts"], writes=Pt.keys)
                    Pv = Pt.f[:, 0:W].rearrange("p (c i) -> p c i", i=CH)
                    VCOPY(decs[:, h, :], Pv[:, :, CH - 1], Pt.keys, [("decs", h)])
                    qd = hpB.alloc()
                    TT(qd.b[:, 0:W], qs.f[:, 0:W], Pt.f[:, 0:W], ALU.mult, qs.keys + Pt.keys, qd.keys)
                    rP = pgB.alloc()
                    RECIP(rP.f[:, 0:W], Pt.f[:, 0:W], Pt.keys, rP.keys)
                    TS(fs.f[:, 0:W], fs.f[:, 0:W], -1.0, 1.0, ALU.mult, ALU.add, fs.keys, fs.keys)
                    TT(rP.f[:, 0:W], fs.f[:, 0:W], rP.f[:, 0:W], ALU.mult, fs.keys + rP.keys, rP.keys)
                    ki = hpB.alloc()
                    ACOPY(ki.b[:, 0:W], rP.f[:, 0:W], rP.keys, ki.keys)
                    ke = hpB.alloc()
                    TT(ke.b[:, 0:W].rearrange("p (c i) -> p c i", i=CH), rP.f[:, 0:W].rearrange("p (c i) -> p c i", i=CH),
                       decs[:, h, :].unsqueeze(2).to_broadcast([128, 16, CH]), ALU.mult, rP.keys + [("decs", h)], ke.keys)
                    pgB.free(fs, rP, qs, Pt)
                    kinvT.append(ki)
                    kendT.append(ke)
                    qdecT.append(qd)
                if (c0, W, kind, tiles) == half["groups"][-1]:
                    wload("C_mq")
                    wload("C_mo")
                if hi == 0 and c0 == 0:
                    dump("qd0", qdecT[0].b, [128, 512], qdecT[0].keys, BF16)
                    dump("ki0", kinvT[0].b, [128, 512], kinvT[0].keys, BF16)
                    dump("ke0", kendT[0].b, [128, 512], kendT[0].keys, BF16)
                nch = 128 // CH
                ptb = bank(1, 1).bitcast(BF16)

                def build_kems(st_, q_):
                    for cc_ in range(4):
                        km = hpB.alloc()
                        crow = crowp[:, cc_:cc_ + 1] if kind == "p" else crows[:, 4 * q_ + cc_:4 * q_ + cc_ + 1]
                        ACT(km.b, ptb[:, 0:512], AF.Copy, bk(1, 1) + ["crowp", "crows"], km.keys, scale=crow)
                        st_["kems"][cc_] = km

                def tile_pre(tt):
                    tcs = slice(128 * tt, 128 * (tt + 1))
                    TRS([(ptb[:, 128 * h:128 * (h + 1)], kendT[h].b[:, tcs]) for h in range(4)], identb[:],
                        [k_ for h in range(4) for k_ in kendT[h].keys] + ["identb"], bk(1, 1))
                    st_ = dict(kems=[None] * 4)
                    if kind == "p":
                        build_kems(st_, 0)
                    smt = hpB.alloc()
                    sb_ = psq_bank()
                    for h in range(4):
                        MM(bank(*sb_)[:, 128 * h:128 * (h + 1)], [(kinvT[h].b[:, tcs], qdecT[h].b[:, tcs])],
                           kinvT[h].keys + qdecT[h].keys, bk(*sb_))
                    TT(smt.b.rearrange("p (h t) -> p h t", h=4), bank(*sb_).rearrange("p (h t) -> p h t", h=4),
                       caus[:].unsqueeze(1).to_broadcast([128, 4, 128]), ALU.mult, bk(*sb_) + ["causp", "causs"], smt.keys)
                    st_["smt"] = smt
                    return st_

                def tile_chain(tt, st_):
                    tc = 128 * tt
                    tcs = slice(tc, tc + 128)
                    vb = vbs[tt]
                    smt = st_["smt"]
                    kems = st_["kems"]
                    for h in range(4):
                        MM(oT[h][:, tcs], [(vb.b[:, 128 * h:128 * (h + 1)], smt.b[:, 128 * h:128 * (h + 1)])],
                           vb.keys + smt.keys, oTk[h], start=True, stop=False)
                    for c in range(nch):
                        q_, cc = divmod(c, 4)
                        if cc == 0 and kind == "s":
                            if kems[0] is not None:
                                hpB.free(*kems)
                            build_kems(st_, q_)
                        bq, bc = divmod(c, 4)
                        if kind == "s" and bc == 0:
                            if bq >= 1 and bq + 1 < 4:
                                load_states(bq + 1)
                        kb_ = psq_bank()
                        for h in range(4):
                            sq_ap, sq_k = bank(*kb_)[:, 128 * h:128 * (h + 1)], bk(*kb_)
                            MM(sq_ap, [(kems[cc].b[:, 128 * h:128 * (h + 1)], vb.b[:, 128 * h:128 * (h + 1)])],
                               kems[cc].keys + vb.keys, sq_k)
                            cs = slice(tc + c * CH, tc + (c + 1) * CH)
                            if kind == "p":
                                s_ap, s_k = Sbf[:, h, :], [("Sbf", h)]
                            else:
                                s_ap, s_k = S0b[bq % 2][:, bc * 4 + h, :], S0b_keys[bq % 2]
                            MM(oT[h][:, cs], [(s_ap, qdecT[h].b[:, cs])], s_k + qdecT[h].keys, oTk[h], start=False, stop=(c == nch - 1))
                        ci = tt * nch + c
                        dbc = decs[:, :, ci].unsqueeze(2).to_broadcast([128, 4, 128])
                        dkeys_ = [("decs", h) for h in range(4)]
                        kv4 = bank(*kb_).rearrange("p (h v) -> p h v", h=4)
                        if kind == "p":
                            s32k = [("S32", h) for h in range(4)]
                            sbfk = [("Sbf", h) for h in range(4)]
                            TT(S32[:], S32[:], dbc, ALU.mult, s32k + dkeys_, s32k)
                            TT(Sbf[:], S32[:], kv4, ALU.add, s32k + bk(*kb_), sbfk)
                            TT(S32[:], S32[:], kv4, ALU.add, s32k + bk(*kb_), s32k)
                        else:
                            sv = S0f[bq % 2][:, bc * 4:bc * 4 + 4, :]
                            TT(sv, sv, dbc, ALU.mult, S0f_keys[bq % 2] + dkeys_, S0f_keys[bq % 2])
                            TT(sv, sv, kv4, ALU.add, S0f_keys[bq % 2] + bk(*kb_), S0f_keys[bq % 2])
                        if kind == "s" and bc == 3:
                            DMA("sp", o_nhs[4 * bq:4 * bq + 4].rearrange("s h d v -> d (s h) v"), S0f[bq % 2], S0f_keys[bq % 2], [],
                                ("so", bq % 2))
                    hpB.free(*kems)
                    hpB.free(smt)

                st_next = tile_pre(0)
                for tt in range(len(tiles)):
                    st_cur = st_next
                    if tt + 1 < len(tiles):
                        st_next = tile_pre(tt + 1)
                    tile_chain(tt, st_cur)
                for h in range(4):
                    hpB.free(kinvT[h], kendT[h])
                hpB.free(*vbs)
                if hi == 1 and c0 == 512 and "skipnhp" not in os.environ.get("KDBG", ""):
                    DMA("sp", o_nhp.rearrange("h d v -> d h v"), S32[:], [("S32", h) for h in range(4)], [], ("so", 1))
                og = []
                rsts = []
                for h in range(4):
                    osq = pgB.alloc()
                    ACT(osq.f[:, 0:W], oT[h][:, 0:W], AF.Square, oTk[h], osq.keys)
                    bs_ = pb4[h]
                    MM(bank(*bs_)[:, 0:W], [(ones32[:], osq.f[:, 0:W])], osq.keys + ["ones32"], bk(*bs_))
                    TS(osq.f[:, 0:W], bank(*bs_)[:, 0:W], 1.0 / 128, EPS, ALU.mult, ALU.add, bk(*bs_), osq.keys)
                    rsts.append(osq)
                gts = []
                for h in range(4):
                    hc = slice(128 * h, 128 * (h + 1))
                    bg_ = pb4[h]
                    MM(bank(*bg_)[:, 0:W], [(w_hgt[:, k, hc], hT[:, k, c0:c0 + W]) for k in range(8)], k_hgt + hk, bk(*bg_))
                    gt = pgB.alloc()
                    ACT(gt.f[:, 0:W], bank(*bg_)[:, 0:W], AF.Silu, bk(*bg_), gt.keys)
                    gts.append(gt)
                for h in range(4):
                    pow_mhalf(rsts[h].f[:, 0:W], rsts[h].keys, W)
                preB = gate_pre(pgB, w_g1a, k_g1a, w_g1b, k_g1b, c0, W, 4, pb4)
                for h in range(4):
                    rst, gt = rsts[h], gts[h]
                    STT(rst.f[:, 0:W], oT[h][:, 0:W], hgnv[:, 0:1], rst.f[:, 0:W], ALU.mult, ALU.mult,
                        oTk[h] + ["hgnv"] + rst.keys, rst.keys)
                    ogh = hpB.alloc()
                    TT(ogh.b[:, 0:W], rst.f[:, 0:W], gt.f[:, 0:W], ALU.mult, rst.keys + gt.keys, ogh.keys)
                    pgB.free(rst, gt)
                    og.append(ogh)
                    hpB.free(qdecT[h])
                if hi == 0 and c0 == 0:
                    dump("og0", og[0].b, [128, 512], og[0].keys, BF16)
                gate_merge(pgB, w_ho, k_ho, [(o_.b[:, 0:W], o_.keys) for o_ in og], w_g1a, k_g1a, w_g1b, k_g1b, c0, W, False, pre=preB)
                for o_ in og:
                    hpB.free(o_)
            wload("C_g2a")
            wload("C_g2b")
            if hi == 0:
                dump("mgB", MG[:, :, 0:1024], [128, 8, 1024], [("mg", c) for c in range(8)])
            if stop_after in ("pB", "pB@%d" % hi):
                stopped = True
                break

            mark("h%d_C" % hi)
            w_mq, k_mq = wget("C_mq")
            w_mo, k_mo = wget("C_mo")
            w_g2a, k_g2a = wget("C_g2a")
            w_g2b, k_g2b = wget("C_g2b")
            Ks = wview(8, 8, 512)
            Ks_keys = akeys(8, 4096)
            Vs = [wview(10 + 2 * i, 8, 512) for i in range(2)]
            Vs_keys = [akeys(10 + 2 * i, 4096) for i in range(2)]
            KTs = [wview(14 + 2 * i, 16, 256) for i in range(2)]
            KTs_keys = [akeys(14 + 2 * i, 4096) for i in range(2)]
            def c_load_k(qq):
                DMA("pool", Ks, ck[4 * qq:4 * qq + 4].rearrange("s (mt p) e -> p (s mt) e", p=128), [], Ks_keys, ("a", 8))

            def c_load_v(qq):
                DMA("pool", Vs[qq % 2], cv[4 * qq:4 * qq + 4].rearrange("s (mt p) e -> p (s mt) e", p=128), [], Vs_keys[qq % 2],
                    ("a", 10 + 2 * (qq % 2)))

            if hi == 1:
                c_load_k(0)
                c_load_v(0)
                c_load_v(1)
            pgC = FreeList(all_t[0:13])
            hpC = FreeList(halves_of(all_t[13:]))
            SCALE = 128 ** -0.5
            ss_i = [0]

            ss_banks = [(1, 0), (1, 1), (0, 0), (0, 1)]

            def pss():
                b_ = ss_banks[ss_i[0] % 4]
                ss_i[0] += 1
                return bank(*b_)[:, 0:256], bk(*b_)

            def softmax_rows(sc_ap, sc_k, np_, pool_f, pool_h):
                mx, mx_k = stat_col()
                P.op("dve", (lambda o_, i_: (lambda e: e.tensor_reduce(out=o_, in_=i_, axis=AX.X, op=ALU.max)))(
                    mx[0:np_, :], sc_ap), reads=sc_k, writes=mx_k)
                TS(mx[0:np_, :], mx[0:np_, :], -SCALE, None, ALU.mult, None, mx_k, mx_k)
                pe_ = pool_f.alloc()
                rs, rs_k = stat_col()
                ACT(pe_.f[0:np_, 0:256], sc_ap, AF.Exp, sc_k + mx_k, pe_.keys + rs_k, bias=mx[0:np_, :], scale=SCALE,
                    accum=rs[0:np_, :])
                RECIP(rs[0:np_, :], rs[0:np_, :], rs_k, rs_k)
                pn = pool_h.alloc()
                TS(pn.b[0:np_, 0:256], pe_.f[0:np_, 0:256], rs[0:np_, :], None, ALU.mult, None, pe_.keys + rs_k, pn.keys)
                pool_f.free(pe_)
                return pn

            for (c0, W, kind, tiles) in half["groups"]:
                hk = hkeys(c0, W)
                qT = []
                for h in range(4):
                    hc = slice(128 * h, 128 * (h + 1))
                    MM(bank(0, h % 2)[:, 0:W], [(w_mq[:, k, hc], hT[:, k, c0:c0 + W]) for k in range(8)], k_mq + hk, bk(0, h % 2))
                    qh = hpC.alloc()
                    ACOPY(qh.b[:, 0:W], bank(0, h % 2)[:, 0:W], bk(0, h % 2), qh.keys)
                    qT.append(qh)
                if (c0, W, kind, tiles) == half["groups"][-1]:
                    wload("D_o1")
                preC = gate_pre(pgC, w_g2a, k_g2a, w_g2b, k_g2b, c0, W, 8, [(2, 0), (2, 1), (3, 0), (3, 1)])
                om = []
                if kind == "p":
                    om = [hpC.alloc() for _ in range(4)]
                    sq_regs = [1, 3]
                    ntl = len(tiles)

                    def c_scores(tt_):
                        qi_ = sq_regs[tt_ % 2]
                        tcs_ = slice(128 * tt_, 128 * (tt_ + 1))
                        for h in range(4):
                            MM(Q[qi_][:, 256 * h:256 * (h + 1)], [(qT[h].b[:, tcs_], KTp[:, h, :])], qT[h].keys + [("KTp", h)],
                               bk(qi_, h // 2))

                    c_scores(0)
                    for tt in range(ntl):
                        qi = sq_regs[tt % 2]
                        sc4 = Q[qi][:, :].rearrange("p (h m) -> p h m", h=4)
                        sck = bk(qi, 0) + bk(qi, 1)
                        if tt + 1 < ntl:
                            c_scores(tt + 1)
                        mx, mx_k = stat_col(4)
                        P.op("dve", (lambda o_, i_: (lambda e: e.tensor_reduce(out=o_, in_=i_, axis=AX.X, op=ALU.max)))(mx, sc4),
                             reads=sck, writes=mx_k)
                        TS(mx, mx, -SCALE, None, ALU.mult, None, mx_k, mx_k)
                        rs, rs_k = stat_col(4)
                        phs = []
                        for h in range(4):
                            ph = hpC.alloc()
                            ACT(ph.f, sc4[:, h, :], AF.Exp, sck + mx_k, ph.keys + rs_k, bias=mx[:, h:h + 1], scale=SCALE,
                                accum=rs[:, h:h + 1])
                            phs.append(ph)
                        RECIP(rs, rs, rs_k, rs_k)
                        pn = pgC.alloc()
                        for h in range(4):
                            TS(pn.b[:, 256 * h:256 * (h + 1)], phs[h].f, rs[:, h:h + 1], None, ALU.mult, None,
                               phs[h].keys + rs_k, pn.keys)
                            hpC.free(phs[h])
                        ptb = bank(0, tt % 2).bitcast(BF16)
                        TRS([(ptb[:, 128 * i:128 * (i + 1)], pn.b[:, 128 * i:128 * (i + 1)]) for i in range(8)], identb[:],
                            pn.keys + ["identb"], bk(0, tt % 2))
                        pT = pgC.alloc()
                        ACOPY(pT.b, ptb, bk(0, tt % 2), pT.keys)
                        sg_, tl = divmod(tt, 2)
                        for h in range(4):
                            oc = 256 * (h % 2) + 128 * tl
                            MM(bank(2, h // 2)[:, oc:oc + 128],
                               [(Vbp[:, mt, 128 * h:128 * (h + 1)], pT.b[:, 128 * (2 * h + mt):128 * (2 * h + mt + 1)]) for mt in range(2)],
                               [("Vbp", 0), ("Vbp", 1)] + pT.keys, bk(2, h // 2))
                        pgC.free(pn, pT)
                        if tl == 1 or tt == ntl - 1:
                            for h in range(4):
                                ACOPY(om[h].b[:, 256 * sg_:256 * sg_ + 128 * (tl + 1)],
                                      bank(2, h // 2)[:, 256 * (h % 2):256 * (h % 2) + 128 * (tl + 1)], bk(2, h // 2), om[h].keys)
                else:
                    ob, obk = bank(2, 0), bk(2, 0)

                    def c_kt(qq):
                        bb = qq % 2
                        for s_ in range(4):
                            pt_ap = bank(3, s_ % 2).bitcast(BF16)
                            pt_k = bk(3, s_ % 2)
                            TRS([(pt_ap[:, 256 * h + 128 * mt:256 * h + 128 * (mt + 1)], Ks[:, s_ * 2 + mt, 128 * h:128 * (h + 1)])
                                 for h in range(4) for mt in range(2)], identb[:], Ks_keys + ["identb"], pt_k)
                            ACOPY(KTs[bb][:, 4 * s_:4 * s_ + 4, :], pt_ap.rearrange("p (h m) -> p h m", h=4), pt_k, KTs_keys[bb])

                    c_kt(0)
                    c_load_k(1)
                    for q_ in range(4):
                        bi = q_ % 2
                        sc4 = Q[1][0:32, :].rearrange("p (h m) -> p h m", h=4)
                        sck = bk(1, 0) + bk(1, 1)
                        for h in range(4):
                            qm = hpC.alloc()
                            TT(qm.b[:, 0:128].rearrange("p (s l) -> p s l", s=4),
                               qT[h].b[:, 32 * q_:32 * q_ + 32].unsqueeze(1).to_broadcast([128, 4, 32]),
                               seqm[:].rearrange("p (s l) -> p s l", s=4), ALU.mult, qT[h].keys + ["seqm"], qm.keys)
                            MM(Q[1][0:32, 256 * h:256 * (h + 1)],
                               [(qm.b[:, 32 * s_:32 * s_ + 32], KTs[bi][:, s_ * 4 + h, :]) for s_ in range(4)],
                               qm.keys + KTs_keys[bi], bk(1, h // 2))
                            hpC.free(qm)
                        mx, mx_k = stat_col(4)
                        P.op("dve", (lambda o_, i_: (lambda e: e.tensor_reduce(out=o_, in_=i_, axis=AX.X, op=ALU.max)))(mx[0:32, :], sc4),
                             reads=sck, writes=mx_k)
                        TS(mx[0:32, :], mx[0:32, :], -SCALE, None, ALU.mult, None, mx_k, mx_k)
                        rs, rs_k = stat_col(4)
                        phs = []
                        for h in range(4):
                            ph = hpC.alloc()
                            ACT(ph.f[0:32, :], sc4[:, h, :], AF.Exp, sck + mx_k, ph.keys + rs_k, bias=mx[0:32, h:h + 1], scale=SCALE,
                                accum=rs[0:32, h:h + 1])
                            phs.append(ph)
                        if q_ + 1 < 4:
                            c_kt(q_ + 1)
                            if q_ + 2 < 4:
                                c_load_k(q_ + 2)
                        RECIP(rs[0:32, :], rs[0:32, :], rs_k, rs_k)
                        pn = pgC.alloc()
                        for h in range(4):
                            TS(pn.b[0:32, 256 * h:256 * (h + 1)], phs[h].f[0:32, :], rs[0:32, h:h + 1], None, ALU.mult, None,
                               phs[h].keys + rs_k, pn.keys)
                            hpC.free(phs[h])
                        ptb = bank(0, q_ % 2).bitcast(BF16)
                        TRS([(ptb[:, 32 * i:32 * (i + 1)], pn.b[0:32, 128 * i:128 * (i + 1)]) for i in range(8)], identb[0:32, 0:32],
                            pn.keys + ["identb"], bk(0, q_ % 2))
                        pT = hpC.alloc()
                        ACOPY(pT.b[:, 0:256], ptb[:, 0:256], bk(0, q_ % 2), pT.keys)
                        for h in range(4):
                            for s_ in range(4):
                                col = 128 * h + 32 * q_ + 8 * s_
                                MM(ob[:, col:col + 8],
                                   [(Vs[bi][:, s_ * 2 + mt, 128 * h:128 * (h + 1)],
                                     pT.b[:, 32 * (2 * h + mt) + 8 * s_:32 * (2 * h + mt) + 8 * s_ + 8]) for mt in range(2)],
                                   Vs_keys[bi] + pT.keys, obk)
                        pgC.free(pn)
                        hpC.free(pT)
                        if q_ + 2 < 4:
                            c_load_v(q_ + 2)
                    for h in range(4):
                        omh = hpC.alloc()
                        ACOPY(omh.b[:, 0:W], ob[:, 128 * h:128 * (h + 1)], obk, omh.keys)
                        om.append(omh)
                for h in range(4):
                    hpC.free(qT[h])
                if hi == 0 and c0 == 0:
                    dump("om0", om[0].b, [128, 512], om[0].keys, BF16)
                gate_merge(pgC, w_mo, k_mo, [(o_.b[:, 0:W], o_.keys) for o_ in om], w_g2a, k_g2a, w_g2b, k_g2b, c0, W, False, pre=preC)
                for o_ in om:
                    hpC.free(o_)
            wload("D_o2")
            if hi == 0:
                dump("mgC", MG[:, :, 0:1024], [128, 8, 1024], [("mg", c) for c in range(8)])
            if stop_after in ("pC", "pC@%d" % hi):
                stopped = True
                break

            mark("h%d_D" % hi)
            w_o1, k_o1 = wget("D_o1")
            w_o2, k_o2 = wget("D_o2")
            if hi == 0:
                for nm in ("A_co", "A_g0a", "A_g0b"):
                    wload(nm)
                build_diag()
            else:
                for nm in ("E_g0", "E_u0", "E_g1", "E_u1", "E_g2", "E_u2", "E_g3", "E_u3", "E_g4", "E_g5"):
                    wload(nm)
            load_gain(gA[:], "gA", 1)
            pgD = FreeList(pg_tiles[0:9] + pg_tiles[11:15])
            dtiles = half["tiles"]
            if hi == 0:
                load_gain(gB, gB_keys, 0)
                ntiles = halves[1]["tiles"]

            qring = [0, 1] if hi == 0 else [0, 1, 3]
            xring = [0, 1] if hi == 0 else [0, 1, 2]
            ddepth = len(qring) - 1

            def d_front(n_):
                ti = dtiles[n_]
                lc = lcol(hi, ti)
                mgb = pgD.alloc()
                mv_ = mgb.b.rearrange("p (k t) -> p k t", k=8)
                if n_ % 2 == 0:
                    ACOPY(mv_, MG[:, :, lc:lc + 128], [("mg", lc // 128)], mgb.keys)
                else:
                    VCOPY(mv_, MG[:, :, lc:lc + 128], [("mg", lc // 128)], mgb.keys)
                qa = qring[n_ % len(qring)]
                MM(bank(qa, 0), [(mv_[:, k, :], w_o1[:, k, :]) for k in range(8)], mgb.keys + k_o1, bk(qa, 0))
                MM(bank(qa, 1), [(mv_[:, k, :], w_o2[:, k, :]) for k in range(8)], mgb.keys + k_o2, bk(qa, 1))
                pgD.free(mgb)
                xb = xring[n_ % len(xring)]
                DMA("sp", PXT[:, xb, :], x_all[128 * ti:128 * (ti + 1), :], [], pxkeys(xb), ("x", xb))

            def d_back(n_):
                ti = dtiles[n_]
                qa = qring[n_ % len(qring)]
                xb = xring[n_ % len(xring)]
                qkeys = bk(qa, 0) + bk(qa, 1)
                ss_ap, ss_keys = sumsq_1024(Q[qa][:, :], qkeys, pgD, junk_ps=2)
                r_ap, r_keys = rstd_from_ss(ss_ap, ss_keys, 1024, EPS)
                STT(tmpE, Q[qa][:, :], r_ap, gA[:], ALU.mult, ALU.mult, qkeys + r_keys + ["gA"], tmpE_keys)
                TT(PXT[:, xb, :], PXT[:, xb, :], tmpE, ALU.add, pxkeys(xb) + tmpE_keys, pxkeys(xb))
                DMA("sp", x1s[128 * ti:128 * (ti + 1), :], PXT[:, xb, :], pxkeys(xb), [("x1s", ti)], ("xs", xb))

            p0_state = {}

            def p0_front(n_):
                if hi != 0 or n_ >= len(ntiles):
                    return
                ti = ntiles[n_]
                p0_state[n_] = norm_front(x_all[128 * ti:128 * (ti + 1), :], 2 + n_ % 2, gB, gB_keys, pgD, junk_ps=2)

            def p0_back(n_):
                if n_ not in p0_state:
                    return
                ti = ntiles[n_]
                lc = lcol(1, ti)
                norm_back(p0_state.pop(n_), hT, lc, [("hT", lc // 128)], pgD, (3, 1))

            for n_ in range(min(ddepth, len(dtiles))):
                d_front(n_)
            p0_front(0)
            for n_ in range(len(dtiles)):
                if n_ + ddepth < len(dtiles):
                    d_front(n_ + ddepth)
                p0_front(n_ + 1)
                p0_back(n_)
                d_back(n_)
            if hi == 0:
                for n_ in range(len(dtiles), len(ntiles)):
                    p0_front(n_ + 1)
                    p0_back(n_)
            if hi == 0:
                wload("A_ca")
                wload("A_cb")
            else:
                for nm in ["E_u4", "E_u5"] + ["E_d%d" % i for i in range(11)]:
                    wload(nm)
            if stop_after in ("pD", "pD@%d" % hi):
                stopped = True
                break

        if not stopped:
            mark("E")
            load_gain(gA[:], "gA", 2)
            load_gain(gB, gB_keys, 3)
            w_gp, k_gp, w_upp, k_upp, w_dn, k_dn = [], [], [], [], [], []
            for i in range(6):
                v_, k_ = wget("E_g%d" % i)
                w_gp.append(v_)
                k_gp.append(k_)
                v_, k_ = wget("E_u%d" % i)
                w_upp.append(v_)
                k_upp.append(k_)
            for i in range(11):
                v_, k_ = wget("E_d%d" % i)
                w_dn.append(v_)
                k_dn.append(k_)
            h2T = [PGT[:, 11 + 2 * b_:13 + 2 * b_, :].rearrange("p a b -> p (a b)").bitcast(BF16).rearrange("p (k t) -> p k t", k=8)
                   for b_ in range(2)]
            h2T_keys = [pg_tiles[11 + 2 * b_].keys + pg_tiles[12 + 2 * b_].keys for b_ in range(2)]
            pgE = FreeList(pg_tiles[0:3])
            hpE = FreeList(halves_of(pg_tiles[3:9]) + halves_of(px_as_pg[0:0]))
            egroups = [[2 * g, 2 * g + 1] for g in range(8)] + [[16]]
            gu_i = [0]

            def pgu():
                b_ = [(0, 0), (0, 1), (1, 1)][gu_i[0] % 3]
                gu_i[0] += 1
                return bank(*b_), bk(*b_)

            e_pend = {}

            def e_load(gi_):
                for tt_, ti_ in enumerate(egroups[gi_]):
                    pxb_ = (2 * gi_ + tt_) % 4
                    DMA("sp", PXT[:, pxb_, :], x1s[128 * ti_:128 * (ti_ + 1), :], [("x1s", ti_)], pxkeys(pxb_), ("x", pxb_))

            def e_norm_front(gi_):
                e_pend[gi_] = [norm_front(x1s[128 * ti_:128 * (ti_ + 1), :], (2 * gi_ + tt_) % 4, gA[:], ["gA"], pgE,
                                          src_keys=[("x1s", ti_)], dve_sumsq=False, preloaded=True)
                               for tt_, ti_ in enumerate(egroups[gi_])]

            def e_norm_back(gi_):
                for tt_, jt2 in enumerate(e_pend.pop(gi_)):
                    norm_back(jt2, h2T[gi_ % 2], 128 * tt_, h2T_keys[gi_ % 2], pgE, (1, 0))

            e_load(0)
            e_norm_front(0)
            e_norm_back(0)
            pending = []

            def make_post(gi_, tt_, ti_):
                def post():
                    pxb = (2 * gi_ + tt_) % 4
                    qd_ = 2 + tt_
                    qkeys = bk(qd_, 0) + bk(qd_, 1)
                    ss_ap, ss_keys = sumsq_1024(Q[qd_][:, :], qkeys, pgE)
                    r_ap, r_keys = rstd_from_ss(ss_ap, ss_keys, 1024, EPS)
                    STT(tmpE, Q[qd_][:, :], r_ap, gB, ALU.mult, ALU.mult, qkeys + r_keys + gB_keys, tmpE_keys)
                    TT(PXT[:, pxb, :], PXT[:, pxb, :], tmpE, ALU.add, pxkeys(pxb) + tmpE_keys, pxkeys(pxb))
                    DMA("sp", y[128 * ti_:128 * (ti_ + 1), :], PXT[:, pxb, :], pxkeys(pxb), [], ("xs", pxb))
                return post

            for gi, tl in enumerate(egroups):
                Wg = 128 * len(tl)
                hb_i = gi % 2
                act = []
                for j in range(22):
                    if j in (1, 3) and pending:
                        pending.pop(0)()
                    if j == 4 and gi + 1 < len(egroups):
                        e_load(gi + 1)
                    if j == 10 and gi + 1 < len(egroups):
                        e_norm_front(gi + 1)
                    if j == 17 and gi + 1 < len(egroups):
                        e_norm_back(gi + 1)
                    gu_ap, g_k = pgu()
                    g_ap, u_ap, u_k = gu_ap[:, 0:256], gu_ap[:, 256:512], g_k
                    pi, po = j // 4, 128 * (j % 4)
                    MM(g_ap[:, 0:Wg], [(w_gp[pi][:, k, po:po + 128], h2T[hb_i][:, k, 0:Wg]) for k in range(8)],
                       k_gp[pi] + h2T_keys[hb_i], g_k)
                    MM(u_ap[:, 0:Wg], [(w_upp[pi][:, k, po:po + 128], h2T[hb_i][:, k, 0:Wg]) for k in range(8)],
                       k_upp[pi] + h2T_keys[hb_i], u_k)
                    sg_h = hpE.alloc()
                    ACT(sg_h.f[:, 0:Wg], g_ap[:, 0:Wg], AF.Silu, g_k, sg_h.keys)
                    if j % 2 == 0:
                        ah = hpE.alloc()
                        act.append(ah)
                    ah = act[j // 2]
                    TT(ah.b[:, 256 * (j % 2):256 * (j % 2) + Wg], sg_h.f[:, 0:Wg], u_ap[:, 0:Wg], ALU.mult, sg_h.keys + u_k, ah.keys)
                    hpE.free(sg_h)
                while pending:
                    pending.pop(0)()
                akeys_all = []
                for a_ in act:
                    akeys_all += a_.keys
                for tt, ti in enumerate(tl):
                    qd_ = 2 + tt
                    for hf in range(2):
                        MM(bank(qd_, hf),
                           [(act[j // 2].b[:, 256 * (j % 2) + 128 * tt:256 * (j % 2) + 128 * tt + 128],
                             w_dn[j // 2][:, j % 2, 512 * hf:512 * (hf + 1)]) for j in range(22)],
                           akeys_all + [k for kk in k_dn for k in kk], bk(qd_, hf))
                    pending.append(make_post(gi, tt, ti))
                for a_ in act:
                    hpE.free(a_)
            while pending:
                pending.pop(0)()

        P.emit(st)
        mark('end')
        info = dict(n_ops=len(P.ops), n_waits=P.n_waits, n_dma_sems=P.n_dma_sems, marks=marks)
    return nc, dbg_outs, info


def _consts():
    bf = ml_dtypes.bfloat16
    c = {}
    c["c_identb"] = np.eye(128, dtype=np.float32).astype(bf)
    c["c_identf"] = np.eye(128, dtype=np.float32)
    sp = np.zeros((128, 512), np.float32); sp[:, ::32] = 1.0
    ss = np.zeros((128, 128), np.float32); ss[:, ::8] = 1.0
    c["c_startp"], c["c_starts"] = sp, ss
    s = np.arange(128)[:, None]; t = np.arange(128)[None, :]
    c["c_causp"] = ((s // 32 == t // 32) & (s <= t)).astype(np.float32)
    c["c_causs"] = ((s // 8 == t // 8) & (s <= t)).astype(np.float32)
    c["c_crowp"] = (s // 32 == np.arange(4)[None, :]).astype(np.float32)
    c["c_crows"] = (s // 8 == np.arange(16)[None, :]).astype(np.float32)
    sm = (np.arange(4)[:, None] == (np.arange(32)[None, :] // 8)).astype(np.float32)
    c["c_seqm"] = np.ascontiguousarray(np.broadcast_to(sm.reshape(1, 128), (128, 128))).astype(bf)
    return c


def make_in_maps(inp):
    f = np.float32
    shared = {
        "w_in": np.ascontiguousarray(inp["w_in"][0], f),
        "w_conv_out": np.ascontiguousarray(inp["w_conv_out"][0], f),
        "w_hg_out": np.ascontiguousarray(inp["w_hg_out"][0], f),
        "w_mem_out": np.ascontiguousarray(inp["w_mem_out"][0], f),
        "w_mem_kv": np.ascontiguousarray(inp["w_mem_kv"][0], f),
        "w_out": np.ascontiguousarray(inp["w_out"][0], f),
        "w_gate": np.ascontiguousarray(inp["w_ffn_gate"][0], f),
        "w_up": np.ascontiguousarray(inp["w_ffn_up"][0], f),
        "w_down": np.ascontiguousarray(inp["w_ffn_down"][0], f),
        "vec36": np.ascontiguousarray(np.concatenate(
            [inp["conv_w"][0], inp["conv_b"], inp["conv_ln_g"], inp["conv_ln_b"], inp["hg_lb_logits"]], axis=0), f),
        "hgn": np.ascontiguousarray(inp["hg_norm_g"][0].reshape(128, 1), f),
        "gains": np.ascontiguousarray(np.concatenate(
            [inp["norm_pre_mix"], inp["norm_post_mix"], inp["norm_pre_ffn"], inp["norm_post_ffn"], inp["mem_norm_g"]], axis=0), f),
    }
    shared.update(_consts())
    maps = []
    for b in range(NCORES):
        m = dict(shared)
        m["x_all"] = np.ascontiguousarray(np.concatenate(
            [inp["x_prompt"][b], inp["x_sample"][16 * b:16 * b + 16].reshape(128, 1024)], axis=0), f)
        m["mem"] = np.ascontiguousarray(inp["mem_prompt"][b], f)
        m["sconv"] = np.ascontiguousarray(inp["state_conv"][0, 16 * b:16 * b + 16], f)
        m["shg"] = np.ascontiguousarray(inp["state_hgrn"][0, 16 * b:16 * b + 16], f)
        m["ck"] = np.ascontiguousarray(inp["cache_mem_k"][0, 16 * b:16 * b + 16].reshape(16, 256, 512), f)
        m["cv"] = np.ascontiguousarray(inp["cache_mem_v"][0, 16 * b:16 * b + 16].reshape(16, 256, 512), f)
        maps.append(m)
    return maps


_CACHE = {}


def kernel(**inp):
    if "nc" not in _CACHE:
        _CACHE["nc"] = build_program()[0]
    nc = _CACHE["nc"]
    maps = make_in_maps(inp)
    res = run_bass_kernel_spmd(nc, maps, core_ids=list(range(NCORES)))
    R = res.results
    yp = np.stack([R[b]["y"][:2048] for b in range(NCORES)], 0)
    ys = np.concatenate([R[b]["y"][2048:].reshape(16, 8, 1024) for b in range(NCORES)], 0)
    ncp = np.stack([R[b]["o_ncp"] for b in range(NCORES)], 0)[None]
    nhp = np.stack([R[b]["o_nhp"] for b in range(NCORES)], 0)[None]
    mk = np.stack([R[b]["o_mk"].reshape(256, 4, 128) for b in range(NCORES)], 0)[None]
    mv = np.stack([R[b]["o_mv"].reshape(256, 4, 128) for b in range(NCORES)], 0)[None]
    ncs = np.concatenate([R[b]["o_ncs"] for b in range(NCORES)], 0)[None]
    nhs = np.concatenate([R[b]["o_nhs"] for b in range(NCORES)], 0)[None]
    return tuple(np.ascontiguousarray(a, np.float32) for a in (yp, ys, ncp, nhp, mk, mv, ncs, nhs))
```

```python
import os
import math
from collections import deque
from contextlib import ExitStack

import numpy as np
import ml_dtypes

import concourse.bass as bass
import concourse.mybir as mybir
from concourse.bass_utils import run_bass_kernel_spmd

F32 = mybir.dt.float32
BF16 = mybir.dt.bfloat16
AF = mybir.ActivationFunctionType
ALU = mybir.AluOpType
AX = mybir.AxisListType

ENGS = ("pe", "act", "dve", "pool", "sp")
EPS = 1e-6
NCORES = 8
T_ALL = 2176
D_IN = 6656


class Op:
    __slots__ = ("eng", "fn", "deps", "is_dma", "sem_key", "needs_inc", "inc_val", "idx", "clock")

    def __init__(self, eng, fn, is_dma, sem_key):
        self.eng = eng
        self.fn = fn
        self.deps = []
        self.is_dma = is_dma
        self.sem_key = sem_key
        self.needs_inc = False
        self.inc_val = None
        self.clock = None


class Prog:
    def __init__(self, nc):
        self.nc = nc
        self.ops = []
        self.last_write = {}
        self.readers = {}

    def op(self, eng, fn, reads=(), writes=(), dma_key=None):
        o = Op(eng, fn, dma_key is not None, (eng, dma_key) if dma_key is not None else None)
        o.idx = len(self.ops)
        deps = {}
        for r in reads:
            w = self.last_write.get(r)
            if w is not None:
                deps[w.idx] = w
        for r in writes:
            w = self.last_write.get(r)
            if w is not None:
                deps[w.idx] = w
            for rd in self.readers.get(r, ()):
                deps[rd.idx] = rd
        o.deps = list(deps.values())
        for r in reads:
            self.readers.setdefault(r, []).append(o)
        for r in writes:
            self.last_write[r] = o
            self.readers[r] = []
        self.ops.append(o)
        return o

    def emit(self, stack):
        nc = self.nc
        for o in self.ops:
            o.deps = [d for d in o.deps if not (d.eng == "pe" and o.eng == "pe" and not d.is_dma and not o.is_dma)]
            for d in o.deps:
                d.needs_inc = True
        sems = {e: stack.enter_context(nc.semaphore("s_" + e)) for e in ENGS}
        dma_sems = {}
        counts = {e: 0 for e in ENGS}
        dma_counts = {}
        for o in self.ops:
            if o.is_dma:
                if o.sem_key not in dma_sems:
                    dma_sems[o.sem_key] = stack.enter_context(nc.semaphore("d%d" % len(dma_sems)))
                    dma_counts[o.sem_key] = 0
                dma_counts[o.sem_key] += 16
                o.inc_val = (dma_sems[o.sem_key], dma_counts[o.sem_key])
            elif o.needs_inc:
                counts[o.eng] += 1
                o.inc_val = (sems[o.eng], counts[o.eng])
        self.n_dma_sems = len(dma_sems)
        eng_clock = {e: {} for e in ENGS}
        waits = {}
        for o in self.ops:
            ck = eng_clock[o.eng]
            wm = {}
            for d in sorted(o.deps, key=lambda d: d.idx):
                sem, val = d.inc_val
                if ck.get(sem, 0) >= val:
                    continue
                if wm.get(sem, 0) < val:
                    wm[sem] = val
                for s, v in d.clock.items():
                    if ck.get(s, 0) < v:
                        ck[s] = v
            waits[o.idx] = list(wm.items())
            oc = dict(ck)
            if o.inc_val is not None:
                s, v = o.inc_val
                if oc.get(s, 0) < v:
                    oc[s] = v
            o.clock = oc
        final = [(dma_sems[k], dma_counts[k]) for k in dma_sems]
        final += [(sems[e], counts[e]) for e in ENGS if counts[e] > 0]
        self.n_waits = sum(len(w) for w in waits.values())
        by_eng = {e: [o for o in self.ops if o.eng == e] for e in ENGS}
        with nc.Block() as block:
            def mk(ename):
                def body(eng):
                    for o in by_eng[ename]:
                        for s, v in waits[o.idx]:
                            eng.wait_ge(s, v)
                        ins = o.fn(eng)
                        if o.inc_val is not None:
                            ins.then_inc(o.inc_val[0], 16 if o.is_dma else 1)
                    if ename == "sp":
                        for s, v in final:
                            eng.wait_ge(s, v)
                return body
            block.tensor(mk("pe"))
            block.scalar(mk("act"))
            block.vector(mk("dve"))
            block.gpsimd(mk("pool"))
            block.sync(mk("sp"))


class Tile:
    def __init__(self, ap_f32, keys):
        self.f = ap_f32
        self.b = ap_f32.bitcast(BF16)
        self.keys = list(keys)

    def half(self, i):
        return Half(self.f[:, 256 * i:256 * (i + 1)], [self.keys[i]])


class Half:
    def __init__(self, ap_f32, keys):
        self.f = ap_f32
        self.b = ap_f32.bitcast(BF16)
        self.keys = list(keys)


class FreeList:
    def __init__(self, items):
        self.q = deque(items)

    def alloc(self):
        if not self.q:
            raise RuntimeError("scratch pool exhausted")
        return self.q.popleft()

    def free(self, *ts):
        for t in ts:
            self.q.append(t)


def build_program(debug=(), stop_after=None):
    nc = bass.Bass("TRN2", target_bir_lowering=False)

    def din(name, shape, dt=F32):
        return nc.dram_tensor(name, list(shape), dt, kind="ExternalInput").ap()

    def dout(name, shape, dt=F32):
        return nc.dram_tensor(name, list(shape), dt, kind="ExternalOutput").ap()

    x_all = din("x_all", [T_ALL, 1024])
    mem = din("mem", [256, 1024])
    sconv = din("sconv", [16, 30, 512])
    shg = din("shg", [16, 4, 128, 128])
    ck = din("ck", [16, 256, 512])
    cv = din("cv", [16, 256, 512])
    w_in = din("w_in", [1024, D_IN])
    w_conv_out = din("w_conv_out", [512, 1024])
    w_hg_out = din("w_hg_out", [512, 1024])
    w_mem_out = din("w_mem_out", [512, 1024])
    w_mem_kv = din("w_mem_kv", [1024, 1024])
    w_out = din("w_out", [1024, 1024])
    w_gate = din("w_gate", [1024, 2816])
    w_up = din("w_up", [1024, 2816])
    w_down = din("w_down", [2816, 1024])
    vec36 = din("vec36", [36, 512])
    hgn = din("hgn", [128, 1])
    gains = din("gains", [5, 1024])
    c_identb = din("c_identb", [128, 128], BF16)
    c_identf = din("c_identf", [128, 128])
    c_startp = din("c_startp", [128, 512])
    c_starts = din("c_starts", [128, 128])
    c_causp = din("c_causp", [128, 128])
    c_causs = din("c_causs", [128, 128])
    c_crowp = din("c_crowp", [128, 4])
    c_crows = din("c_crows", [128, 16])
    c_seqm = din("c_seqm", [128, 128], BF16)

    y = dout("y", [T_ALL, 1024])
    o_ncp = dout("o_ncp", [30, 512])
    o_nhp = dout("o_nhp", [4, 128, 128])
    o_mk = dout("o_mk", [256, 512])
    o_mv = dout("o_mv", [256, 512])
    o_ncs = dout("o_ncs", [16, 30, 512])
    o_nhs = dout("o_nhs", [16, 4, 128, 128])
    x1s = nc.dram_tensor("x1s", [T_ALL, 1024], F32, kind="Internal").ap()

    dbg_outs = {}
    st = ExitStack()
    with st:
        def sb(name, shape, dt):
            return st.enter_context(nc.sbuf_tensor(name, list(shape), dt))

        def ps(name, shape, dt):
            return st.enter_context(nc.psum_tensor(name, list(shape), dt))

        P = Prog(nc)

        NSLOT = 33
        AR = sb("arena", [128, NSLOT * 2048], BF16)
        NPG = 17
        PGT = sb("pgt", [128, NPG, 512], F32)
        PXT = sb("pxt", [128, 4, 1024], F32)
        gA = sb("gA", [128, 1024], F32)
        UB = sb("ubuf", [128, 4, 608], BF16)
        identb = sb("identb", [128, 128], BF16)
        identf = sb("identf", [128, 128], F32)
        ones32 = sb("ones32", [128, 128], F32)
        mhalf = sb("mhalf", [128, 1], F32)
        epst = sb("epst", [128, 1], F32)
        startp = sb("startp", [128, 512], F32)
        starts = sb("starts", [128, 128], F32)
        causp = sb("causp", [128, 128], F32)
        causs = sb("causs", [128, 128], F32)
        crowp = sb("crowp", [128, 4], F32)
        crows = sb("crows", [128, 16], F32)
        seqm = sb("seqm", [128, 128], BF16)
        vecT = sb("vecT", [128, 4, 36], F32)
        lbv = sb("lbv", [128, 4], F32)
        omlv = sb("omlv", [128, 4], F32)
        hgnv = sb("hgnv", [128, 1], F32)
        KTp = sb("KTp", [128, 4, 256], BF16)
        Vbp = sb("Vbp", [128, 2, 512], BF16)
        S32 = sb("S32", [128, 4, 128], F32)
        Sbf = sb("Sbf", [128, 4, 128], BF16)
        stat = sb("stat", [128, 64], F32)
        decs = sb("decs", [128, 4, 16], F32)

        Q = [ps("Q%d" % i, [128, 1024], F32) for i in range(4)]

        def bank(q, h):
            return Q[q][:, 512 * h:512 * (h + 1)]

        def bk(q, h):
            return [("ps", q, h)]

        pg_tiles = [Tile(PGT[:, i, :], [("g", i, 0), ("g", i, 1)]) for i in range(NPG)]
        px_as_pg = []
        for b_ in range(4):
            for hh in range(2):
                px_as_pg.append(Tile(PXT[:, b_, 512 * hh:512 * (hh + 1)], [("x", b_, 2 * hh), ("x", b_, 2 * hh + 1)]))

        def pxkeys(b_):
            return [("x", b_, i) for i in range(4)]

        def halves_of(tiles):
            out = []
            for t in tiles:
                out.append(t.half(0))
                out.append(t.half(1))
            return out

        def slot(s0, n_el):
            return AR[:, s0 * 2048:s0 * 2048 + n_el]

        def akeys(s0, n_el):
            return [("a", s) for s in range(s0, s0 + (n_el * 2 + 4095) // 4096)]

        def wview(s0, kc, cols):
            return slot(s0, kc * cols).rearrange("p (k e) -> p k e", k=kc)

        def DMA(eng, out, in_, reads, writes, key):
            return P.op(eng, lambda e: e.dma_start(out=out, in_=in_), reads=reads, writes=writes, dma_key=key)

        def ACT(out, in_, func, reads, writes, bias=None, scale=None, accum=None):
            kw = {}
            if bias is not None:
                kw["bias"] = bias
            if scale is not None:
                kw["scale"] = scale
            if accum is not None:
                kw["accum_out"] = accum
            return P.op("act", lambda e: e.activation(out=out, in_=in_, func=func, **kw), reads=reads, writes=writes)

        def ACOPY(out, in_, reads, writes):
            return P.op("act", lambda e: e.copy(out, in_), reads=reads, writes=writes)

        def VCOPY(out, in_, reads, writes, eng="dve"):
            return P.op(eng, lambda e: e.tensor_copy(out, in_), reads=reads, writes=writes)

        def TT(out, a, b, op, reads, writes, eng="dve"):
            return P.op(eng, lambda e: e.tensor_tensor(out, a, b, op=op), reads=reads, writes=writes)

        def TS(out, a, s1, s2, op0, op1, reads, writes, eng="dve"):
            if op1 is None:
                return P.op(eng, lambda e: e.tensor_scalar(out, a, s1, None, op0=op0), reads=reads, writes=writes)
            return P.op(eng, lambda e: e.tensor_scalar(out, a, s1, s2, op0=op0, op1=op1), reads=reads, writes=writes)

        def STT(out, a, s, b, op0, op1, reads, writes):
            return P.op("dve", lambda e: e.scalar_tensor_tensor(out, a, s, b, op0=op0, op1=op1), reads=reads, writes=writes)

        def RECIP(out, in_, reads, writes):
            return P.op("dve", lambda e: e.reciprocal(out, in_), reads=reads, writes=writes)

        mm_count = [0]
        marks = []

        def mark(name):
            marks.append((name, mm_count[0]))

        def MM(out_ap, pairs, reads, writes, start=True, stop=True):
            pairs = list(pairs)
            mm_count[0] += len(pairs)

            def fn(e):
                n = len(pairs)
                ins = None
                for i, (l, r) in enumerate(pairs):
                    ins = e.matmul(out_ap, lhsT=l, rhs=r, start=(start and i == 0), stop=(stop and i == n - 1))
                return ins
            return P.op("pe", fn, reads=reads, writes=writes)

        def TRS(items, ident, reads, writes):
            items = list(items)
            mm_count[0] += len(items)

            def fn(e):
                ins = None
                for (o_, i_) in items:
                    ins = e.transpose(o_, i_, ident)
                return ins
            return P.op("pe", fn, reads=reads, writes=writes)

        def load_w(s0, src, kc, extra_writes=()):
            cols = src.shape[1]
            v = wview(s0, kc, cols)
            keys = akeys(s0, kc * cols)
            DMA("pool", v, src.rearrange("(k p) e -> p k e", p=128), [], keys + list(extra_writes), ("a", s0))
            return v, keys

        all_hm_keys = [("hT", c) for c in range(9)] + [("mg", c) for c in range(9)]
        WDEF = {
            "K_k": (8, w_mem_kv[:, 0:512], 8), "K_v": (10, w_mem_kv[:, 512:1024], 8),
            "A_ca": (0, w_in[:, 0:512], 8), "A_cb": (2, w_in[:, 512:1024], 8), "A_co": (12, w_conv_out, 4),
            "A_g0a": (14, w_in[:, 3584:4096], 8), "A_g0b": (16, w_in[:, 4096:4608], 8),
            "B_q": (0, w_in[:, 1024:1536], 8), "B_f": (2, w_in[:, 1536:2048], 8), "B_i": (4, w_in[:, 2048:2560], 8),
            "B_hgt": (6, w_in[:, 2560:3072], 8), "B_ho": (8, w_hg_out, 4),
            "B_g1a": (10, w_in[:, 4608:5120], 8), "B_g1b": (12, w_in[:, 5120:5632], 8),
            "C_mq": (0, w_in[:, 3072:3584], 8), "C_mo": (2, w_mem_out, 4),
            "C_g2a": (4, w_in[:, 5632:6144], 8), "C_g2b": (6, w_in[:, 6144:6656], 8),
            "D_o1": (0, w_out[:, 0:512], 8), "D_o2": (2, w_out[:, 512:1024], 8),
        }
        _gslots = [4, 8, 12, 16, 20, 22]
        _uslots = [6, 10, 14, 18, 0, 2]
        for i in range(6):
            cw = 512 if i < 5 else 256
            WDEF["E_g%d" % i] = (_gslots[i], w_gate[:, 512 * i:512 * i + cw], 8)
            WDEF["E_u%d" % i] = (_uslots[i], w_up[:, 512 * i:512 * i + cw], 8)
        _dslots = [3] + list(range(23, 33))
        for i in range(11):
            WDEF["E_d%d" % i] = (_dslots[i], w_down[256 * i:256 * (i + 1), :], 2)

        def wget(name):
            s0, src, kc = WDEF[name]
            cols = src.shape[1]
            return wview(s0, kc, cols), akeys(s0, kc * cols)

        def wload(name):
            s0, src, kc = WDEF[name]
            cols = src.shape[1]
            nsl = (kc * cols * 2 + 4095) // 4096
            extra = []
            if name.startswith("E_"):
                if s0 + nsl > 18 and s0 < 23:
                    extra += [("hT", c) for c in range(9)]
                if s0 + nsl > 23:
                    extra += [("mg", c) for c in range(9)]
            return load_w(s0, src, kc, extra_writes=extra)

        stat_i = [0]

        def stat_col(n=1):
            i = stat_i[0]
            if i + n > 64:
                i = 0
            stat_i[0] = i + n
            return stat[:, i:i + n], [("st", j) for j in range(i, i + n)]

        def dump(name, ap, shape, reads, dt=F32):
            if name not in debug:
                return
            t = dout("dbg_" + name, shape, dt)
            dbg_outs[name] = t
            DMA("sp", t, ap, reads, [], ("dbg", name))

        def pow_mhalf(ap, keys, W):
            ACT(ap, ap, AF.Sqrt, keys, keys)
            RECIP(ap, ap, keys, keys)

        def rstd_from_ss(ss_ap, ss_keys, n, eps):
            r_ap, r_keys = stat_col()
            ACT(r_ap, ss_ap, AF.Sqrt, ss_keys + ["epst"], r_keys, scale=1.0 / n, bias=epst[:, 0:1])
            RECIP(r_ap, r_ap, r_keys, r_keys)
            return r_ap, r_keys

        def sumsq_1024(src, src_keys, pool, junk_ps=None):
            if junk_ps is not None:
                a_ap, a_keys = stat_col()
                ACT(Q[junk_ps][:, :], src, AF.Square, src_keys, bk(junk_ps, 0) + bk(junk_ps, 1) + a_keys, accum=a_ap)
                return a_ap, a_keys
            jt = pool.alloc()
            a_ap, a_keys = stat_col()
            b_ap, b_keys = stat_col()
            ACT(jt.f, src[:, 0:512], AF.Square, src_keys, jt.keys + a_keys, accum=a_ap)
            ACT(jt.f, src[:, 512:1024], AF.Square, src_keys, jt.keys + b_keys, accum=b_ap)
            TT(a_ap, a_ap, b_ap, ALU.add, a_keys + b_keys, a_keys)
            pool.free(jt)
            return a_ap, a_keys

        for dst, src, key in ((identb, c_identb, "identb"), (identf, c_identf, "identf"), (startp, c_startp, "startp"),
                              (starts, c_starts, "starts"), (causp, c_causp, "causp"), (causs, c_causs, "causs"),
                              (crowp, c_crowp, "crowp"), (crows, c_crows, "crows"), (seqm, c_seqm, "seqm"),
                              (hgnv, hgn, "hgnv")):
            DMA("sp", dst[:], src, [], [key], ("c", key))
        P.op("dve", lambda e: e.memset(ones32[:], 1.0), writes=["ones32"])
        P.op("dve", lambda e: e.memset(mhalf[:], -0.5), writes=["mhalf"])
        P.op("dve", lambda e: e.memset(epst[:], EPS), writes=["epst"])
        P.op("dve", lambda e: e.memset(UB[:], 0.0), writes=["ub"])
        P.op("dve", lambda e: e.memset(S32[:], 0.0), writes=[("S32", h) for h in range(4)])
        P.op("dve", lambda e: e.memset(Sbf[:], 0.0), writes=[("Sbf", h) for h in range(4)])

        DMA("sp", PXT[0:36, 0, 0:512], vec36, [], pxkeys(0), ("x", 0))
        TRS([(bank(0, 0)[:, 36 * j:36 * (j + 1)], PXT[0:36, 0, 128 * j:128 * (j + 1)]) for j in range(4)],
            identf[0:36, 0:36], pxkeys(0) + ["identf"], bk(0, 0))
        ACOPY(vecT[:].rearrange("p j k -> p (j k)"), bank(0, 0)[:, 0:144], bk(0, 0), ["vecT"])
        TT(lbv[:], vecT[:, :, 34], vecT[:, :, 35], ALU.subtract, ["vecT"], ["lbv"])
        ACT(lbv[:], lbv[:], AF.Sigmoid, ["lbv"], ["lbv"])
        TS(omlv[:], lbv[:], -1.0, 1.0, ALU.mult, ALU.add, ["lbv"], ["omlv"])

        def load_gain(dst, key, row):
            DMA("sp", dst, gains[row:row + 1, :].partition_broadcast(128), [], key if isinstance(key, list) else [key], ("gain", str(key)))

        def norm_front(src_rows, pxb, gain_ap, gain_keys, pool, src_keys=(), dve_sumsq=False, preloaded=False, junk_ps=None):
            xt = PXT[:, pxb, :]
            if not preloaded:
                DMA("sp", xt, src_rows, list(src_keys), pxkeys(pxb), ("x", pxb))
            jt2 = pool.alloc()
            if dve_sumsq:
                ss_ap, ss_keys = stat_col()
                P.op("dve", (lambda o_, a_, acc_: (lambda e: e.scalar_tensor_tensor(o_, a_, 1.0, a_, op0=ALU.mult, op1=ALU.mult,
                                                                               accum_out=acc_)))(jt2.f, xt[:, 0:512], ss_ap),
                     reads=pxkeys(pxb), writes=jt2.keys + ss_keys)
                s2_ap, s2_keys = stat_col()
                P.op("dve", (lambda o_, a_, acc_: (lambda e: e.scalar_tensor_tensor(o_, a_, 1.0, a_, op0=ALU.mult, op1=ALU.mult,
                                                                               accum_out=acc_)))(jt2.f, xt[:, 512:1024], s2_ap),
                     reads=pxkeys(pxb), writes=jt2.keys + s2_keys)
                TT(ss_ap, ss_ap, s2_ap, ALU.add, ss_keys + s2_keys, ss_keys)
            else:
                ss_ap, ss_keys = sumsq_1024(xt, pxkeys(pxb), pool, junk_ps=junk_ps)
            r_ap, r_keys = rstd_from_ss(ss_ap, ss_keys, 1024, EPS)
            hb = jt2.b
            STT(hb, xt, r_ap, gain_ap, ALU.mult, ALU.mult, pxkeys(pxb) + r_keys + gain_keys, jt2.keys)
            return jt2

        def norm_back(jt2, dstT, dcol, dkeys, pool, tb):
            hb = jt2.b
            pb = bank(*tb).bitcast(BF16)
            TRS([(pb[:, 128 * k:128 * (k + 1)], hb[:, 128 * k:128 * (k + 1)]) for k in range(8)], identb[:],
                jt2.keys + ["identb"], bk(*tb))
            ACOPY(dstT[:, :, dcol:dcol + 128], pb.rearrange("p (k t) -> p k t", k=8), bk(*tb), dkeys)
            pool.free(jt2)

        def norm_tile_to_T(src_rows, pxb, dstT, dcol, dkeys, gain_ap, gain_keys, pool, tb, src_keys=(), junk_ps=None):
            jt2 = norm_front(src_rows, pxb, gain_ap, gain_keys, pool, src_keys, junk_ps=junk_ps)
            norm_back(jt2, dstT, dcol, dkeys, pool, tb)

        gB = PGT[:, 15:17, :].rearrange("p a b -> p (a b)")
        gB_keys = pg_tiles[15].keys + pg_tiles[16].keys
        tmpE = PGT[:, 9:11, :].rearrange("p a b -> p (a b)")
        tmpE_keys = pg_tiles[9].keys + pg_tiles[10].keys
        for nm in ("A_ca", "A_cb", "A_co", "A_g0a", "A_g0b", "K_k", "K_v"):
            wload(nm)
        load_gain(gB, gB_keys, 4)
        diag = [slot(4 + 2 * j, 31 * 128).rearrange("p (k c) -> p k c", k=31) for j in range(4)]
        dkeys = [akeys(4 + 2 * j, 31 * 128) for j in range(4)]

        def build_diag():
            for j in range(4):
                TT(diag[j], identf[:].unsqueeze(1).to_broadcast([128, 31, 128]),
                   vecT[:, j, 0:31].unsqueeze(2).to_broadcast([128, 31, 128]), ALU.mult, ["identf", "vecT"], dkeys[j])

        def phaseK():
            wK, wK_keys = wget("K_k")
            wV, wV_keys = wget("K_v")
            memT = wview(6, 8, 256)
            memT_keys = akeys(6, 2048)
            pgK = FreeList(pg_tiles[0:9])
            kfr = [norm_front(mem[128 * mt:128 * (mt + 1), :], 1 + mt, gB, gB_keys, pgK, junk_ps=2) for mt in range(2)]
            for mt in range(2):
                norm_back(kfr[mt], memT, 128 * mt, memT_keys, pgK, (3, 1))
            for mt in range(2):
                for which, wv, wkeys, odram in ((0, wK, wK_keys, o_mk), (1, wV, wV_keys, o_mv)):
                    bq = which
                    MM(bank(0, bq), [(memT[:, k, 128 * mt:128 * (mt + 1)], wv[:, k, :]) for k in range(8)],
                       memT_keys + wkeys, bk(0, bq))
                    t = pg_tiles[2 * mt + which]
                    ACOPY(t.f, bank(0, bq), bk(0, bq), t.keys)
                    DMA("sp", odram[128 * mt:128 * (mt + 1), :], t.f, t.keys, [], ("o", 2 * mt + which))
                    if which == 1:
                        VCOPY(Vbp[:, mt, :], t.f, t.keys, [("Vbp", mt)])
            for h in range(4):
                bq = h % 2
                MM(bank(1, bq)[:, 0:256], [(wK[:, k, 128 * h:128 * (h + 1)], memT[:, k, :]) for k in range(8)],
                   memT_keys + wK_keys, bk(1, bq))
                ACOPY(KTp[:, h, :], bank(1, bq)[:, 0:256], bk(1, bq), [("KTp", h)])
            dump("KTp", KTp[:], [128, 4, 256], [("KTp", h) for h in range(4)], BF16)

        hT = slot(18, 8 * 1152).rearrange("p (k t) -> p k t", k=8)
        MG = AR[:, 23 * 2048:33 * 2048].bitcast(F32)[:, 0:8 * 1152].rearrange("p (k t) -> p k t", k=8)
        halves = [
            dict(tiles=list(range(0, 8)), groups=[(0, 512, "p", [0, 1, 2, 3]), (512, 512, "p", [4, 5, 6, 7])]),
            dict(tiles=list(range(8, 16)) + [16],
                 groups=[(0, 512, "p", [8, 9, 10, 11]), (512, 512, "p", [12, 13, 14, 15]), (1024, 128, "s", [16])]),
        ]

        def lcol(hi, ti):
            return 128 * ti if hi == 0 else 128 * (ti - 8)

        def hkeys(c0, W):
            return [("hT", c) for c in range(c0 // 128, (c0 + W) // 128)]

        def mkeys(c0, W):
            return [("mg", c) for c in range(c0 // 128, (c0 + W) // 128)]

        def gate_pre(pool, w_ga, k_ga, w_gb, k_gb, c0, W, n_pre, banks, dm0=0):
            hk = hkeys(c0, W)
            out = []
            for dm in range(dm0, n_pre):
                bg = banks[dm % len(banks)]
                wg, kg = (w_ga, k_ga) if dm < 4 else (w_gb, k_gb)
                dmo = 128 * (dm % 4)
                MM(bank(*bg)[:, 0:W], [(wg[:, k, dmo:dmo + 128], hT[:, k, c0:c0 + W]) for k in range(8)], kg + hk, bk(*bg))
                sg_t = pool.alloc()
                ACT(sg_t.f[:, 0:W], bank(*bg)[:, 0:W], AF.Sigmoid, bk(*bg), sg_t.keys)
                out.append(sg_t)
            return out

        def gate_merge(pool, w_pr, k_pr, in_list, w_ga, k_ga, w_gb, k_gb, c0, W, first, pre=()):
            hk = hkeys(c0, W)
            in_keys = []
            for (_, ks) in in_list:
                in_keys += ks
            pre = list(pre)
            for dm in range(8):
                bp, bg = ((0, 0), (0, 1)) if dm % 2 == 0 else ((1, 0), (1, 1))
                MM(bank(*bp)[:, 0:W], [(w_pr[:, j, 128 * dm:128 * (dm + 1)], ap) for j, (ap, _) in enumerate(in_list)],
                   k_pr + in_keys, bk(*bp))
                if dm < len(pre):
                    sg_t = pre[dm]
                else:
                    wg, kg = (w_ga, k_ga) if dm < 4 else (w_gb, k_gb)
                    dmo = 128 * (dm % 4)
                    MM(bank(*bg)[:, 0:W], [(wg[:, k, dmo:dmo + 128], hT[:, k, c0:c0 + W]) for k in range(8)], kg + hk, bk(*bg))
                    sg_t = pool.alloc()
                    ACT(sg_t.f[:, 0:W], bank(*bg)[:, 0:W], AF.Sigmoid, bk(*bg), sg_t.keys)
                if first:
                    TT(MG[:, dm, c0:c0 + W], bank(*bp)[:, 0:W], sg_t.f[:, 0:W], ALU.mult, bk(*bp) + sg_t.keys, mkeys(c0, W))
                else:
                    TT(sg_t.f[:, 0:W], bank(*bp)[:, 0:W], sg_t.f[:, 0:W], ALU.mult, bk(*bp) + sg_t.keys, sg_t.keys)
                    TT(MG[:, dm, c0:c0 + W], MG[:, dm, c0:c0 + W], sg_t.f[:, 0:W], ALU.add, mkeys(c0, W) + sg_t.keys,
                       mkeys(c0, W))
                pool.free(sg_t)

        stopped = False
        for hi, half in enumerate(halves):
            mark("h%d_0" % hi)
            if hi == 0:
                load_gain(gA[:], "gA", 0)
                pg0 = FreeList(pg_tiles)
                p0t = half["tiles"]

                def p0_front(n_):
                    ti_ = p0t[n_]
                    return norm_front(x_all[128 * ti_:128 * (ti_ + 1), :], n_ % 4, gA[:], ["gA"], pg0, junk_ps=2)

                fr_ = p0_front(0)
                for n_, ti in enumerate(p0t):
                    lc = lcol(hi, ti)
                    nx_ = p0_front(n_ + 1) if n_ + 1 < len(p0t) else None
                    norm_back(fr_, hT, lc, [("hT", lc // 128)], pg0, (3, 1))
                    fr_ = nx_
            if hi == 0:
                phaseK()
                build_diag()
                dump("hT0", hT[:, :, 0:1024], [128, 8, 1024], [("hT", c) for c in range(8)], BF16)
            if stop_after in ("p0", "p0@%d" % hi):
                stopped = True
                break

            mark("h%d_A" % hi)
            w_ca, k_ca = wget("A_ca")
            w_cb, k_cb = wget("A_cb")
            w_co, k_co = wget("A_co")
            w_g0a, k_g0a = wget("A_g0a")
            w_g0b, k_g0b = wget("A_g0b")
            pgA = FreeList(pg_tiles + px_as_pg[2:])
            ubs = UB[:].rearrange("p j (s t) -> p j s t", s=16)
            for (c0, W, kind, tiles) in half["groups"]:
                hk = hkeys(c0, W)
                if kind == "p":
                    if not (hi == 0 and c0 == 0):
                        VCOPY(UB[:, :, 0:30], UB[:, :, 512:542], ["ub"], ["ub"])
                else:
                    xt = PXT[:, 0, :]
                    t_keys = pxkeys(0)
                    for sg in range(4):
                        DMA("sp", xt[0:120, 0:512], sconv[4 * sg:4 * sg + 4].rearrange("s r c -> (s r) c"), [], t_keys, ("x", 0))
                        for s_ in range(4):
                            DMA("sp", o_ncs[4 * sg + s_, 0:22, :], xt[30 * s_ + 8:30 * s_ + 30, 0:512], t_keys, [], ("xo", 0))
                        TRS([(bank(3, 0)[:, 120 * j:120 * (j + 1)], xt[0:120, 128 * j:128 * (j + 1)]) for j in range(4)],
                            identf[0:120, 0:120], t_keys + ["identf"], bk(3, 0))
                        ACOPY(ubs[:, :, 4 * sg:4 * sg + 4, 0:30],
                              bank(3, 0)[:, 0:480].rearrange("p (j s r) -> p j s r", j=4, s=4), bk(3, 0), ["ub"])
                for j in range(4):
                    ba, bb = ((0, 0), (0, 1)) if j % 2 == 0 else ((2, 0), (2, 1))
                    MM(bank(*ba)[:, 0:W], [(w_ca[:, k, 128 * j:128 * (j + 1)], hT[:, k, c0:c0 + W]) for k in range(8)],
                       k_ca + hk, bk(*ba))
                    MM(bank(*bb)[:, 0:W], [(w_cb[:, k, 128 * j:128 * (j + 1)], hT[:, k, c0:c0 + W]) for k in range(8)],
                       k_cb + hk, bk(*bb))
                    sg_t = pgA.alloc()
                    ACT(sg_t.f[:, 0:W], bank(*bb)[:, 0:W], AF.Sigmoid, bk(*bb), sg_t.keys)
                    if kind == "p":
                        TT(UB[:, j, 30:30 + W], bank(*ba)[:, 0:W], sg_t.f[:, 0:W], ALU.mult, bk(*ba) + sg_t.keys, ["ub"])
                    else:
                        TT(ubs[:, j, :, 30:38], bank(*ba)[:, 0:W].rearrange("p (s t) -> p s t", s=16),
                           sg_t.f[:, 0:W].rearrange("p (s t) -> p s t", s=16), ALU.mult, bk(*ba) + sg_t.keys, ["ub"])
                    pgA.free(sg_t)
                if hi == 0 and c0 == 0:
                    dump("u0", UB[:, :, 0:542], [128, 4, 542], ["ub"], BF16)
                for ti in tiles:
                    if ti not in (15, 16):
                        continue
                    lc = lcol(hi, ti)
                    MM(bank(1, 0), [(hT[:, k, lc:lc + 128], w_ca[:, k, :]) for k in range(8)], k_ca + [("hT", lc // 128)], bk(1, 0))
                    MM(bank(1, 1), [(hT[:, k, lc:lc + 128], w_cb[:, k, :]) for k in range(8)], k_cb + [("hT", lc // 128)], bk(1, 1))
                    ut = pgA.alloc()
                    ACT(ut.f, bank(1, 1), AF.Sigmoid, bk(1, 1), ut.keys)
                    TT(ut.f, bank(1, 0), ut.f, ALU.mult, bk(1, 0) + ut.keys, ut.keys)
                    if ti == 15:
                        DMA("sp", o_ncp, ut.f[98:128, :], ut.keys, [], ("uo", 0))
                    else:
                        for s_ in range(16):
                            DMA("sp", o_ncs[s_, 22:30, :], ut.f[8 * s_:8 * s_ + 8, :], ut.keys, [], ("uo", 0))
                    pgA.free(ut)
                is_last = (c0, W, kind, tiles) == half["groups"][-1]
                if is_last:
                    wload("B_q")
                    wload("B_f")
                c32 = []
                for j in range(4):
                    cb_ = [(1, 0), (1, 1), (3, 0), (3, 1)][j]
                    if kind == "p":
                        pairs = [(diag[j][:, k, :], UB[:, j, k:k + W]) for k in range(31)]
                        oap = bank(*cb_)[:, 0:W]
                    else:
                        pairs = [(diag[j][:, k, :], ubs[:, j, :, k:k + 8]) for k in range(31)]
                        oap = bank(*cb_)[:, 0:W].rearrange("p (s t) -> p s t", s=16)
                    MM(oap, pairs, dkeys[j] + ["ub"], bk(*cb_))
                    ct = pgA.alloc()
                    cq = pgA.alloc()
                    ACT(ct.f[:, 0:W], bank(*cb_)[:, 0:W], AF.Identity, bk(*cb_) + ["vecT"], ct.keys, bias=vecT[:, j, 31:32])
                    ACT(cq.f[:, 0:W], bank(*cb_)[:, 0:W], AF.Square, bk(*cb_) + ["vecT"], cq.keys, bias=vecT[:, j, 31:32])
                    MM(bank(2, 0)[:, 0:W], [(ones32[:], ct.f[:, 0:W])], ct.keys + ["ones32"], bk(2, 0), start=(j == 0), stop=(j == 3))
                    MM(bank(2, 1)[:, 0:W], [(ones32[:], cq.f[:, 0:W])], cq.keys + ["ones32"], bk(2, 1), start=(j == 0), stop=(j == 3))
                    pgA.free(cq)
                    c32.append(ct)
                if hi == 0 and c0 == 0:
                    dump("c0", c32[0].f, [128, 512], c32[0].keys)
                preA = gate_pre(pgA, w_g0a, k_g0a, w_g0b, k_g0b, c0, W, 4, [(0, 0), (0, 1), (1, 0), (1, 1)])
                mean = pgA.alloc()
                rstd = pgA.alloc()
                TS(mean.f[:, 0:W], bank(2, 0)[:, 0:W], 1.0 / 512, None, ALU.mult, None, bk(2, 0), mean.keys)
                TT(rstd.f[:, 0:W], mean.f[:, 0:W], mean.f[:, 0:W], ALU.mult, mean.keys, rstd.keys)
                STT(rstd.f[:, 0:W], bank(2, 1)[:, 0:W], 1.0 / 512, rstd.f[:, 0:W], ALU.mult, ALU.subtract,
                    bk(2, 1) + rstd.keys, rstd.keys)
                ACT(rstd.f[:, 0:W], rstd.f[:, 0:W], AF.Sqrt, rstd.keys + ["epst"], rstd.keys, bias=epst[:, 0:1])
                RECIP(rstd.f[:, 0:W], rstd.f[:, 0:W], rstd.keys, rstd.keys)
                preA = preA + gate_pre(pgA, w_g0a, k_g0a, w_g0b, k_g0b, c0, W, 8, [(0, 0), (0, 1), (1, 0), (1, 1)], dm0=4)
                cT = [pgA.alloc(), pgA.alloc()]
                cT_list = []
                for j in range(4):
                    ct = c32[j]
                    cv_ = cT[j // 2].b[:, 512 * (j % 2):512 * (j % 2) + W]
                    ckey = [cT[j // 2].keys[j % 2]]
                    TT(ct.f[:, 0:W], ct.f[:, 0:W], mean.f[:, 0:W], ALU.subtract, ct.keys + mean.keys, ct.keys)
                    TT(ct.f[:, 0:W], ct.f[:, 0:W], rstd.f[:, 0:W], ALU.mult, ct.keys + rstd.keys, ct.keys)
                    ACT(cv_, ct.f[:, 0:W], AF.Silu, ct.keys + ["vecT"], ckey, scale=vecT[:, j, 32:33], bias=vecT[:, j, 33:34])
                    cT_list.append((cv_, ckey))
                    pgA.free(ct)
                pgA.free(mean, rstd)
                if hi == 0 and c0 == 0:
                    dump("cl0", cT[0].b[:, 0:512], [128, 512], cT[0].keys, BF16)
                gate_merge(pgA, w_co, k_co, cT_list, w_g0a, k_g0a, w_g0b, k_g0b, c0, W, True, pre=preA)
                pgA.free(*cT)
            for nm in ("B_i", "B_hgt", "B_ho", "B_g1a", "B_g1b"):
                wload(nm)
            if hi == 0:
                dump("mgA", MG[:, :, 0:1024], [128, 8, 1024], [("mg", c) for c in range(8)])
            if stop_after in ("pA", "pA@%d" % hi):
                stopped = True
                break

            if hi == 0 and "h0A" in os.environ.get("KDBG", ""):
                continue
            mark("h%d_B" % hi)
            w_q, k_q = wget("B_q")
            w_f, k_f = wget("B_f")
            w_i, k_i = wget("B_i")
            w_hgt, k_hgt = wget("B_hgt")
            w_ho, k_ho = wget("B_ho")
            w_g1a, k_g1a = wget("B_g1a")
            w_g1b, k_g1b = wget("B_g1b")
            S0b = [slot(14 + i, 2048).rearrange("p (s v) -> p s v", s=16) for i in range(2)]
            S0b_keys = [akeys(14 + i, 2048) for i in range(2)]
            S0f = [AR[:, 16 * 2048:18 * 2048].bitcast(F32).rearrange("p (s v) -> p s v", s=16),
                   PXT[:, 2:4, :].rearrange("p a b -> p (a b)").rearrange("p (s v) -> p s v", s=16)]
            S0f_keys = [akeys(16, 4096), pxkeys(2) + pxkeys(3)]

            def load_states(b_):
                src = shg[4 * b_:4 * b_ + 4].rearrange("s h d v -> d (s h) v")
                DMA("pool", S0b[b_ % 2], src, [], S0b_keys[b_ % 2], ("s0b", b_ % 2))
                DMA("sp", S0f[b_ % 2], src, [], S0f_keys[b_ % 2], ("s0f", b_ % 2))
            all_t = pg_tiles + px_as_pg
            pgB = FreeList(all_t[0:14])
            hpB = FreeList(halves_of(all_t[14:]))
            oT = [bank(2, 0), bank(2, 1), bank(3, 0), bank(3, 1)]
            oTk = [bk(2, 0), bk(2, 1), bk(3, 0), bk(3, 1)]
            sq_i = [0]
            st_i = [0]

            sq_banks = [(1, 0), (0, 0), (0, 1)]

            def psq_bank():
                b_ = sq_banks[sq_i[0] % 3]
                sq_i[0] += 1
                return b_

            for (c0, W, kind, tiles) in half["groups"]:
                if kind == "s":
                    pgB = FreeList(all_t[0:12])
                    hpB = FreeList(halves_of(pg_tiles[12:17] + px_as_pg[0:4]))
                    load_states(0)
                    load_states(1)
                hk = hkeys(c0, W)
                CH = 32 if kind == "p" else 8
                smask = startp if kind == "p" else starts
                caus = causp if kind == "p" else causs
                kinvT, kendT, qdecT = [], [], []
                fss, qss = [], []
                pb4 = [(0, 0), (0, 1), (1, 0), (1, 1)]
                for h in range(4):
                    hc = slice(128 * h, 128 * (h + 1))
                    b1, b2 = pb4[(2 * h) % 4], pb4[(2 * h + 1) % 4]
                    MM(bank(*b1)[:, 0:W], [(w_f[:, k, hc], hT[:, k, c0:c0 + W]) for k in range(8)], k_f + hk, bk(*b1))
                    fs = pgB.alloc()
                    ACT(fs.f[:, 0:W], bank(*b1)[:, 0:W], AF.Sigmoid, bk(*b1), fs.keys)
                    MM(bank(*b2)[:, 0:W], [(w_q[:, k, hc], hT[:, k, c0:c0 + W]) for k in range(8)], k_q + hk, bk(*b2))
                    qs = pgB.alloc()
                    ACT(qs.f[:, 0:W], bank(*b2)[:, 0:W], AF.Silu, bk(*b2), qs.keys)
                    fss.append(fs)
                    qss.append(qs)
                vbs = []
                for tt in range(len(tiles)):
                    gc = c0 + 128 * tt
                    bv = pb4[tt % 4]
                    MM(bank(*bv), [(hT[:, k, gc:gc + 128], w_i[:, k, :]) for k in range(8)], k_i + [("hT", gc // 128)], bk(*bv))
                    vb = hpB.alloc()
                    ACOPY(vb.b, bank(*bv), bk(*bv), vb.keys)
                    vbs.append(vb)
                for h in range(4):
                    fs, qs = fss[h], qss[h]
                    TS(fs.f[:, 0:W], fs.f[:, 0:W], omlv[:, h:h + 1], lbv[:, h:h + 1], ALU.mult, ALU.add,
                       fs.keys + ["omlv", "lbv"], fs.keys)
                    Pt = pgB.alloc()
                    P.op("dve", (lambda o_, m_, f_: (lambda e: e.tensor_tensor_scan(o_, m_, f_, 1.0, op0=ALU.max, op1=ALU.mult)))(
                        Pt.f[:, 0:W], smask[:, 0:W], fs.f[:, 0:W]), reads=fs.keys + ["startp", "starts"], writes=Pt.keys)
                    Pv = Pt.f[:, 0:W].rearrange("p (c i) -> p c i", i=CH)
                    VCOPY(decs[:, h, :], Pv[:, :, CH - 1], Pt.keys, [("decs", h)])
                    qd = hpB.alloc()
                    TT(qd.b[:, 0:W], qs.f[:, 0:W], Pt.f[:, 0:W], ALU.mult, qs.keys + Pt.keys, qd.keys)
                    rP = pgB.alloc()
                    RECIP(rP.f[:, 0:W], Pt.f[:, 0:W], Pt.keys, rP.keys)
                    TS(fs.f[:, 0:W], fs.f[:, 0:W], -1.0, 1.0, ALU.mult, ALU.add, fs.keys, fs.keys)
                    TT(rP.f[:, 0:W], fs.f[:, 0:W], rP.f[:, 0:W], ALU.mult, fs.keys + rP.keys, rP.keys)
                    ki = hpB.alloc()
                    ACOPY(ki.b[:, 0:W], rP.f[:, 0:W], rP.keys, ki.keys)
                    ke = hpB.alloc()
                    TT(ke.b[:, 0:W].rearrange("p (c i) -> p c i", i=CH), rP.f[:, 0:W].rearrange("p (c i) -> p c i", i=CH),
                       decs[:, h, :].unsqueeze(2).to_broadcast([128, 16, CH]), ALU.mult, rP.keys + [("decs", h)], ke.keys)
                    pgB.free(fs, rP, qs, Pt)
                    kinvT.append(ki)
                    kendT.append(ke)
                    qdecT.append(qd)
                if (c0, W, kind, tiles) == half["groups"][-1]:
                    wload("C_mq")
                    wload("C_mo")
                if hi == 0 and c0 == 0:
                    dump("qd0", qdecT[0].b, [128, 512], qdecT[0].keys, BF16)
                    dump("ki0", kinvT[0].b, [128, 512], kinvT[0].keys, BF16)
                    dump("ke0", kendT[0].b, [128, 512], kendT[0].keys, BF16)
                for tt, ti in enumerate(tiles):
                    tc = 128 * tt
                    gc = c0 + tc
                    tcs = slice(tc, tc + 128)
                    vb = vbs[tt]
                    nq = 1 if kind == "p" else 4
                    ptb = bank(1, 1).bitcast(BF16)
                    TRS([(ptb[:, 128 * h:128 * (h + 1)], kendT[h].b[:, tcs]) for h in range(4)], identb[:],
                        [k_ for h in range(4) for k_ in kendT[h].keys] + ["identb"], bk(1, 1))
                    kps = [(ptb[:, 128 * h:128 * (h + 1)], bk(1, 1)) for h in range(4)]
                    kems = [None] * 4

                    def build_kems(q_):
                        for cc_ in range(4):
                            km = hpB.alloc()
                            crow = crowp[:, cc_:cc_ + 1] if kind == "p" else crows[:, 4 * q_ + cc_:4 * q_ + cc_ + 1]
                            ACT(km.b, ptb[:, 0:512], AF.Copy, bk(1, 1) + ["crowp", "crows"], km.keys, scale=crow)
                            kems[cc_] = km
                    smt = hpB.alloc()
                    sb_ = psq_bank()
                    for h in range(4):
                        MM(bank(*sb_)[:, 128 * h:128 * (h + 1)], [(kinvT[h].b[:, tcs], qdecT[h].b[:, tcs])],
                           kinvT[h].keys + qdecT[h].keys, bk(*sb_))
                    TT(smt.b.rearrange("p (h t) -> p h t", h=4), bank(*sb_).rearrange("p (h t) -> p h t", h=4),
                       caus[:].unsqueeze(1).to_broadcast([128, 4, 128]), ALU.mult, bk(*sb_) + ["causp", "causs"], smt.keys)
                    for h in range(4):
                        MM(oT[h][:, tcs], [(vb.b[:, 128 * h:128 * (h + 1)], smt.b[:, 128 * h:128 * (h + 1)])],
                           vb.keys + smt.keys, oTk[h], start=True, stop=False)
                    nch = 128 // CH
                    if kind == "s":
                        pass
                    for c in range(nch):
                        q_, cc = divmod(c, 4)
                        if cc == 0:
                            if kems[0] is not None:
                                hpB.free(*kems)
                            build_kems(q_)
                        bq, bc = divmod(c, 4)
                        if kind == "s" and bc == 0:
                            if bq >= 1 and bq + 1 < 4:
                                load_states(bq + 1)
                        kvs = []
                        kb_ = psq_bank()
                        for h in range(4):
                            sq_ap, sq_k = bank(*kb_)[:, 128 * h:128 * (h + 1)], bk(*kb_)
                            MM(sq_ap, [(kems[cc].b[:, 128 * h:128 * (h + 1)], vb.b[:, 128 * h:128 * (h + 1)])],
                               kems[cc].keys + vb.keys, sq_k)
                            kvs.append((sq_ap, sq_k))
                            cs = slice(tc + c * CH, tc + (c + 1) * CH)
                            if kind == "p":
                                s_ap, s_k = Sbf[:, h, :], [("Sbf", h)]
                            else:
                                s_ap, s_k = S0b[bq % 2][:, bc * 4 + h, :], S0b_keys[bq % 2]
                            MM(oT[h][:, cs], [(s_ap, qdecT[h].b[:, cs])], s_k + qdecT[h].keys, oTk[h], start=False, stop=(c == nch - 1))
                        ci = tt * nch + c
                        dbc = decs[:, :, ci].unsqueeze(2).to_broadcast([128, 4, 128])
                        dkeys_ = [("decs", h) for h in range(4)]
                        kv4 = bank(*kb_).rearrange("p (h v) -> p h v", h=4)
                        if kind == "p":
                            s32k = [("S32", h) for h in range(4)]
                            sbfk = [("Sbf", h) for h in range(4)]
                            TT(S32[:], S32[:], dbc, ALU.mult, s32k + dkeys_, s32k)
                            TT(Sbf[:], S32[:], kv4, ALU.add, s32k + bk(*kb_), sbfk)
                            TT(S32[:], S32[:], kv4, ALU.add, s32k + bk(*kb_), s32k)
                        else:
                            sv = S0f[bq % 2][:, bc * 4:bc * 4 + 4, :]
                            TT(sv, sv, dbc, ALU.mult, S0f_keys[bq % 2] + dkeys_, S0f_keys[bq % 2])
                            TT(sv, sv, kv4, ALU.add, S0f_keys[bq % 2] + bk(*kb_), S0f_keys[bq % 2])
                        if kind == "s" and bc == 3:
                            DMA("sp", o_nhs[4 * bq:4 * bq + 4].rearrange("s h d v -> d (s h) v"), S0f[bq % 2], S0f_keys[bq % 2], [],
                                ("so", bq % 2))
                    hpB.free(*kems)
                    hpB.free(smt)
                for h in range(4):
                    hpB.free(kinvT[h], kendT[h])
                hpB.free(*vbs)
                if hi == 1 and c0 == 512 and "skipnhp" not in os.environ.get("KDBG", ""):
                    DMA("sp", o_nhp.rearrange("h d v -> d h v"), S32[:], [("S32", h) for h in range(4)], [], ("so", 1))
                og = []
                rsts = []
                for h in range(4):
                    osq = pgB.alloc()
                    ACT(osq.f[:, 0:W], oT[h][:, 0:W], AF.Square, oTk[h], osq.keys)
                    bs_ = pb4[h]
                    MM(bank(*bs_)[:, 0:W], [(ones32[:], osq.f[:, 0:W])], osq.keys + ["ones32"], bk(*bs_))
                    TS(osq.f[:, 0:W], bank(*bs_)[:, 0:W], 1.0 / 128, EPS, ALU.mult, ALU.add, bk(*bs_), osq.keys)
                    rsts.append(osq)
                gts = []
                for h in range(4):
                    hc = slice(128 * h, 128 * (h + 1))
                    bg_ = pb4[h]
                    MM(bank(*bg_)[:, 0:W], [(w_hgt[:, k, hc], hT[:, k, c0:c0 + W]) for k in range(8)], k_hgt + hk, bk(*bg_))
                    gt = pgB.alloc()
                    ACT(gt.f[:, 0:W], bank(*bg_)[:, 0:W], AF.Silu, bk(*bg_), gt.keys)
                    gts.append(gt)
                for h in range(4):
                    pow_mhalf(rsts[h].f[:, 0:W], rsts[h].keys, W)
                preB = gate_pre(pgB, w_g1a, k_g1a, w_g1b, k_g1b, c0, W, 6 if kind == "p" else 4, pb4)
                for h in range(4):
                    rst, gt = rsts[h], gts[h]
                    STT(rst.f[:, 0:W], oT[h][:, 0:W], hgnv[:, 0:1], rst.f[:, 0:W], ALU.mult, ALU.mult,
                        oTk[h] + ["hgnv"] + rst.keys, rst.keys)
                    ogh = hpB.alloc()
                    TT(ogh.b[:, 0:W], rst.f[:, 0:W], gt.f[:, 0:W], ALU.mult, rst.keys + gt.keys, ogh.keys)
                    pgB.free(rst, gt)
                    og.append(ogh)
                    hpB.free(qdecT[h])
                if hi == 0 and c0 == 0:
                    dump("og0", og[0].b, [128, 512], og[0].keys, BF16)
                gate_merge(pgB, w_ho, k_ho, [(o_.b[:, 0:W], o_.keys) for o_ in og], w_g1a, k_g1a, w_g1b, k_g1b, c0, W, False, pre=preB)
                for o_ in og:
                    hpB.free(o_)
            wload("C_g2a")
            wload("C_g2b")
            if hi == 0:
                dump("mgB", MG[:, :, 0:1024], [128, 8, 1024], [("mg", c) for c in range(8)])
            if stop_after in ("pB", "pB@%d" % hi):
                stopped = True
                break

            mark("h%d_C" % hi)
            w_mq, k_mq = wget("C_mq")
            w_mo, k_mo = wget("C_mo")
            w_g2a, k_g2a = wget("C_g2a")
            w_g2b, k_g2b = wget("C_g2b")
            Ks = wview(8, 8, 512)
            Ks_keys = akeys(8, 4096)
            Vs = [wview(10 + 2 * i, 8, 512) for i in range(2)]
            Vs_keys = [akeys(10 + 2 * i, 4096) for i in range(2)]
            KTs = [wview(14 + 2 * i, 16, 256) for i in range(2)]
            KTs_keys = [akeys(14 + 2 * i, 4096) for i in range(2)]
            def c_load_k(qq):
                DMA("pool", Ks, ck[4 * qq:4 * qq + 4].rearrange("s (mt p) e -> p (s mt) e", p=128), [], Ks_keys, ("a", 8))

            def c_load_v(qq):
                DMA("pool", Vs[qq % 2], cv[4 * qq:4 * qq + 4].rearrange("s (mt p) e -> p (s mt) e", p=128), [], Vs_keys[qq % 2],
                    ("a", 10 + 2 * (qq % 2)))

            if hi == 1:
                c_load_k(0)
                c_load_v(0)
                c_load_v(1)
            pgC = FreeList(all_t[0:13])
            hpC = FreeList(halves_of(all_t[13:]))
            SCALE = 128 ** -0.5
            ss_i = [0]

            ss_banks = [(1, 0), (1, 1), (0, 0), (0, 1)]

            def pss():
                b_ = ss_banks[ss_i[0] % 4]
                ss_i[0] += 1
                return bank(*b_)[:, 0:256], bk(*b_)

            def softmax_rows(sc_ap, sc_k, np_, pool_f, pool_h):
                mx, mx_k = stat_col()
                P.op("dve", (lambda o_, i_: (lambda e: e.tensor_reduce(out=o_, in_=i_, axis=AX.X, op=ALU.max)))(
                    mx[0:np_, :], sc_ap), reads=sc_k, writes=mx_k)
                TS(mx[0:np_, :], mx[0:np_, :], -SCALE, None, ALU.mult, None, mx_k, mx_k)
                pe_ = pool_f.alloc()
                rs, rs_k = stat_col()
                ACT(pe_.f[0:np_, 0:256], sc_ap, AF.Exp, sc_k + mx_k, pe_.keys + rs_k, bias=mx[0:np_, :], scale=SCALE,
                    accum=rs[0:np_, :])
                RECIP(rs[0:np_, :], rs[0:np_, :], rs_k, rs_k)
                pn = pool_h.alloc()
                TS(pn.b[0:np_, 0:256], pe_.f[0:np_, 0:256], rs[0:np_, :], None, ALU.mult, None, pe_.keys + rs_k, pn.keys)
                pool_f.free(pe_)
                return pn

            for (c0, W, kind, tiles) in half["groups"]:
                hk = hkeys(c0, W)
                qT = []
                for h in range(4):
                    hc = slice(128 * h, 128 * (h + 1))
                    MM(bank(0, h % 2)[:, 0:W], [(w_mq[:, k, hc], hT[:, k, c0:c0 + W]) for k in range(8)], k_mq + hk, bk(0, h % 2))
                    qh = hpC.alloc()
                    ACOPY(qh.b[:, 0:W], bank(0, h % 2)[:, 0:W], bk(0, h % 2), qh.keys)
                    qT.append(qh)
                if (c0, W, kind, tiles) == half["groups"][-1]:
                    wload("D_o1")
                preC = gate_pre(pgC, w_g2a, k_g2a, w_g2b, k_g2b, c0, W, 8, [(2, 0), (2, 1), (3, 0), (3, 1)])
                om = []
                if kind == "p":
                    om = [hpC.alloc() for _ in range(4)]
                    sq_regs = [1, 3]
                    ntl = len(tiles)

                    def c_scores(tt_):
                        qi_ = sq_regs[tt_ % 2]
                        tcs_ = slice(128 * tt_, 128 * (tt_ + 1))
                        for h in range(4):
                            MM(Q[qi_][:, 256 * h:256 * (h + 1)], [(qT[h].b[:, tcs_], KTp[:, h, :])], qT[h].keys + [("KTp", h)],
                               bk(qi_, h // 2))

                    def c_front(tt):
                        qi = sq_regs[tt % 2]
                        sc4 = Q[qi][:, :].rearrange("p (h m) -> p h m", h=4)
                        sck = bk(qi, 0) + bk(qi, 1)
                        mx, mx_k = stat_col(4)
                        P.op("dve", (lambda o_, i_: (lambda e: e.tensor_reduce(out=o_, in_=i_, axis=AX.X, op=ALU.max)))(mx, sc4),
                             reads=sck, writes=mx_k)
                        TS(mx, mx, -SCALE, None, ALU.mult, None, mx_k, mx_k)
                        rs, rs_k = stat_col(4)
                        phs = []
                        for h in range(4):
                            ph = hpC.alloc()
                            ACT(ph.f, sc4[:, h, :], AF.Exp, sck + mx_k, ph.keys + rs_k, bias=mx[:, h:h + 1], scale=SCALE,
                                accum=rs[:, h:h + 1])
                            phs.append(ph)
                        return phs, rs, rs_k

                    def c_back(tt, st_):
                        phs, rs, rs_k = st_
                        RECIP(rs, rs, rs_k, rs_k)
                        pn = pgC.alloc()
                        for h in range(4):
                            TS(pn.b[:, 256 * h:256 * (h + 1)], phs[h].f, rs[:, h:h + 1], None, ALU.mult, None,
                               phs[h].keys + rs_k, pn.keys)
                            hpC.free(phs[h])
                        ptb = bank(0, tt % 2).bitcast(BF16)
                        TRS([(ptb[:, 128 * i:128 * (i + 1)], pn.b[:, 128 * i:128 * (i + 1)]) for i in range(8)], identb[:],
                            pn.keys + ["identb"], bk(0, tt % 2))
                        pT = pgC.alloc()
                        ACOPY(pT.b, ptb, bk(0, tt % 2), pT.keys)
                        sg_, tl = divmod(tt, 2)
                        for h in range(4):
                            oc = 256 * (h % 2) + 128 * tl
                            MM(bank(2, h // 2)[:, oc:oc + 128],
                               [(Vbp[:, mt, 128 * h:128 * (h + 1)], pT.b[:, 128 * (2 * h + mt):128 * (2 * h + mt + 1)]) for mt in range(2)],
                               [("Vbp", 0), ("Vbp", 1)] + pT.keys, bk(2, h // 2))
                        pgC.free(pn, pT)
                        if tl == 1 or tt == ntl - 1:
                            for h in range(4):
                                ACOPY(om[h].b[:, 256 * sg_:256 * sg_ + 128 * (tl + 1)],
                                      bank(2, h // 2)[:, 256 * (h % 2):256 * (h % 2) + 128 * (tl + 1)], bk(2, h // 2), om[h].keys)

                    c_scores(0)
                    prev_ = None
                    for tt in range(ntl):
                        if tt + 1 < ntl:
                            c_scores(tt + 1)
                        cur_ = c_front(tt)
                        if prev_ is not None:
                            c_back(tt - 1, prev_)
                        prev_ = cur_
                    c_back(ntl - 1, prev_)
                else:
                    ob, obk = bank(2, 0), bk(2, 0)

                    def c_kt(qq):
                        bb = qq % 2
                        for s_ in range(4):
                            pt_ap = bank(3, s_ % 2).bitcast(BF16)
                            pt_k = bk(3, s_ % 2)
                            TRS([(pt_ap[:, 256 * h + 128 * mt:256 * h + 128 * (mt + 1)], Ks[:, s_ * 2 + mt, 128 * h:128 * (h + 1)])
                                 for h in range(4) for mt in range(2)], identb[:], Ks_keys + ["identb"], pt_k)
                            ACOPY(KTs[bb][:, 4 * s_:4 * s_ + 4, :], pt_ap.rearrange("p (h m) -> p h m", h=4), pt_k, KTs_keys[bb])

                    c_kt(0)
                    c_load_k(1)
                    for q_ in range(4):
                        bi = q_ % 2
                        sc4 = Q[1][0:32, :].rearrange("p (h m) -> p h m", h=4)
                        sck = bk(1, 0) + bk(1, 1)
                        for h in range(4):
                            qm = hpC.alloc()
                            TT(qm.b[:, 0:128].rearrange("p (s l) -> p s l", s=4),
                               qT[h].b[:, 32 * q_:32 * q_ + 32].unsqueeze(1).to_broadcast([128, 4, 32]),
                               seqm[:].rearrange("p (s l) -> p s l", s=4), ALU.mult, qT[h].keys + ["seqm"], qm.keys)
                            MM(Q[1][0:32, 256 * h:256 * (h + 1)],
                               [(qm.b[:, 32 * s_:32 * s_ + 32], KTs[bi][:, s_ * 4 + h, :]) for s_ in range(4)],
                               qm.keys + KTs_keys[bi], bk(1, h // 2))
                            hpC.free(qm)
                        mx, mx_k = stat_col(4)
                        P.op("dve", (lambda o_, i_: (lambda e: e.tensor_reduce(out=o_, in_=i_, axis=AX.X, op=ALU.max)))(mx[0:32, :], sc4),
                             reads=sck, writes=mx_k)
                        TS(mx[0:32, :], mx[0:32, :], -SCALE, None, ALU.mult, None, mx_k, mx_k)
                        rs, rs_k = stat_col(4)
                        phs = []
                        for h in range(4):
                            ph = hpC.alloc()
                            ACT(ph.f[0:32, :], sc4[:, h, :], AF.Exp, sck + mx_k, ph.keys + rs_k, bias=mx[0:32, h:h + 1], scale=SCALE,
                                accum=rs[0:32, h:h + 1])
                            phs.append(ph)
                        if q_ + 1 < 4:
                            c_kt(q_ + 1)
                            if q_ + 2 < 4:
                                c_load_k(q_ + 2)
                        RECIP(rs[0:32, :], rs[0:32, :], rs_k, rs_k)
                        pn = pgC.alloc()
                        for h in range(4):
                            TS(pn.b[0:32, 256 * h:256 * (h + 1)], phs[h].f[0:32, :], rs[0:32, h:h + 1], None, ALU.mult, None,
                               phs[h].keys + rs_k, pn.keys)
                            hpC.free(phs[h])
                        ptb = bank(0, q_ % 2).bitcast(BF16)
                        TRS([(ptb[:, 32 * i:32 * (i + 1)], pn.b[0:32, 128 * i:128 * (i + 1)]) for i in range(8)], identb[0:32, 0:32],
                            pn.keys + ["identb"], bk(0, q_ % 2))
                        pT = hpC.alloc()
                        ACOPY(pT.b[:, 0:256], ptb[:, 0:256], bk(0, q_ % 2), pT.keys)
                        for h in range(4):
                            for s_ in range(4):
                                col = 128 * h + 32 * q_ + 8 * s_
                                MM(ob[:, col:col + 8],
                                   [(Vs[bi][:, s_ * 2 + mt, 128 * h:128 * (h + 1)],
                                     pT.b[:, 32 * (2 * h + mt) + 8 * s_:32 * (2 * h + mt) + 8 * s_ + 8]) for mt in range(2)],
                                   Vs_keys[bi] + pT.keys, obk)
                        pgC.free(pn)
                        hpC.free(pT)
                        if q_ + 2 < 4:
                            c_load_v(q_ + 2)
                    for h in range(4):
                        omh = hpC.alloc()
                        ACOPY(omh.b[:, 0:W], ob[:, 128 * h:128 * (h + 1)], obk, omh.keys)
                        om.append(omh)
                for h in range(4):
                    hpC.free(qT[h])
                if hi == 0 and c0 == 0:
                    dump("om0", om[0].b, [128, 512], om[0].keys, BF16)
                gate_merge(pgC, w_mo, k_mo, [(o_.b[:, 0:W], o_.keys) for o_ in om], w_g2a, k_g2a, w_g2b, k_g2b, c0, W, False, pre=preC)
                for o_ in om:
                    hpC.free(o_)
            wload("D_o2")
            if hi == 0:
                dump("mgC", MG[:, :, 0:1024], [128, 8, 1024], [("mg", c) for c in range(8)])
            if stop_after in ("pC", "pC@%d" % hi):
                stopped = True
                break

            mark("h%d_D" % hi)
            w_o1, k_o1 = wget("D_o1")
            w_o2, k_o2 = wget("D_o2")
            if hi == 0:
                for nm in ("A_co", "A_g0a", "A_g0b"):
                    wload(nm)
                build_diag()
            else:
                for nm in ("E_g0", "E_u0", "E_g1", "E_u1", "E_g2", "E_u2", "E_g3", "E_u3", "E_g4", "E_g5"):
                    wload(nm)
            load_gain(gA[:], "gA", 1)
            pgD = FreeList(pg_tiles[0:9] + pg_tiles[11:15])
            dtiles = half["tiles"]
            if hi == 0:
                load_gain(gB, gB_keys, 0)
                ntiles = halves[1]["tiles"]

            qring = [0, 1] if hi == 0 else [0, 1, 3]
            xring = [0, 1] if hi == 0 else [0, 1, 2]
            ddepth = len(qring) - 1

            def d_front(n_):
                ti = dtiles[n_]
                lc = lcol(hi, ti)
                mgb = pgD.alloc()
                mv_ = mgb.b.rearrange("p (k t) -> p k t", k=8)
                if n_ % 2 == 0:
                    ACOPY(mv_, MG[:, :, lc:lc + 128], [("mg", lc // 128)], mgb.keys)
                else:
                    VCOPY(mv_, MG[:, :, lc:lc + 128], [("mg", lc // 128)], mgb.keys)
                qa = qring[n_ % len(qring)]
                MM(bank(qa, 0), [(mv_[:, k, :], w_o1[:, k, :]) for k in range(8)], mgb.keys + k_o1, bk(qa, 0))
                MM(bank(qa, 1), [(mv_[:, k, :], w_o2[:, k, :]) for k in range(8)], mgb.keys + k_o2, bk(qa, 1))
                pgD.free(mgb)
                xb = xring[n_ % len(xring)]
                DMA("sp", PXT[:, xb, :], x_all[128 * ti:128 * (ti + 1), :], [], pxkeys(xb), ("x", xb))

            def d_back(n_):
                ti = dtiles[n_]
                qa = qring[n_ % len(qring)]
                xb = xring[n_ % len(xring)]
                qkeys = bk(qa, 0) + bk(qa, 1)
                ss_ap, ss_keys = sumsq_1024(Q[qa][:, :], qkeys, pgD, junk_ps=2)
                r_ap, r_keys = rstd_from_ss(ss_ap, ss_keys, 1024, EPS)
                STT(tmpE, Q[qa][:, :], r_ap, gA[:], ALU.mult, ALU.mult, qkeys + r_keys + ["gA"], tmpE_keys)
                TT(PXT[:, xb, :], PXT[:, xb, :], tmpE, ALU.add, pxkeys(xb) + tmpE_keys, pxkeys(xb))
                DMA("sp", x1s[128 * ti:128 * (ti + 1), :], PXT[:, xb, :], pxkeys(xb), [("x1s", ti)], ("xs", xb))

            p0_state = {}

            def p0_front(n_):
                if hi != 0 or n_ >= len(ntiles):
                    return
                ti = ntiles[n_]
                p0_state[n_] = norm_front(x_all[128 * ti:128 * (ti + 1), :], 2 + n_ % 2, gB, gB_keys, pgD, junk_ps=2)

            def p0_back(n_):
                if n_ not in p0_state:
                    return
                ti = ntiles[n_]
                lc = lcol(1, ti)
                norm_back(p0_state.pop(n_), hT, lc, [("hT", lc // 128)], pgD, (3, 1))

            for n_ in range(min(ddepth, len(dtiles))):
                d_front(n_)
            p0_front(0)
            for n_ in range(len(dtiles)):
                if n_ + ddepth < len(dtiles):
                    d_front(n_ + ddepth)
                p0_front(n_ + 1)
                p0_back(n_)
                d_back(n_)
            if hi == 0:
                for n_ in range(len(dtiles), len(ntiles)):
                    p0_front(n_ + 1)
                    p0_back(n_)
            if hi == 0:
                wload("A_ca")
                wload("A_cb")
            else:
                for nm in ["E_u4", "E_u5"] + ["E_d%d" % i for i in range(11)]:
                    wload(nm)
            if stop_after in ("pD", "pD@%d" % hi):
                stopped = True
                break

        if not stopped:
            mark("E")
            load_gain(gA[:], "gA", 2)
            load_gain(gB, gB_keys, 3)
            w_gp, k_gp, w_upp, k_upp, w_dn, k_dn = [], [], [], [], [], []
            for i in range(6):
                v_, k_ = wget("E_g%d" % i)
                w_gp.append(v_)
                k_gp.append(k_)
                v_, k_ = wget("E_u%d" % i)
                w_upp.append(v_)
                k_upp.append(k_)
            for i in range(11):
                v_, k_ = wget("E_d%d" % i)
                w_dn.append(v_)
                k_dn.append(k_)
            h2T = [PGT[:, 11 + 2 * b_:13 + 2 * b_, :].rearrange("p a b -> p (a b)").bitcast(BF16).rearrange("p (k t) -> p k t", k=8)
                   for b_ in range(2)]
            h2T_keys = [pg_tiles[11 + 2 * b_].keys + pg_tiles[12 + 2 * b_].keys for b_ in range(2)]
            pgE = FreeList(pg_tiles[0:3])
            hpE = FreeList(halves_of(pg_tiles[3:9]) + halves_of(px_as_pg[0:0]))
            egroups = [[2 * g, 2 * g + 1] for g in range(8)] + [[16]]
            gu_i = [0]

            def pgu():
                b_ = [(0, 0), (0, 1), (1, 1)][gu_i[0] % 3]
                gu_i[0] += 1
                return bank(*b_), bk(*b_)

            e_pend = {}

            def e_load(gi_):
                for tt_, ti_ in enumerate(egroups[gi_]):
                    pxb_ = (2 * gi_ + tt_) % 4
                    DMA("sp", PXT[:, pxb_, :], x1s[128 * ti_:128 * (ti_ + 1), :], [("x1s", ti_)], pxkeys(pxb_), ("x", pxb_))

            def e_norm_front(gi_):
                e_pend[gi_] = [norm_front(x1s[128 * ti_:128 * (ti_ + 1), :], (2 * gi_ + tt_) % 4, gA[:], ["gA"], pgE,
                                          src_keys=[("x1s", ti_)], dve_sumsq=False, preloaded=True)
                               for tt_, ti_ in enumerate(egroups[gi_])]

            def e_norm_back(gi_):
                for tt_, jt2 in enumerate(e_pend.pop(gi_)):
                    norm_back(jt2, h2T[gi_ % 2], 128 * tt_, h2T_keys[gi_ % 2], pgE, (1, 0))

            e_load(0)
            e_norm_front(0)
            e_norm_back(0)
            pending = []

            def make_post(gi_, tt_, ti_):
                def post():
                    pxb = (2 * gi_ + tt_) % 4
                    qd_ = 2 + tt_
                    qkeys = bk(qd_, 0) + bk(qd_, 1)
                    ss_ap, ss_keys = sumsq_1024(Q[qd_][:, :], qkeys, pgE)
                    r_ap, r_keys = rstd_from_ss(ss_ap, ss_keys, 1024, EPS)
                    STT(tmpE, Q[qd_][:, :], r_ap, gB, ALU.mult, ALU.mult, qkeys + r_keys + gB_keys, tmpE_keys)
                    TT(PXT[:, pxb, :], PXT[:, pxb, :], tmpE, ALU.add, pxkeys(pxb) + tmpE_keys, pxkeys(pxb))
                    DMA("sp", y[128 * ti_:128 * (ti_ + 1), :], PXT[:, pxb, :], pxkeys(pxb), [], ("xs", pxb))
                return post

            for gi, tl in enumerate(egroups):
                Wg = 128 * len(tl)
                hb_i = gi % 2
                act = []
                for j in range(22):
                    if j in (1, 3) and pending:
                        pending.pop(0)()
                    if j == 4 and gi + 1 < len(egroups):
                        e_load(gi + 1)
                    if j == 10 and gi + 1 < len(egroups):
                        e_norm_front(gi + 1)
                    if j == 17 and gi + 1 < len(egroups):
                        e_norm_back(gi + 1)
                    gu_ap, g_k = pgu()
                    g_ap, u_ap, u_k = gu_ap[:, 0:256], gu_ap[:, 256:512], g_k
                    pi, po = j // 4, 128 * (j % 4)
                    MM(g_ap[:, 0:Wg], [(w_gp[pi][:, k, po:po + 128], h2T[hb_i][:, k, 0:Wg]) for k in range(8)],
                       k_gp[pi] + h2T_keys[hb_i], g_k)
                    MM(u_ap[:, 0:Wg], [(w_upp[pi][:, k, po:po + 128], h2T[hb_i][:, k, 0:Wg]) for k in range(8)],
                       k_upp[pi] + h2T_keys[hb_i], u_k)
                    sg_h = hpE.alloc()
                    ACT(sg_h.f[:, 0:Wg], g_ap[:, 0:Wg], AF.Silu, g_k, sg_h.keys)
                    if j % 2 == 0:
                        ah = hpE.alloc()
                        act.append(ah)
                    ah = act[j // 2]
                    TT(ah.b[:, 256 * (j % 2):256 * (j % 2) + Wg], sg_h.f[:, 0:Wg], u_ap[:, 0:Wg], ALU.mult, sg_h.keys + u_k, ah.keys)
                    hpE.free(sg_h)
                while pending:
                    pending.pop(0)()
                akeys_all = []
                for a_ in act:
                    akeys_all += a_.keys
                for tt, ti in enumerate(tl):
                    qd_ = 2 + tt
                    for hf in range(2):
                        MM(bank(qd_, hf),
                           [(act[j // 2].b[:, 256 * (j % 2) + 128 * tt:256 * (j % 2) + 128 * tt + 128],
                             w_dn[j // 2][:, j % 2, 512 * hf:512 * (hf + 1)]) for j in range(22)],
                           akeys_all + [k for kk in k_dn for k in kk], bk(qd_, hf))
                    pending.append(make_post(gi, tt, ti))
                for a_ in act:
                    hpE.free(a_)
            while pending:
                pending.pop(0)()

        P.emit(st)
        mark('end')
        info = dict(n_ops=len(P.ops), n_waits=P.n_waits, n_dma_sems=P.n_dma_sems, marks=marks)
    return nc, dbg_outs, info


def _consts():
    bf = ml_dtypes.bfloat16
    c = {}
    c["c_identb"] = np.eye(128, dtype=np.float32).astype(bf)
    c["c_identf"] = np.eye(128, dtype=np.float32)
    sp = np.zeros((128, 512), np.float32); sp[:, ::32] = 1.0
    ss = np.zeros((128, 128), np.float32); ss[:, ::8] = 1.0
    c["c_startp"], c["c_starts"] = sp, ss
    s = np.arange(128)[:, None]; t = np.arange(128)[None, :]
    c["c_causp"] = ((s // 32 == t // 32) & (s <= t)).astype(np.float32)
    c["c_causs"] = ((s // 8 == t // 8) & (s <= t)).astype(np.float32)
    c["c_crowp"] = (s // 32 == np.arange(4)[None, :]).astype(np.float32)
    c["c_crows"] = (s // 8 == np.arange(16)[None, :]).astype(np.float32)
    sm = (np.arange(4)[:, None] == (np.arange(32)[None, :] // 8)).astype(np.float32)
    c["c_seqm"] = np.ascontiguousarray(np.broadcast_to(sm.reshape(1, 128), (128, 128))).astype(bf)
    return c


def make_in_maps(inp):
    f = np.float32
    shared = {
        "w_in": np.ascontiguousarray(inp["w_in"][0], f),
        "w_conv_out": np.ascontiguousarray(inp["w_conv_out"][0], f),
        "w_hg_out": np.ascontiguousarray(inp["w_hg_out"][0], f),
        "w_mem_out": np.ascontiguousarray(inp["w_mem_out"][0], f),
        "w_mem_kv": np.ascontiguousarray(inp["w_mem_kv"][0], f),
        "w_out": np.ascontiguousarray(inp["w_out"][0], f),
        "w_gate": np.ascontiguousarray(inp["w_ffn_gate"][0], f),
        "w_up": np.ascontiguousarray(inp["w_ffn_up"][0], f),
        "w_down": np.ascontiguousarray(inp["w_ffn_down"][0], f),
        "vec36": np.ascontiguousarray(np.concatenate(
            [inp["conv_w"][0], inp["conv_b"], inp["conv_ln_g"], inp["conv_ln_b"], inp["hg_lb_logits"]], axis=0), f),
        "hgn": np.ascontiguousarray(inp["hg_norm_g"][0].reshape(128, 1), f),
        "gains": np.ascontiguousarray(np.concatenate(
            [inp["norm_pre_mix"], inp["norm_post_mix"], inp["norm_pre_ffn"], inp["norm_post_ffn"], inp["mem_norm_g"]], axis=0), f),
    }
    shared.update(_consts())
    maps = []
    for b in range(NCORES):
        m = dict(shared)
        m["x_all"] = np.ascontiguousarray(np.concatenate(
            [inp["x_prompt"][b], inp["x_sample"][16 * b:16 * b + 16].reshape(128, 1024)], axis=0), f)
        m["mem"] = np.ascontiguousarray(inp["mem_prompt"][b], f)
        m["sconv"] = np.ascontiguousarray(inp["state_conv"][0, 16 * b:16 * b + 16], f)
        m["shg"] = np.ascontiguousarray(inp["state_hgrn"][0, 16 * b:16 * b + 16], f)
        m["ck"] = np.ascontiguousarray(inp["cache_mem_k"][0, 16 * b:16 * b + 16].reshape(16, 256, 512), f)
        m["cv"] = np.ascontiguousarray(inp["cache_mem_v"][0, 16 * b:16 * b + 16].reshape(16, 256, 512), f)
        maps.append(m)
    return maps


_CACHE = {}


def kernel(**inp):
    if "nc" not in _CACHE:
        _CACHE["nc"] = build_program()[0]
    nc = _CACHE["nc"]
    maps = make_in_maps(inp)
    res = run_bass_kernel_spmd(nc, maps, core_ids=list(range(NCORES)))
    R = res.results
    yp = np.stack([R[b]["y"][:2048] for b in range(NCORES)], 0)
    ys = np.concatenate([R[b]["y"][2048:].reshape(16, 8, 1024) for b in range(NCORES)], 0)
    ncp = np.stack([R[b]["o_ncp"] for b in range(NCORES)], 0)[None]
    nhp = np.stack([R[b]["o_nhp"] for b in range(NCORES)], 0)[None]
    mk = np.stack([R[b]["o_mk"].reshape(256, 4, 128) for b in range(NCORES)], 0)[None]
    mv = np.stack([R[b]["o_mv"].reshape(256, 4, 128) for b in range(NCORES)], 0)[None]
    ncs = np.concatenate([R[b]["o_ncs"] for b in range(NCORES)], 0)[None]
    nhs = np.concatenate([R[b]["o_nhs"] for b in range(NCORES)], 0)[None]
    return tuple(np.ascontiguousarray(a, np.float32) for a in (yp, ys, ncp, nhp, mk, mv, ncs, nhs))
```

```python
import os
import math
from collections import deque
from contextlib import ExitStack

import numpy as np
import ml_dtypes

import concourse.bass as bass
import concourse.mybir as mybir
from concourse.bass_utils import run_bass_kernel_spmd

F32 = mybir.dt.float32
BF16 = mybir.dt.bfloat16
AF = mybir.ActivationFunctionType
ALU = mybir.AluOpType
AX = mybir.AxisListType

ENGS = ("pe", "act", "dve", "pool", "sp")
EPS = 1e-6
NCORES = 8
T_ALL = 2176
D_IN = 6656


class Op:
    __slots__ = ("eng", "fn", "deps", "is_dma", "sem_key", "needs_inc", "inc_val", "idx", "clock")

    def __init__(self, eng, fn, is_dma, sem_key):
        self.eng = eng
        self.fn = fn
        self.deps = []
        self.is_dma = is_dma
        self.sem_key = sem_key
        self.needs_inc = False
        self.inc_val = None
        self.clock = None


class Prog:
    def __init__(self, nc):
        self.nc = nc
        self.ops = []
        self.last_write = {}
        self.readers = {}

    def op(self, eng, fn, reads=(), writes=(), dma_key=None):
        o = Op(eng, fn, dma_key is not None, (eng, dma_key) if dma_key is not None else None)
        o.idx = len(self.ops)
        deps = {}
        for r in reads:
            w = self.last_write.get(r)
            if w is not None:
                deps[w.idx] = w
        for r in writes:
            w = self.last_write.get(r)
            if w is not None:
                deps[w.idx] = w
            for rd in self.readers.get(r, ()):
                deps[rd.idx] = rd
        o.deps = list(deps.values())
        for r in reads:
            self.readers.setdefault(r, []).append(o)
        for r in writes:
            self.last_write[r] = o
            self.readers[r] = []
        self.ops.append(o)
        return o

    def emit(self, stack):
        nc = self.nc
        for o in self.ops:
            o.deps = [d for d in o.deps if not (d.eng == "pe" and o.eng == "pe" and not d.is_dma and not o.is_dma)]
            for d in o.deps:
                d.needs_inc = True
        sems = {e: stack.enter_context(nc.semaphore("s_" + e)) for e in ENGS}
        dma_sems = {}
        counts = {e: 0 for e in ENGS}
        dma_counts = {}
        for o in self.ops:
            if o.is_dma:
                if o.sem_key not in dma_sems:
                    dma_sems[o.sem_key] = stack.enter_context(nc.semaphore("d%d" % len(dma_sems)))
                    dma_counts[o.sem_key] = 0
                dma_counts[o.sem_key] += 16
                o.inc_val = (dma_sems[o.sem_key], dma_counts[o.sem_key])
            elif o.needs_inc:
                counts[o.eng] += 1
                o.inc_val = (sems[o.eng], counts[o.eng])
        self.n_dma_sems = len(dma_sems)
        eng_clock = {e: {} for e in ENGS}
        waits = {}
        for o in self.ops:
            ck = eng_clock[o.eng]
            wm = {}
            for d in sorted(o.deps, key=lambda d: d.idx):
                sem, val = d.inc_val
                if ck.get(sem, 0) >= val:
                    continue
                if wm.get(sem, 0) < val:
                    wm[sem] = val
                for s, v in d.clock.items():
                    if ck.get(s, 0) < v:
                        ck[s] = v
            waits[o.idx] = list(wm.items())
            oc = dict(ck)
            if o.inc_val is not None:
                s, v = o.inc_val
                if oc.get(s, 0) < v:
                    oc[s] = v
            o.clock = oc
        final = [(dma_sems[k], dma_counts[k]) for k in dma_sems]
        final += [(sems[e], counts[e]) for e in ENGS if counts[e] > 0]
        self.n_waits = sum(len(w) for w in waits.values())
        by_eng = {e: [o for o in self.ops if o.eng == e] for e in ENGS}
        with nc.Block() as block:
            def mk(ename):
                def body(eng):
                    for o in by_eng[ename]:
                        for s, v in waits[o.idx]:
                            eng.wait_ge(s, v)
                        ins = o.fn(eng)
                        if o.inc_val is not None:
                            ins.then_inc(o.inc_val[0], 16 if o.is_dma else 1)
                    if ename == "sp":
                        for s, v in final:
                            eng.wait_ge(s, v)
                return body
            block.tensor(mk("pe"))
            block.scalar(mk("act"))
            block.vector(mk("dve"))
            block.gpsimd(mk("pool"))
            block.sync(mk("sp"))


class Tile:
    def __init__(self, ap_f32, keys):
        self.f = ap_f32
        self.b = ap_f32.bitcast(BF16)
        self.keys = list(keys)

    def half(self, i):
        return Half(self.f[:, 256 * i:256 * (i + 1)], [self.keys[i]])


class Half:
    def __init__(self, ap_f32, keys):
        self.f = ap_f32
        self.b = ap_f32.bitcast(BF16)
        self.keys = list(keys)


class FreeList:
    def __init__(self, items):
        self.q = deque(items)

    def alloc(self):
        if not self.q:
            raise RuntimeError("scratch pool exhausted")
        return self.q.popleft()

    def free(self, *ts):
        for t in ts:
            self.q.append(t)


def build_program(debug=(), stop_after=None):
    nc = bass.Bass("TRN2", target_bir_lowering=False)

    def din(name, shape, dt=F32):
        return nc.dram_tensor(name, list(shape), dt, kind="ExternalInput").ap()

    def dout(name, shape, dt=F32):
        return nc.dram_tensor(name, list(shape), dt, kind="ExternalOutput").ap()

    x_all = din("x_all", [T_ALL, 1024])
    mem = din("mem", [256, 1024])
    sconv = din("sconv", [16, 30, 512])
    shg = din("shg", [16, 4, 128, 128])
    ck = din("ck", [16, 256, 512])
    cv = din("cv", [16, 256, 512])
    w_in = din("w_in", [1024, D_IN])
    w_conv_out = din("w_conv_out", [512, 1024])
    w_hg_out = din("w_hg_out", [512, 1024])
    w_mem_out = din("w_mem_out", [512, 1024])
    w_mem_kv = din("w_mem_kv", [1024, 1024])
    w_out = din("w_out", [1024, 1024])
    w_gate = din("w_gate", [1024, 2816])
    w_up = din("w_up", [1024, 2816])
    w_down = din("w_down", [2816, 1024])
    vec36 = din("vec36", [36, 512])
    hgn = din("hgn", [128, 1])
    gains = din("gains", [5, 1024])
    c_identb = din("c_identb", [128, 128], BF16)
    c_identf = din("c_identf", [128, 128])
    c_startp = din("c_startp", [128, 512])
    c_starts = din("c_starts", [128, 128])
    c_causp = din("c_causp", [128, 128])
    c_causs = din("c_causs", [128, 128])
    c_crowp = din("c_crowp", [128, 4])
    c_crows = din("c_crows", [128, 16])
    c_seqm = din("c_seqm", [128, 128], BF16)

    y = dout("y", [T_ALL, 1024])
    o_ncp = dout("o_ncp", [30, 512])
    o_nhp = dout("o_nhp", [4, 128, 128])
    o_mk = dout("o_mk", [256, 512])
    o_mv = dout("o_mv", [256, 512])
    o_ncs = dout("o_ncs", [16, 30, 512])
    o_nhs = dout("o_nhs", [16, 4, 128, 128])
    x1s = nc.dram_tensor("x1s", [T_ALL, 1024], F32, kind="Internal").ap()

    dbg_outs = {}
    st = ExitStack()
    with st:
        def sb(name, shape, dt):
            return st.enter_context(nc.sbuf_tensor(name, list(shape), dt))

        def ps(name, shape, dt):
            return st.enter_context(nc.psum_tensor(name, list(shape), dt))

        P = Prog(nc)

        NSLOT = 33
        AR = sb("arena", [128, NSLOT * 2048], BF16)
        NPG = 17
        PGT = sb("pgt", [128, NPG, 512], F32)
        PXT = sb("pxt", [128, 4, 1024], F32)
        gA = sb("gA", [128, 1024], F32)
        UB = sb("ubuf", [128, 4, 608], BF16)
        identb = sb("identb", [128, 128], BF16)
        identf = sb("identf", [128, 128], F32)
        ones32 = sb("ones32", [128, 128], F32)
        mhalf = sb("mhalf", [128, 1], F32)
        epst = sb("epst", [128, 1], F32)
        startp = sb("startp", [128, 512], F32)
        starts = sb("starts", [128, 128], F32)
        causp = sb("causp", [128, 128], F32)
        causs = sb("causs", [128, 128], F32)
        crowp = sb("crowp", [128, 4], F32)
        crows = sb("crows", [128, 16], F32)
        seqm = sb("seqm", [128, 128], BF16)
        vecT = sb("vecT", [128, 4, 36], F32)
        lbv = sb("lbv", [128, 4], F32)
        omlv = sb("omlv", [128, 4], F32)
        hgnv = sb("hgnv", [128, 1], F32)
        KTp = sb("KTp", [128, 4, 256], BF16)
        Vbp = sb("Vbp", [128, 2, 512], BF16)
        S32 = sb("S32", [128, 4, 128], F32)
        Sbf = sb("Sbf", [128, 4, 128], BF16)
        stat = sb("stat", [128, 64], F32)
        decs = sb("decs", [128, 4, 16], F32)

        Q = [ps("Q%d" % i, [128, 1024], F32) for i in range(4)]

        def bank(q, h):
            return Q[q][:, 512 * h:512 * (h + 1)]

        def bk(q, h):
            return [("ps", q, h)]

        pg_tiles = [Tile(PGT[:, i, :], [("g", i, 0), ("g", i, 1)]) for i in range(NPG)]
        px_as_pg = []
        for b_ in range(4):
            for hh in range(2):
                px_as_pg.append(Tile(PXT[:, b_, 512 * hh:512 * (hh + 1)], [("x", b_, 2 * hh), ("x", b_, 2 * hh + 1)]))

        def pxkeys(b_):
            return [("x", b_, i) for i in range(4)]

        def halves_of(tiles):
            out = []
            for t in tiles:
                out.append(t.half(0))
                out.append(t.half(1))
            return out

        def slot(s0, n_el):
            return AR[:, s0 * 2048:s0 * 2048 + n_el]

        def akeys(s0, n_el):
            return [("a", s) for s in range(s0, s0 + (n_el * 2 + 4095) // 4096)]

        def wview(s0, kc, cols):
            return slot(s0, kc * cols).rearrange("p (k e) -> p k e", k=kc)

        def DMA(eng, out, in_, reads, writes, key):
            return P.op(eng, lambda e: e.dma_start(out=out, in_=in_), reads=reads, writes=writes, dma_key=key)

        def ACT(out, in_, func, reads, writes, bias=None, scale=None, accum=None):
            kw = {}
            if bias is not None:
                kw["bias"] = bias
            if scale is not None:
                kw["scale"] = scale
            if accum is not None:
                kw["accum_out"] = accum
            return P.op("act", lambda e: e.activation(out=out, in_=in_, func=func, **kw), reads=reads, writes=writes)

        def ACOPY(out, in_, reads, writes):
            return P.op("act", lambda e: e.copy(out, in_), reads=reads, writes=writes)

        def VCOPY(out, in_, reads, writes, eng="dve"):
            return P.op(eng, lambda e: e.tensor_copy(out, in_), reads=reads, writes=writes)

        def TT(out, a, b, op, reads, writes, eng="dve"):
            return P.op(eng, lambda e: e.tensor_tensor(out, a, b, op=op), reads=reads, writes=writes)

        def TS(out, a, s1, s2, op0, op1, reads, writes, eng="dve"):
            if op1 is None:
                return P.op(eng, lambda e: e.tensor_scalar(out, a, s1, None, op0=op0), reads=reads, writes=writes)
            return P.op(eng, lambda e: e.tensor_scalar(out, a, s1, s2, op0=op0, op1=op1), reads=reads, writes=writes)

        def STT(out, a, s, b, op0, op1, reads, writes):
            return P.op("dve", lambda e: e.scalar_tensor_tensor(out, a, s, b, op0=op0, op1=op1), reads=reads, writes=writes)

        def RECIP(out, in_, reads, writes):
            return P.op("dve", lambda e: e.reciprocal(out, in_), reads=reads, writes=writes)

        mm_count = [0]
        marks = []

        def mark(name):
            marks.append((name, mm_count[0]))

        def MM(out_ap, pairs, reads, writes, start=True, stop=True):
            pairs = list(pairs)
            mm_count[0] += len(pairs)

            def fn(e):
                n = len(pairs)
                ins = None
                for i, (l, r) in enumerate(pairs):
                    ins = e.matmul(out_ap, lhsT=l, rhs=r, start=(start and i == 0), stop=(stop and i == n - 1))
                return ins
            return P.op("pe", fn, reads=reads, writes=writes)

        def TRS(items, ident, reads, writes):
            items = list(items)
            mm_count[0] += len(items)

            def fn(e):
                ins = None
                for (o_, i_) in items:
                    ins = e.transpose(o_, i_, ident)
                return ins
            return P.op("pe", fn, reads=reads, writes=writes)

        def load_w(s0, src, kc, extra_writes=()):
            cols = src.shape[1]
            v = wview(s0, kc, cols)
            keys = akeys(s0, kc * cols)
            DMA("pool", v, src.rearrange("(k p) e -> p k e", p=128), [], keys + list(extra_writes), ("a", s0))
            return v, keys

        all_hm_keys = [("hT", c) for c in range(9)] + [("mg", c) for c in range(9)]
        WDEF = {
            "K_k": (8, w_mem_kv[:, 0:512], 8), "K_v": (10, w_mem_kv[:, 512:1024], 8),
            "A_ca": (0, w_in[:, 0:512], 8), "A_cb": (2, w_in[:, 512:1024], 8), "A_co": (12, w_conv_out, 4),
            "A_g0a": (14, w_in[:, 3584:4096], 8), "A_g0b": (16, w_in[:, 4096:4608], 8),
            "B_q": (0, w_in[:, 1024:1536], 8), "B_f": (2, w_in[:, 1536:2048], 8), "B_i": (4, w_in[:, 2048:2560], 8),
            "B_hgt": (6, w_in[:, 2560:3072], 8), "B_ho": (8, w_hg_out, 4),
            "B_g1a": (10, w_in[:, 4608:5120], 8), "B_g1b": (12, w_in[:, 5120:5632], 8),
            "C_mq": (0, w_in[:, 3072:3584], 8), "C_mo": (2, w_mem_out, 4),
            "C_g2a": (4, w_in[:, 5632:6144], 8), "C_g2b": (6, w_in[:, 6144:6656], 8),
            "D_o1": (0, w_out[:, 0:512], 8), "D_o2": (2, w_out[:, 512:1024], 8),
        }
        _gslots = [4, 8, 12, 16, 20, 22]
        _uslots = [6, 10, 14, 18, 0, 2]
        for i in range(6):
            cw = 512 if i < 5 else 256
            WDEF["E_g%d" % i] = (_gslots[i], w_gate[:, 512 * i:512 * i + cw], 8)
            WDEF["E_u%d" % i] = (_uslots[i], w_up[:, 512 * i:512 * i + cw], 8)
        _dslots = [3] + list(range(23, 33))
        for i in range(11):
            WDEF["E_d%d" % i] = (_dslots[i], w_down[256 * i:256 * (i + 1), :], 2)

        def wget(name):
            s0, src, kc = WDEF[name]
            cols = src.shape[1]
            return wview(s0, kc, cols), akeys(s0, kc * cols)

        def wload(name):
            s0, src, kc = WDEF[name]
            cols = src.shape[1]
            nsl = (kc * cols * 2 + 4095) // 4096
            extra = []
            if name.startswith("E_"):
                if s0 + nsl > 18 and s0 < 23:
                    extra += [("hT", c) for c in range(9)]
                if s0 + nsl > 23:
                    extra += [("mg", c) for c in range(9)]
            return load_w(s0, src, kc, extra_writes=extra)

        stat_i = [0]

        def stat_col(n=1):
            i = stat_i[0]
            if i + n > 64:
                i = 0
            stat_i[0] = i + n
            return stat[:, i:i + n], [("st", j) for j in range(i, i + n)]

        def dump(name, ap, shape, reads, dt=F32):
            if name not in debug:
                return
            t = dout("dbg_" + name, shape, dt)
            dbg_outs[name] = t
            DMA("sp", t, ap, reads, [], ("dbg", name))

        def pow_mhalf(ap, keys, W):
            ACT(ap, ap, AF.Sqrt, keys, keys)
            RECIP(ap, ap, keys, keys)

        def rstd_from_ss(ss_ap, ss_keys, n, eps):
            r_ap, r_keys = stat_col()
            ACT(r_ap, ss_ap, AF.Sqrt, ss_keys + ["epst"], r_keys, scale=1.0 / n, bias=epst[:, 0:1])
            RECIP(r_ap, r_ap, r_keys, r_keys)
            return r_ap, r_keys

        def sumsq_1024(src, src_keys, pool, junk_ps=None):
            if junk_ps is not None:
                a_ap, a_keys = stat_col()
                ACT(Q[junk_ps][:, :], src, AF.Square, src_keys, bk(junk_ps, 0) + bk(junk_ps, 1) + a_keys, accum=a_ap)
                return a_ap, a_keys
            jt = pool.alloc()
            a_ap, a_keys = stat_col()
            b_ap, b_keys = stat_col()
            ACT(jt.f, src[:, 0:512], AF.Square, src_keys, jt.keys + a_keys, accum=a_ap)
            ACT(jt.f, src[:, 512:1024], AF.Square, src_keys, jt.keys + b_keys, accum=b_ap)
            TT(a_ap, a_ap, b_ap, ALU.add, a_keys + b_keys, a_keys)
            pool.free(jt)
            return a_ap, a_keys

        for dst, src, key in ((identb, c_identb, "identb"), (identf, c_identf, "identf"), (startp, c_startp, "startp"),
                              (starts, c_starts, "starts"), (causp, c_causp, "causp"), (causs, c_causs, "causs"),
                              (crowp, c_crowp, "crowp"), (crows, c_crows, "crows"), (seqm, c_seqm, "seqm"),
                              (hgnv, hgn, "hgnv")):
            DMA("sp", dst[:], src, [], [key], ("c", key))
        P.op("dve", lambda e: e.memset(ones32[:], 1.0), writes=["ones32"])
        P.op("dve", lambda e: e.memset(mhalf[:], -0.5), writes=["mhalf"])
        P.op("dve", lambda e: e.memset(epst[:], EPS), writes=["epst"])
        P.op("dve", lambda e: e.memset(UB[:], 0.0), writes=["ub"])
        P.op("dve", lambda e: e.memset(S32[:], 0.0), writes=[("S32", h) for h in range(4)])
        P.op("dve", lambda e: e.memset(Sbf[:], 0.0), writes=[("Sbf", h) for h in range(4)])

        DMA("sp", PXT[0:36, 0, 0:512], vec36, [], pxkeys(0), ("x", 0))
        TRS([(bank(0, 0)[:, 36 * j:36 * (j + 1)], PXT[0:36, 0, 128 * j:128 * (j + 1)]) for j in range(4)],
            identf[0:36, 0:36], pxkeys(0) + ["identf"], bk(0, 0))
        ACOPY(vecT[:].rearrange("p j k -> p (j k)"), bank(0, 0)[:, 0:144], bk(0, 0), ["vecT"])
        TT(lbv[:], vecT[:, :, 34], vecT[:, :, 35], ALU.subtract, ["vecT"], ["lbv"])
        ACT(lbv[:], lbv[:], AF.Sigmoid, ["lbv"], ["lbv"])
        TS(omlv[:], lbv[:], -1.0, 1.0, ALU.mult, ALU.add, ["lbv"], ["omlv"])

        def load_gain(dst, key, row):
            DMA("sp", dst, gains[row:row + 1, :].partition_broadcast(128), [], key if isinstance(key, list) else [key], ("gain", str(key)))

        def norm_front(src_rows, pxb, gain_ap, gain_keys, pool, src_keys=(), dve_sumsq=False, preloaded=False, junk_ps=None):
            xt = PXT[:, pxb, :]
            if not preloaded:
                DMA("sp", xt, src_rows, list(src_keys), pxkeys(pxb), ("x", pxb))
            jt2 = pool.alloc()
            if dve_sumsq:
                ss_ap, ss_keys = stat_col()
                P.op("dve", (lambda o_, a_, acc_: (lambda e: e.scalar_tensor_tensor(o_, a_, 1.0, a_, op0=ALU.mult, op1=ALU.mult,
                                                                               accum_out=acc_)))(jt2.f, xt[:, 0:512], ss_ap),
                     reads=pxkeys(pxb), writes=jt2.keys + ss_keys)
                s2_ap, s2_keys = stat_col()
                P.op("dve", (lambda o_, a_, acc_: (lambda e: e.scalar_tensor_tensor(o_, a_, 1.0, a_, op0=ALU.mult, op1=ALU.mult,
                                                                               accum_out=acc_)))(jt2.f, xt[:, 512:1024], s2_ap),
                     reads=pxkeys(pxb), writes=jt2.keys + s2_keys)
                TT(ss_ap, ss_ap, s2_ap, ALU.add, ss_keys + s2_keys, ss_keys)
            else:
                ss_ap, ss_keys = sumsq_1024(xt, pxkeys(pxb), pool, junk_ps=junk_ps)
            r_ap, r_keys = rstd_from_ss(ss_ap, ss_keys, 1024, EPS)
            hb = jt2.b
            STT(hb, xt, r_ap, gain_ap, ALU.mult, ALU.mult, pxkeys(pxb) + r_keys + gain_keys, jt2.keys)
            return jt2

        def norm_back(jt2, dstT, dcol, dkeys, pool, tb):
            hb = jt2.b
            pb = bank(*tb).bitcast(BF16)
            TRS([(pb[:, 128 * k:128 * (k + 1)], hb[:, 128 * k:128 * (k + 1)]) for k in range(8)], identb[:],
                jt2.keys + ["identb"], bk(*tb))
            ACOPY(dstT[:, :, dcol:dcol + 128], pb.rearrange("p (k t) -> p k t", k=8), bk(*tb), dkeys)
            pool.free(jt2)

        def norm_tile_to_T(src_rows, pxb, dstT, dcol, dkeys, gain_ap, gain_keys, pool, tb, src_keys=(), junk_ps=None):
            jt2 = norm_front(src_rows, pxb, gain_ap, gain_keys, pool, src_keys, junk_ps=junk_ps)
            norm_back(jt2, dstT, dcol, dkeys, pool, tb)

        gB = PGT[:, 15:17, :].rearrange("p a b -> p (a b)")
        gB_keys = pg_tiles[15].keys + pg_tiles[16].keys
        tmpE = PGT[:, 9:11, :].rearrange("p a b -> p (a b)")
        tmpE_keys = pg_tiles[9].keys + pg_tiles[10].keys
        for nm in ("A_ca", "A_cb", "A_co", "A_g0a", "A_g0b", "K_k", "K_v"):
            wload(nm)
        load_gain(gB, gB_keys, 4)
        diag = [slot(4 + 2 * j, 31 * 128).rearrange("p (k c) -> p k c", k=31) for j in range(4)]
        dkeys = [akeys(4 + 2 * j, 31 * 128) for j in range(4)]

        def build_diag():
            for j in range(4):
                TT(diag[j], identf[:].unsqueeze(1).to_broadcast([128, 31, 128]),
                   vecT[:, j, 0:31].unsqueeze(2).to_broadcast([128, 31, 128]), ALU.mult, ["identf", "vecT"], dkeys[j])

        def phaseK():
            wK, wK_keys = wget("K_k")
            wV, wV_keys = wget("K_v")
            memT = wview(6, 8, 256)
            memT_keys = akeys(6, 2048)
            pgK = FreeList(pg_tiles[0:9])
            kfr = [norm_front(mem[128 * mt:128 * (mt + 1), :], 1 + mt, gB, gB_keys, pgK, junk_ps=2) for mt in range(2)]
            for mt in range(2):
                norm_back(kfr[mt], memT, 128 * mt, memT_keys, pgK, (3, 1))
            for mt in range(2):
                for which, wv, wkeys, odram in ((0, wK, wK_keys, o_mk), (1, wV, wV_keys, o_mv)):
                    bq = which
                    MM(bank(0, bq), [(memT[:, k, 128 * mt:128 * (mt + 1)], wv[:, k, :]) for k in range(8)],
                       memT_keys + wkeys, bk(0, bq))
                    t = pg_tiles[2 * mt + which]
                    ACOPY(t.f, bank(0, bq), bk(0, bq), t.keys)
                    DMA("sp", odram[128 * mt:128 * (mt + 1), :], t.f, t.keys, [], ("o", 2 * mt + which))
                    if which == 1:
                        VCOPY(Vbp[:, mt, :], t.f, t.keys, [("Vbp", mt)])
            for h in range(4):
                bq = h % 2
                MM(bank(1, bq)[:, 0:256], [(wK[:, k, 128 * h:128 * (h + 1)], memT[:, k, :]) for k in range(8)],
                   memT_keys + wK_keys, bk(1, bq))
                ACOPY(KTp[:, h, :], bank(1, bq)[:, 0:256], bk(1, bq), [("KTp", h)])
            dump("KTp", KTp[:], [128, 4, 256], [("KTp", h) for h in range(4)], BF16)

        hT = slot(18, 8 * 1152).rearrange("p (k t) -> p k t", k=8)
        MG = AR[:, 23 * 2048:33 * 2048].bitcast(F32)[:, 0:8 * 1152].rearrange("p (k t) -> p k t", k=8)
        halves = [
            dict(tiles=list(range(0, 8)), groups=[(0, 512, "p", [0, 1, 2, 3]), (512, 512, "p", [4, 5, 6, 7])]),
            dict(tiles=list(range(8, 16)) + [16],
                 groups=[(0, 512, "p", [8, 9, 10, 11]), (512, 512, "p", [12, 13, 14, 15]), (1024, 128, "s", [16])]),
        ]

        def lcol(hi, ti):
            return 128 * ti if hi == 0 else 128 * (ti - 8)

        def hkeys(c0, W):
            return [("hT", c) for c in range(c0 // 128, (c0 + W) // 128)]

        def mkeys(c0, W):
            return [("mg", c) for c in range(c0 // 128, (c0 + W) // 128)]

        def gate_pre(pool, w_ga, k_ga, w_gb, k_gb, c0, W, n_pre, banks, dm0=0):
            hk = hkeys(c0, W)
            out = []
            for dm in range(dm0, n_pre):
                bg = banks[dm % len(banks)]
                wg, kg = (w_ga, k_ga) if dm < 4 else (w_gb, k_gb)
                dmo = 128 * (dm % 4)
                MM(bank(*bg)[:, 0:W], [(wg[:, k, dmo:dmo + 128], hT[:, k, c0:c0 + W]) for k in range(8)], kg + hk, bk(*bg))
                sg_t = pool.alloc()
                ACT(sg_t.f[:, 0:W], bank(*bg)[:, 0:W], AF.Sigmoid, bk(*bg), sg_t.keys)
                out.append(sg_t)
            return out

        def gate_merge(pool, w_pr, k_pr, in_list, w_ga, k_ga, w_gb, k_gb, c0, W, first, pre=()):
            hk = hkeys(c0, W)
            in_keys = []
            for (_, ks) in in_list:
                in_keys += ks
            pre = list(pre)
            for dm in range(8):
                bp, bg = ((0, 0), (0, 1)) if dm % 2 == 0 else ((1, 0), (1, 1))
                MM(bank(*bp)[:, 0:W], [(w_pr[:, j, 128 * dm:128 * (dm + 1)], ap) for j, (ap, _) in enumerate(in_list)],
                   k_pr + in_keys, bk(*bp))
                if dm < len(pre):
                    sg_t = pre[dm]
                else:
                    wg, kg = (w_ga, k_ga) if dm < 4 else (w_gb, k_gb)
                    dmo = 128 * (dm % 4)
                    MM(bank(*bg)[:, 0:W], [(wg[:, k, dmo:dmo + 128], hT[:, k, c0:c0 + W]) for k in range(8)], kg + hk, bk(*bg))
                    sg_t = pool.alloc()
                    ACT(sg_t.f[:, 0:W], bank(*bg)[:, 0:W], AF.Sigmoid, bk(*bg), sg_t.keys)
                if first:
                    TT(MG[:, dm, c0:c0 + W], bank(*bp)[:, 0:W], sg_t.f[:, 0:W], ALU.mult, bk(*bp) + sg_t.keys, mkeys(c0, W))
                else:
                    TT(sg_t.f[:, 0:W], bank(*bp)[:, 0:W], sg_t.f[:, 0:W], ALU.mult, bk(*bp) + sg_t.keys, sg_t.keys)
                    TT(MG[:, dm, c0:c0 + W], MG[:, dm, c0:c0 + W], sg_t.f[:, 0:W], ALU.add, mkeys(c0, W) + sg_t.keys,
                       mkeys(c0, W))
                pool.free(sg_t)

        stopped = False
        for hi, half in enumerate(halves):
            mark("h%d_0" % hi)
            if hi == 0:
                load_gain(gA[:], "gA", 0)
                pg0 = FreeList(pg_tiles)
                p0t = half["tiles"]

                def p0_front(n_):
                    ti_ = p0t[n_]
                    return norm_front(x_all[128 * ti_:128 * (ti_ + 1), :], n_ % 4, gA[:], ["gA"], pg0, junk_ps=2)

                fr_ = p0_front(0)
                for n_, ti in enumerate(p0t):
                    lc = lcol(hi, ti)
                    nx_ = p0_front(n_ + 1) if n_ + 1 < len(p0t) else None
                    norm_back(fr_, hT, lc, [("hT", lc // 128)], pg0, (3, 1))
                    fr_ = nx_
            if hi == 0:
                phaseK()
                build_diag()
                dump("hT0", hT[:, :, 0:1024], [128, 8, 1024], [("hT", c) for c in range(8)], BF16)
            if stop_after in ("p0", "p0@%d" % hi):
                stopped = True
                break

            mark("h%d_A" % hi)
            w_ca, k_ca = wget("A_ca")
            w_cb, k_cb = wget("A_cb")
            w_co, k_co = wget("A_co")
            w_g0a, k_g0a = wget("A_g0a")
            w_g0b, k_g0b = wget("A_g0b")
            pgA = FreeList(pg_tiles + px_as_pg[2:])
            ubs = UB[:].rearrange("p j (s t) -> p j s t", s=16)
            for (c0, W, kind, tiles) in half["groups"]:
                hk = hkeys(c0, W)
                if kind == "p":
                    if not (hi == 0 and c0 == 0):
                        VCOPY(UB[:, :, 0:30], UB[:, :, 512:542], ["ub"], ["ub"])
                else:
                    xt = PXT[:, 0, :]
                    t_keys = pxkeys(0)
                    for sg in range(4):
                        DMA("sp", xt[0:120, 0:512], sconv[4 * sg:4 * sg + 4].rearrange("s r c -> (s r) c"), [], t_keys, ("x", 0))
                        for s_ in range(4):
                            DMA("sp", o_ncs[4 * sg + s_, 0:22, :], xt[30 * s_ + 8:30 * s_ + 30, 0:512], t_keys, [], ("xo", 0))
                        TRS([(bank(3, 0)[:, 120 * j:120 * (j + 1)], xt[0:120, 128 * j:128 * (j + 1)]) for j in range(4)],
                            identf[0:120, 0:120], t_keys + ["identf"], bk(3, 0))
                        ACOPY(ubs[:, :, 4 * sg:4 * sg + 4, 0:30],
                              bank(3, 0)[:, 0:480].rearrange("p (j s r) -> p j s r", j=4, s=4), bk(3, 0), ["ub"])
                for j in range(4):
                    ba, bb = ((0, 0), (0, 1)) if j % 2 == 0 else ((2, 0), (2, 1))
                    MM(bank(*ba)[:, 0:W], [(w_ca[:, k, 128 * j:128 * (j + 1)], hT[:, k, c0:c0 + W]) for k in range(8)],
                       k_ca + hk, bk(*ba))
                    MM(bank(*bb)[:, 0:W], [(w_cb[:, k, 128 * j:128 * (j + 1)], hT[:, k, c0:c0 + W]) for k in range(8)],
                       k_cb + hk, bk(*bb))
                    sg_t = pgA.alloc()
                    ACT(sg_t.f[:, 0:W], bank(*bb)[:, 0:W], AF.Sigmoid, bk(*bb), sg_t.keys)
                    if kind == "p":
                        TT(UB[:, j, 30:30 + W], bank(*ba)[:, 0:W], sg_t.f[:, 0:W], ALU.mult, bk(*ba) + sg_t.keys, ["ub"])
                    else:
                        TT(ubs[:, j, :, 30:38], bank(*ba)[:, 0:W].rearrange("p (s t) -> p s t", s=16),
                           sg_t.f[:, 0:W].rearrange("p (s t) -> p s t", s=16), ALU.mult, bk(*ba) + sg_t.keys, ["ub"])
                    pgA.free(sg_t)
                if hi == 0 and c0 == 0:
                    dump("u0", UB[:, :, 0:542], [128, 4, 542], ["ub"], BF16)
                for ti in tiles:
                    if ti not in (15, 16):
                        continue
                    lc = lcol(hi, ti)
                    MM(bank(1, 0), [(hT[:, k, lc:lc + 128], w_ca[:, k, :]) for k in range(8)], k_ca + [("hT", lc // 128)], bk(1, 0))
                    MM(bank(1, 1), [(hT[:, k, lc:lc + 128], w_cb[:, k, :]) for k in range(8)], k_cb + [("hT", lc // 128)], bk(1, 1))
                    ut = pgA.alloc()
                    ACT(ut.f, bank(1, 1), AF.Sigmoid, bk(1, 1), ut.keys)
                    TT(ut.f, bank(1, 0), ut.f, ALU.mult, bk(1, 0) + ut.keys, ut.keys)
                    if ti == 15:
                        DMA("sp", o_ncp, ut.f[98:128, :], ut.keys, [], ("uo", 0))
                    else:
                        for s_ in range(16):
                            DMA("sp", o_ncs[s_, 22:30, :], ut.f[8 * s_:8 * s_ + 8, :], ut.keys, [], ("uo", 0))
                    pgA.free(ut)
                is_last = (c0, W, kind, tiles) == half["groups"][-1]
                if is_last:
                    wload("B_q")
                    wload("B_f")
                c32 = []
                for j in range(4):
                    cb_ = [(1, 0), (1, 1), (3, 0), (3, 1)][j]
                    if kind == "p":
                        pairs = [(diag[j][:, k, :], UB[:, j, k:k + W]) for k in range(31)]
                        oap = bank(*cb_)[:, 0:W]
                    else:
                        pairs = [(diag[j][:, k, :], ubs[:, j, :, k:k + 8]) for k in range(31)]
                        oap = bank(*cb_)[:, 0:W].rearrange("p (s t) -> p s t", s=16)
                    MM(oap, pairs, dkeys[j] + ["ub"], bk(*cb_))
                    ct = pgA.alloc()
                    cq = pgA.alloc()
                    ACT(ct.f[:, 0:W], bank(*cb_)[:, 0:W], AF.Identity, bk(*cb_) + ["vecT"], ct.keys, bias=vecT[:, j, 31:32])
                    ACT(cq.f[:, 0:W], bank(*cb_)[:, 0:W], AF.Square, bk(*cb_) + ["vecT"], cq.keys, bias=vecT[:, j, 31:32])
                    MM(bank(2, 0)[:, 0:W], [(ones32[:], ct.f[:, 0:W])], ct.keys + ["ones32"], bk(2, 0), start=(j == 0), stop=(j == 3))
                    MM(bank(2, 1)[:, 0:W], [(ones32[:], cq.f[:, 0:W])], cq.keys + ["ones32"], bk(2, 1), start=(j == 0), stop=(j == 3))
                    pgA.free(cq)
                    c32.append(ct)
                if hi == 0 and c0 == 0:
                    dump("c0", c32[0].f, [128, 512], c32[0].keys)
                preA = gate_pre(pgA, w_g0a, k_g0a, w_g0b, k_g0b, c0, W, 4, [(0, 0), (0, 1), (1, 0), (1, 1)])
                mean = pgA.alloc()
                rstd = pgA.alloc()
                TS(mean.f[:, 0:W], bank(2, 0)[:, 0:W], 1.0 / 512, None, ALU.mult, None, bk(2, 0), mean.keys)
                TT(rstd.f[:, 0:W], mean.f[:, 0:W], mean.f[:, 0:W], ALU.mult, mean.keys, rstd.keys)
                STT(rstd.f[:, 0:W], bank(2, 1)[:, 0:W], 1.0 / 512, rstd.f[:, 0:W], ALU.mult, ALU.subtract,
                    bk(2, 1) + rstd.keys, rstd.keys)
                ACT(rstd.f[:, 0:W], rstd.f[:, 0:W], AF.Sqrt, rstd.keys + ["epst"], rstd.keys, bias=epst[:, 0:1])
                RECIP(rstd.f[:, 0:W], rstd.f[:, 0:W], rstd.keys, rstd.keys)
                preA = preA + gate_pre(pgA, w_g0a, k_g0a, w_g0b, k_g0b, c0, W, 8, [(0, 0), (0, 1), (1, 0), (1, 1)], dm0=4)
                cT = [pgA.alloc(), pgA.alloc()]
                cT_list = []
                for j in range(4):
                    ct = c32[j]
                    cv_ = cT[j // 2].b[:, 512 * (j % 2):512 * (j % 2) + W]
                    ckey = [cT[j // 2].keys[j % 2]]
                    TT(ct.f[:, 0:W], ct.f[:, 0:W], mean.f[:, 0:W], ALU.subtract, ct.keys + mean.keys, ct.keys)
                    TT(ct.f[:, 0:W], ct.f[:, 0:W], rstd.f[:, 0:W], ALU.mult, ct.keys + rstd.keys, ct.keys)
                    ACT(cv_, ct.f[:, 0:W], AF.Silu, ct.keys + ["vecT"], ckey, scale=vecT[:, j, 32:33], bias=vecT[:, j, 33:34])
                    cT_list.append((cv_, ckey))
                    pgA.free(ct)
                pgA.free(mean, rstd)
                if hi == 0 and c0 == 0:
                    dump("cl0", cT[0].b[:, 0:512], [128, 512], cT[0].keys, BF16)
                gate_merge(pgA, w_co, k_co, cT_list, w_g0a, k_g0a, w_g0b, k_g0b, c0, W, True, pre=preA)
                pgA.free(*cT)
            for nm in ("B_i", "B_hgt", "B_ho", "B_g1a", "B_g1b"):
                wload(nm)
            if hi == 0:
                dump("mgA", MG[:, :, 0:1024], [128, 8, 1024], [("mg", c) for c in range(8)])
            if stop_after in ("pA", "pA@%d" % hi):
                stopped = True
                break

            if hi == 0 and "h0A" in os.environ.get("KDBG", ""):
                continue
            mark("h%d_B" % hi)
            w_q, k_q = wget("B_q")
            w_f, k_f = wget("B_f")
            w_i, k_i = wget("B_i")
            w_hgt, k_hgt = wget("B_hgt")
            w_ho, k_ho = wget("B_ho")
            w_g1a, k_g1a = wget("B_g1a")
            w_g1b, k_g1b = wget("B_g1b")
            S0b = [slot(14 + i, 2048).rearrange("p (s v) -> p s v", s=16) for i in range(2)]
            S0b_keys = [akeys(14 + i, 2048) for i in range(2)]
            S0f = [AR[:, 16 * 2048:18 * 2048].bitcast(F32).rearrange("p (s v) -> p s v", s=16),
                   PXT[:, 2:4, :].rearrange("p a b -> p (a b)").rearrange("p (s v) -> p s v", s=16)]
            S0f_keys = [akeys(16, 4096), pxkeys(2) + pxkeys(3)]

            def load_states(b_):
                src = shg[4 * b_:4 * b_ + 4].rearrange("s h d v -> d (s h) v")
                DMA("pool", S0b[b_ % 2], src, [], S0b_keys[b_ % 2], ("s0b", b_ % 2))
                DMA("sp", S0f[b_ % 2], src, [], S0f_keys[b_ % 2], ("s0f", b_ % 2))
            all_t = pg_tiles + px_as_pg
            pgB = FreeList(all_t[0:14])
            hpB = FreeList(halves_of(all_t[14:]))
            oT = [bank(2, 0), bank(2, 1), bank(3, 0), bank(3, 1)]
            oTk = [bk(2, 0), bk(2, 1), bk(3, 0), bk(3, 1)]
            sq_i = [0]
            st_i = [0]

            sq_banks = [(1, 0), (0, 0), (0, 1)]

            def psq_bank():
                b_ = sq_banks[sq_i[0] % 3]
                sq_i[0] += 1
                return b_

            for (c0, W, kind, tiles) in half["groups"]:
                if kind == "s":
                    pgB = FreeList(all_t[0:12])
                    hpB = FreeList(halves_of(pg_tiles[12:17] + px_as_pg[0:4]))
                    load_states(0)
                    load_states(1)
                hk = hkeys(c0, W)
                CH = 32 if kind == "p" else 8
                smask = startp if kind == "p" else starts
                caus = causp if kind == "p" else causs
                kinvT, kendT, qdecT = [], [], []
                fss, qss = [], []
                pb4 = [(0, 0), (0, 1), (1, 0), (1, 1)]
                for h in range(4):
                    hc = slice(128 * h, 128 * (h + 1))
                    b1, b2 = pb4[(2 * h) % 4], pb4[(2 * h + 1) % 4]
                    MM(bank(*b1)[:, 0:W], [(w_f[:, k, hc], hT[:, k, c0:c0 + W]) for k in range(8)], k_f + hk, bk(*b1))
                    fs = pgB.alloc()
                    ACT(fs.f[:, 0:W], bank(*b1)[:, 0:W], AF.Sigmoid, bk(*b1), fs.keys)
                    MM(bank(*b2)[:, 0:W], [(w_q[:, k, hc], hT[:, k, c0:c0 + W]) for k in range(8)], k_q + hk, bk(*b2))
                    qs = pgB.alloc()
                    ACT(qs.f[:, 0:W], bank(*b2)[:, 0:W], AF.Silu, bk(*b2), qs.keys)
                    fss.append(fs)
                    qss.append(qs)
                vbs = []
                for tt in range(len(tiles)):
                    gc = c0 + 128 * tt
                    bv = pb4[tt % 4]
                    MM(bank(*bv), [(hT[:, k, gc:gc + 128], w_i[:, k, :]) for k in range(8)], k_i + [("hT", gc // 128)], bk(*bv))
                    vb = hpB.alloc()
                    ACOPY(vb.b, bank(*bv), bk(*bv), vb.keys)
                    vbs.append(vb)
                for h in range(4):
                    fs, qs = fss[h], qss[h]
                    TS(fs.f[:, 0:W], fs.f[:, 0:W], omlv[:, h:h + 1], lbv[:, h:h + 1], ALU.mult, ALU.add,
                       fs.keys + ["omlv", "lbv"], fs.keys)
                    Pt = pgB.alloc()
                    P.op("dve", (lambda o_, m_, f_: (lambda e: e.tensor_tensor_scan(o_, m_, f_, 1.0, op0=ALU.max, op1=ALU.mult)))(
                        Pt.f[:, 0:W], smask[:, 0:W], fs.f[:, 0:W]), reads=fs.keys + ["startp", "starts"], writes=Pt.keys)
                    Pv = Pt.f[:, 0:W].rearrange("p (c i) -> p c i", i=CH)
                    VCOPY(decs[:, h, :], Pv[:, :, CH - 1], Pt.keys, [("decs", h)])
                    qd = hpB.alloc()
                    TT(qd.b[:, 0:W], qs.f[:, 0:W], Pt.f[:, 0:W], ALU.mult, qs.keys + Pt.keys, qd.keys)
                    rP = pgB.alloc()
                    RECIP(rP.f[:, 0:W], Pt.f[:, 0:W], Pt.keys, rP.keys)
                    TS(fs.f[:, 0:W], fs.f[:, 0:W], -1.0, 1.0, ALU.mult, ALU.add, fs.keys, fs.keys)
                    TT(rP.f[:, 0:W], fs.f[:, 0:W], rP.f[:, 0:W], ALU.mult, fs.keys + rP.keys, rP.keys)
                    ki = hpB.alloc()
                    ACOPY(ki.b[:, 0:W], rP.f[:, 0:W], rP.keys, ki.keys)
                    ke = hpB.alloc()
                    TT(ke.b[:, 0:W].rearrange("p (c i) -> p c i", i=CH), rP.f[:, 0:W].rearrange("p (c i) -> p c i", i=CH),
                       decs[:, h, :].unsqueeze(2).to_broadcast([128, 16, CH]), ALU.mult, rP.keys + [("decs", h)], ke.keys)
                    pgB.free(fs, rP, qs, Pt)
                    kinvT.append(ki)
                    kendT.append(ke)
                    qdecT.append(qd)
                if (c0, W, kind, tiles) == half["groups"][-1]:
                    wload("C_mq")
                    wload("C_mo")
                if hi == 0 and c0 == 0:
                    dump("qd0", qdecT[0].b, [128, 512], qdecT[0].keys, BF16)
                    dump("ki0", kinvT[0].b, [128, 512], kinvT[0].keys, BF16)
                    dump("ke0", kendT[0].b, [128, 512], kendT[0].keys, BF16)
                for tt, ti in enumerate(tiles):
                    tc = 128 * tt
                    gc = c0 + tc
                    tcs = slice(tc, tc + 128)
                    vb = vbs[tt]
                    nq = 1 if kind == "p" else 4
                    ptb = bank(1, 1).bitcast(BF16)
                    TRS([(ptb[:, 128 * h:128 * (h + 1)], kendT[h].b[:, tcs]) for h in range(4)], identb[:],
                        [k_ for h in range(4) for k_ in kendT[h].keys] + ["identb"], bk(1, 1))
                    kps = [(ptb[:, 128 * h:128 * (h + 1)], bk(1, 1)) for h in range(4)]
                    kems = [None] * 4

                    def build_kems(q_):
                        for cc_ in range(4):
                            km = hpB.alloc()
                            crow = crowp[:, cc_:cc_ + 1] if kind == "p" else crows[:, 4 * q_ + cc_:4 * q_ + cc_ + 1]
                            ACT(km.b, ptb[:, 0:512], AF.Copy, bk(1, 1) + ["crowp", "crows"], km.keys, scale=crow)
                            kems[cc_] = km
                    smt = hpB.alloc()
                    sb_ = psq_bank()
                    for h in range(4):
                        MM(bank(*sb_)[:, 128 * h:128 * (h + 1)], [(kinvT[h].b[:, tcs], qdecT[h].b[:, tcs])],
                           kinvT[h].keys + qdecT[h].keys, bk(*sb_))
                    TT(smt.b.rearrange("p (h t) -> p h t", h=4), bank(*sb_).rearrange("p (h t) -> p h t", h=4),
                       caus[:].unsqueeze(1).to_broadcast([128, 4, 128]), ALU.mult, bk(*sb_) + ["causp", "causs"], smt.keys)
                    for h in range(4):
                        MM(oT[h][:, tcs], [(vb.b[:, 128 * h:128 * (h + 1)], smt.b[:, 128 * h:128 * (h + 1)])],
                           vb.keys + smt.keys, oTk[h], start=True, stop=False)
                    nch = 128 // CH
                    if kind == "s":
                        pass
                    for c in range(nch):
                        q_, cc = divmod(c, 4)
                        if cc == 0:
                            if kems[0] is not None:
                                hpB.free(*kems)
                            build_kems(q_)
                        bq, bc = divmod(c, 4)
                        if kind == "s" and bc == 0:
                            if bq >= 1 and bq + 1 < 4:
                                load_states(bq + 1)
                        kvs = []
                        kb_ = psq_bank()
                        for h in range(4):
                            sq_ap, sq_k = bank(*kb_)[:, 128 * h:128 * (h + 1)], bk(*kb_)
                            MM(sq_ap, [(kems[cc].b[:, 128 * h:128 * (h + 1)], vb.b[:, 128 * h:128 * (h + 1)])],
                               kems[cc].keys + vb.keys, sq_k)
                            kvs.append((sq_ap, sq_k))
                            cs = slice(tc + c * CH, tc + (c + 1) * CH)
                            if kind == "p":
                                s_ap, s_k = Sbf[:, h, :], [("Sbf", h)]
                            else:
                                s_ap, s_k = S0b[bq % 2][:, bc * 4 + h, :], S0b_keys[bq % 2]
                            MM(oT[h][:, cs], [(s_ap, qdecT[h].b[:, cs])], s_k + qdecT[h].keys, oTk[h], start=False, stop=(c == nch - 1))
                        ci = tt * nch + c
                        dbc = decs[:, :, ci].unsqueeze(2).to_broadcast([128, 4, 128])
                        dkeys_ = [("decs", h) for h in range(4)]
                        kv4 = bank(*kb_).rearrange("p (h v) -> p h v", h=4)
                        if kind == "p":
                            s32k = [("S32", h) for h in range(4)]
                            sbfk = [("Sbf", h) for h in range(4)]
                            TT(S32[:], S32[:], dbc, ALU.mult, s32k + dkeys_, s32k)
                            TT(Sbf[:], S32[:], kv4, ALU.add, s32k + bk(*kb_), sbfk)
                            TT(S32[:], S32[:], kv4, ALU.add, s32k + bk(*kb_), s32k)
                        else:
                            sv = S0f[bq % 2][:, bc * 4:bc * 4 + 4, :]
                            TT(sv, sv, dbc, ALU.mult, S0f_keys[bq % 2] + dkeys_, S0f_keys[bq % 2])
                            TT(sv, sv, kv4, ALU.add, S0f_keys[bq % 2] + bk(*kb_), S0f_keys[bq % 2])
                        if kind == "s" and bc == 3:
                            DMA("sp", o_nhs[4 * bq:4 * bq + 4].rearrange("s h d v -> d (s h) v"), S0f[bq % 2], S0f_keys[bq % 2], [],
                                ("so", bq % 2))
                    hpB.free(*kems)
                    hpB.free(smt)
                for h in range(4):
                    hpB.free(kinvT[h], kendT[h])
                hpB.free(*vbs)
                if hi == 1 and c0 == 512 and "skipnhp" not in os.environ.get("KDBG", ""):
                    DMA("sp", o_nhp.rearrange("h d v -> d h v"), S32[:], [("S32", h) for h in range(4)], [], ("so", 1))
                og = []
                rsts = []
                for h in range(4):
                    osq = pgB.alloc()
                    ACT(osq.f[:, 0:W], oT[h][:, 0:W], AF.Square, oTk[h], osq.keys)
                    bs_ = pb4[h]
                    MM(bank(*bs_)[:, 0:W], [(ones32[:], osq.f[:, 0:W])], osq.keys + ["ones32"], bk(*bs_))
                    TS(osq.f[:, 0:W], bank(*bs_)[:, 0:W], 1.0 / 128, EPS, ALU.mult, ALU.add, bk(*bs_), osq.keys)
                    rsts.append(osq)
                gts = []
                for h in range(4):
                    hc = slice(128 * h, 128 * (h + 1))
                    bg_ = pb4[h]
                    MM(bank(*bg_)[:, 0:W], [(w_hgt[:, k, hc], hT[:, k, c0:c0 + W]) for k in range(8)], k_hgt + hk, bk(*bg_))
                    gt = pgB.alloc()
                    ACT(gt.f[:, 0:W], bank(*bg_)[:, 0:W], AF.Silu, bk(*bg_), gt.keys)
                    gts.append(gt)
                for h in range(4):
                    pow_mhalf(rsts[h].f[:, 0:W], rsts[h].keys, W)
                preB = gate_pre(pgB, w_g1a, k_g1a, w_g1b, k_g1b, c0, W, 6 if kind == "p" else 4, pb4)
                for h in range(4):
                    rst, gt = rsts[h], gts[h]
                    STT(rst.f[:, 0:W], oT[h][:, 0:W], hgnv[:, 0:1], rst.f[:, 0:W], ALU.mult, ALU.mult,
                        oTk[h] + ["hgnv"] + rst.keys, rst.keys)
                    ogh = hpB.alloc()
                    TT(ogh.b[:, 0:W], rst.f[:, 0:W], gt.f[:, 0:W], ALU.mult, rst.keys + gt.keys, ogh.keys)
                    pgB.free(rst, gt)
                    og.append(ogh)
                    hpB.free(qdecT[h])
                if hi == 0 and c0 == 0:
                    dump("og0", og[0].b, [128, 512], og[0].keys, BF16)
                gate_merge(pgB, w_ho, k_ho, [(o_.b[:, 0:W], o_.keys) for o_ in og], w_g1a, k_g1a, w_g1b, k_g1b, c0, W, False, pre=preB)
                for o_ in og:
                    hpB.free(o_)
            wload("C_g2a")
            wload("C_g2b")
            if hi == 0:
                dump("mgB", MG[:, :, 0:1024], [128, 8, 1024], [("mg", c) for c in range(8)])
            if stop_after in ("pB", "pB@%d" % hi):
                stopped = True
                break

            mark("h%d_C" % hi)
            w_mq, k_mq = wget("C_mq")
            w_mo, k_mo = wget("C_mo")
            w_g2a, k_g2a = wget("C_g2a")
            w_g2b, k_g2b = wget("C_g2b")
            Ks = wview(8, 8, 512)
            Ks_keys = akeys(8, 4096)
            Vs = [wview(10 + 2 * i, 8, 512) for i in range(2)]
            Vs_keys = [akeys(10 + 2 * i, 4096) for i in range(2)]
            KTs = [wview(14 + 2 * i, 16, 256) for i in range(2)]
            KTs_keys = [akeys(14 + 2 * i, 4096) for i in range(2)]
            def c_load_k(qq):
                DMA("pool", Ks, ck[4 * qq:4 * qq + 4].rearrange("s (mt p) e -> p (s mt) e", p=128), [], Ks_keys, ("a", 8))

            def c_load_v(qq):
                DMA("pool", Vs[qq % 2], cv[4 * qq:4 * qq + 4].rearrange("s (mt p) e -> p (s mt) e", p=128), [], Vs_keys[qq % 2],
                    ("a", 10 + 2 * (qq % 2)))

            if hi == 1:
                c_load_k(0)
                c_load_v(0)
                c_load_v(1)
            pgC = FreeList(all_t[0:13])
            hpC = FreeList(halves_of(all_t[13:]))
            SCALE = 128 ** -0.5
            ss_i = [0]

            ss_banks = [(1, 0), (1, 1), (0, 0), (0, 1)]

            def pss():
                b_ = ss_banks[ss_i[0] % 4]
                ss_i[0] += 1
                return bank(*b_)[:, 0:256], bk(*b_)

            def softmax_rows(sc_ap, sc_k, np_, pool_f, pool_h):
                mx, mx_k = stat_col()
                P.op("dve", (lambda o_, i_: (lambda e: e.tensor_reduce(out=o_, in_=i_, axis=AX.X, op=ALU.max)))(
                    mx[0:np_, :], sc_ap), reads=sc_k, writes=mx_k)
                TS(mx[0:np_, :], mx[0:np_, :], -SCALE, None, ALU.mult, None, mx_k, mx_k)
                pe_ = pool_f.alloc()
                rs, rs_k = stat_col()
                ACT(pe_.f[0:np_, 0:256], sc_ap, AF.Exp, sc_k + mx_k, pe_.keys + rs_k, bias=mx[0:np_, :], scale=SCALE,
                    accum=rs[0:np_, :])
                RECIP(rs[0:np_, :], rs[0:np_, :], rs_k, rs_k)
                pn = pool_h.alloc()
                TS(pn.b[0:np_, 0:256], pe_.f[0:np_, 0:256], rs[0:np_, :], None, ALU.mult, None, pe_.keys + rs_k, pn.keys)
                pool_f.free(pe_)
                return pn

            for (c0, W, kind, tiles) in half["groups"]:
                hk = hkeys(c0, W)
                qT = []
                for h in range(4):
                    hc = slice(128 * h, 128 * (h + 1))
                    MM(bank(0, h % 2)[:, 0:W], [(w_mq[:, k, hc], hT[:, k, c0:c0 + W]) for k in range(8)], k_mq + hk, bk(0, h % 2))
                    qh = hpC.alloc()
                    ACOPY(qh.b[:, 0:W], bank(0, h % 2)[:, 0:W], bk(0, h % 2), qh.keys)
                    qT.append(qh)
                if (c0, W, kind, tiles) == half["groups"][-1]:
                    wload("D_o1")
                preC = gate_pre(pgC, w_g2a, k_g2a, w_g2b, k_g2b, c0, W, 8, [(2, 0), (2, 1), (3, 0), (3, 1)])
                om = []
                if kind == "p":
                    om = [hpC.alloc() for _ in range(4)]
                    sq_regs = [1, 3]
                    ntl = len(tiles)

                    def c_scores(tt_):
                        qi_ = sq_regs[tt_ % 2]
                        tcs_ = slice(128 * tt_, 128 * (tt_ + 1))
                        for h in range(4):
                            MM(Q[qi_][:, 256 * h:256 * (h + 1)], [(qT[h].b[:, tcs_], KTp[:, h, :])], qT[h].keys + [("KTp", h)],
                               bk(qi_, h // 2))

                    def c_front(tt):
                        qi = sq_regs[tt % 2]
                        sc4 = Q[qi][:, :].rearrange("p (h m) -> p h m", h=4)
                        sck = bk(qi, 0) + bk(qi, 1)
                        mx, mx_k = stat_col(4)
                        P.op("dve", (lambda o_, i_: (lambda e: e.tensor_reduce(out=o_, in_=i_, axis=AX.X, op=ALU.max)))(mx, sc4),
                             reads=sck, writes=mx_k)
                        TS(mx, mx, -SCALE, None, ALU.mult, None, mx_k, mx_k)
                        rs, rs_k = stat_col(4)
                        phs = []
                        for h in range(4):
                            ph = hpC.alloc()
                            ACT(ph.f, sc4[:, h, :], AF.Exp, sck + mx_k, ph.keys + rs_k, bias=mx[:, h:h + 1], scale=SCALE,
                                accum=rs[:, h:h + 1])
                            phs.append(ph)
                        return phs, rs, rs_k

                    def c_back(tt, st_):
                        phs, rs, rs_k = st_
                        RECIP(rs, rs, rs_k, rs_k)
                        pn = pgC.alloc()
                        for h in range(4):
                            TS(pn.b[:, 256 * h:256 * (h + 1)], phs[h].f, rs[:, h:h + 1], None, ALU.mult, None,
                               phs[h].keys + rs_k, pn.keys)
                            hpC.free(phs[h])
                        ptb = bank(0, tt % 2).bitcast(BF16)
                        TRS([(ptb[:, 128 * i:128 * (i + 1)], pn.b[:, 128 * i:128 * (i + 1)]) for i in range(8)], identb[:],
                            pn.keys + ["identb"], bk(0, tt % 2))
                        pT = pgC.alloc()
                        ACOPY(pT.b, ptb, bk(0, tt % 2), pT.keys)
                        sg_, tl = divmod(tt, 2)
                        for h in range(4):
                            oc = 256 * (h % 2) + 128 * tl
                            MM(bank(2, h // 2)[:, oc:oc + 128],
                               [(Vbp[:, mt, 128 * h:128 * (h + 1)], pT.b[:, 128 * (2 * h + mt):128 * (2 * h + mt + 1)]) for mt in range(2)],
                               [("Vbp", 0), ("Vbp", 1)] + pT.keys, bk(2, h // 2))
                        pgC.free(pn, pT)
                        if tl == 1 or tt == ntl - 1:
                            for h in range(4):
                                ACOPY(om[h].b[:, 256 * sg_:256 * sg_ + 128 * (tl + 1)],
                                      bank(2, h // 2)[:, 256 * (h % 2):256 * (h % 2) + 128 * (tl + 1)], bk(2, h // 2), om[h].keys)

                    c_scores(0)
                    prev_ = None
                    for tt in range(ntl):
                        if tt + 1 < ntl:
                            c_scores(tt + 1)
                        cur_ = c_front(tt)
                        if prev_ is not None:
                            c_back(tt - 1, prev_)
                        prev_ = cur_
                    c_back(ntl - 1, prev_)
                else:
                    ob, obk = bank(2, 0), bk(2, 0)

                    def c_kt(qq):
                        bb = qq % 2
                        for s_ in range(4):
                            pt_ap = bank(3, s_ % 2).bitcast(BF16)
                            pt_k = bk(3, s_ % 2)
                            TRS([(pt_ap[:, 256 * h + 128 * mt:256 * h + 128 * (mt + 1)], Ks[:, s_ * 2 + mt, 128 * h:128 * (h + 1)])
                                 for h in range(4) for mt in range(2)], identb[:], Ks_keys + ["identb"], pt_k)
                            ACOPY(KTs[bb][:, 4 * s_:4 * s_ + 4, :], pt_ap.rearrange("p (h m) -> p h m", h=4), pt_k, KTs_keys[bb])

                    def s_front(q_):
                        bi = q_ % 2
                        sc4 = Q[1][0:32, :].rearrange("p (h m) -> p h m", h=4)
                        sck = bk(1, 0) + bk(1, 1)
                        for h in range(4):
                            qm = hpC.alloc()
                            TT(qm.b[:, 0:128].rearrange("p (s l) -> p s l", s=4),
                               qT[h].b[:, 32 * q_:32 * q_ + 32].unsqueeze(1).to_broadcast([128, 4, 32]),
                               seqm[:].rearrange("p (s l) -> p s l", s=4), ALU.mult, qT[h].keys + ["seqm"], qm.keys)
                            MM(Q[1][0:32, 256 * h:256 * (h + 1)],
                               [(qm.b[:, 32 * s_:32 * s_ + 32], KTs[bi][:, s_ * 4 + h, :]) for s_ in range(4)],
                               qm.keys + KTs_keys[bi], bk(1, h // 2))
                            hpC.free(qm)
                        mx, mx_k = stat_col(4)
                        P.op("dve", (lambda o_, i_: (lambda e: e.tensor_reduce(out=o_, in_=i_, axis=AX.X, op=ALU.max)))(mx[0:32, :], sc4),
                             reads=sck, writes=mx_k)
                        TS(mx[0:32, :], mx[0:32, :], -SCALE, None, ALU.mult, None, mx_k, mx_k)
                        rs, rs_k = stat_col(4)
                        phs = []
                        for h in range(4):
                            ph = hpC.alloc()
                            ACT(ph.f[0:32, :], sc4[:, h, :], AF.Exp, sck + mx_k, ph.keys + rs_k, bias=mx[0:32, h:h + 1], scale=SCALE,
                                accum=rs[0:32, h:h + 1])
                            phs.append(ph)
                        return phs, rs, rs_k

                    def s_back(q_, st_):
                        bi = q_ % 2
                        phs, rs, rs_k = st_
                        RECIP(rs[0:32, :], rs[0:32, :], rs_k, rs_k)
                        pn = pgC.alloc()
                        for h in range(4):
                            TS(pn.b[0:32, 256 * h:256 * (h + 1)], phs[h].f[0:32, :], rs[0:32, h:h + 1], None, ALU.mult, None,
                               phs[h].keys + rs_k, pn.keys)
                            hpC.free(phs[h])
                        ptb = bank(0, q_ % 2).bitcast(BF16)
                        TRS([(ptb[:, 32 * i:32 * (i + 1)], pn.b[0:32, 128 * i:128 * (i + 1)]) for i in range(8)], identb[0:32, 0:32],
                            pn.keys + ["identb"], bk(0, q_ % 2))
                        pT = hpC.alloc()
                        ACOPY(pT.b[:, 0:256], ptb[:, 0:256], bk(0, q_ % 2), pT.keys)
                        for h in range(4):
                            for s_ in range(4):
                                col = 128 * h + 32 * q_ + 8 * s_
                                MM(ob[:, col:col + 8],
                                   [(Vs[bi][:, s_ * 2 + mt, 128 * h:128 * (h + 1)],
                                     pT.b[:, 32 * (2 * h + mt) + 8 * s_:32 * (2 * h + mt) + 8 * s_ + 8]) for mt in range(2)],
                                   Vs_keys[bi] + pT.keys, obk)
                        pgC.free(pn)
                        hpC.free(pT)
                        if q_ + 2 < 4:
                            c_load_v(q_ + 2)

                    c_kt(0)
                    c_load_k(1)
                    st_q = s_front(0)
                    for q_ in range(4):
                        st_n = None
                        if q_ + 1 < 4:
                            c_kt(q_ + 1)
                            if q_ + 2 < 4:
                                c_load_k(q_ + 2)
                            st_n = s_front(q_ + 1)
                        s_back(q_, st_q)
                        st_q = st_n
                    for h in range(4):
                        omh = hpC.alloc()
                        ACOPY(omh.b[:, 0:W], ob[:, 128 * h:128 * (h + 1)], obk, omh.keys)
                        om.append(omh)
                for h in range(4):
                    hpC.free(qT[h])
                if hi == 0 and c0 == 0:
                    dump("om0", om[0].b, [128, 512], om[0].keys, BF16)
                gate_merge(pgC, w_mo, k_mo, [(o_.b[:, 0:W], o_.keys) for o_ in om], w_g2a, k_g2a, w_g2b, k_g2b, c0, W, False, pre=preC)
                for o_ in om:
                    hpC.free(o_)
            wload("D_o2")
            if hi == 0:
                dump("mgC", MG[:, :, 0:1024], [128, 8, 1024], [("mg", c) for c in range(8)])
            if stop_after in ("pC", "pC@%d" % hi):
                stopped = True
                break

            mark("h%d_D" % hi)
            w_o1, k_o1 = wget("D_o1")
            w_o2, k_o2 = wget("D_o2")
            if hi == 0:
                for nm in ("A_co", "A_g0a", "A_g0b"):
                    wload(nm)
                build_diag()
            else:
                for nm in ("E_g0", "E_u0", "E_g1", "E_u1", "E_g2", "E_u2", "E_g3", "E_u3", "E_g4", "E_g5"):
                    wload(nm)
            load_gain(gA[:], "gA", 1)
            pgD = FreeList(pg_tiles[0:9] + pg_tiles[11:15])
            dtiles = half["tiles"]
            if hi == 0:
                load_gain(gB, gB_keys, 0)
                ntiles = halves[1]["tiles"]

            qring = [0, 1] if hi == 0 else [0, 1, 3]
            xring = [0, 1] if hi == 0 else [0, 1, 2]
            ddepth = len(qring) - 1

            def d_front(n_):
                ti = dtiles[n_]
                lc = lcol(hi, ti)
                mgb = pgD.alloc()
                mv_ = mgb.b.rearrange("p (k t) -> p k t", k=8)
                if n_ % 2 == 0:
                    ACOPY(mv_, MG[:, :, lc:lc + 128], [("mg", lc // 128)], mgb.keys)
                else:
                    VCOPY(mv_, MG[:, :, lc:lc + 128], [("mg", lc // 128)], mgb.keys)
                qa = qring[n_ % len(qring)]
                MM(bank(qa, 0), [(mv_[:, k, :], w_o1[:, k, :]) for k in range(8)], mgb.keys + k_o1, bk(qa, 0))
                MM(bank(qa, 1), [(mv_[:, k, :], w_o2[:, k, :]) for k in range(8)], mgb.keys + k_o2, bk(qa, 1))
                pgD.free(mgb)
                xb = xring[n_ % len(xring)]
                DMA("sp", PXT[:, xb, :], x_all[128 * ti:128 * (ti + 1), :], [], pxkeys(xb), ("x", xb))

            def d_back(n_):
                ti = dtiles[n_]
                qa = qring[n_ % len(qring)]
                xb = xring[n_ % len(xring)]
                qkeys = bk(qa, 0) + bk(qa, 1)
                ss_ap, ss_keys = sumsq_1024(Q[qa][:, :], qkeys, pgD, junk_ps=2)
                r_ap, r_keys = rstd_from_ss(ss_ap, ss_keys, 1024, EPS)
                STT(tmpE, Q[qa][:, :], r_ap, gA[:], ALU.mult, ALU.mult, qkeys + r_keys + ["gA"], tmpE_keys)
                TT(PXT[:, xb, :], PXT[:, xb, :], tmpE, ALU.add, pxkeys(xb) + tmpE_keys, pxkeys(xb))
                DMA("sp", x1s[128 * ti:128 * (ti + 1), :], PXT[:, xb, :], pxkeys(xb), [("x1s", ti)], ("xs", xb))

            p0_state = {}

            def p0_front(n_):
                if hi != 0 or n_ >= len(ntiles):
                    return
                ti = ntiles[n_]
                p0_state[n_] = norm_front(x_all[128 * ti:128 * (ti + 1), :], 2 + n_ % 2, gB, gB_keys, pgD, junk_ps=2)

            def p0_back(n_):
                if n_ not in p0_state:
                    return
                ti = ntiles[n_]
                lc = lcol(1, ti)
                norm_back(p0_state.pop(n_), hT, lc, [("hT", lc // 128)], pgD, (3, 1))

            for n_ in range(min(ddepth, len(dtiles))):
                d_front(n_)
            p0_front(0)
            for n_ in range(len(dtiles)):
                if n_ + ddepth < len(dtiles):
                    d_front(n_ + ddepth)
                p0_front(n_ + 1)
                p0_back(n_)
                d_back(n_)
            if hi == 0:
                for n_ in range(len(dtiles), len(ntiles)):
                    p0_front(n_ + 1)
                    p0_back(n_)
            if hi == 0:
                wload("A_ca")
                wload("A_cb")
            else:
                for nm in ["E_u4", "E_u5"] + ["E_d%d" % i for i in range(11)]:
                    wload(nm)
            if stop_after in ("pD", "pD@%d" % hi):
                stopped = True
                break

        if not stopped:
            mark("E")
            load_gain(gA[:], "gA", 2)
            load_gain(gB, gB_keys, 3)
            w_gp, k_gp, w_upp, k_upp, w_dn, k_dn = [], [], [], [], [], []
            for i in range(6):
                v_, k_ = wget("E_g%d" % i)
                w_gp.append(v_)
                k_gp.append(k_)
                v_, k_ = wget("E_u%d" % i)
                w_upp.append(v_)
                k_upp.append(k_)
            for i in range(11):
                v_, k_ = wget("E_d%d" % i)
                w_dn.append(v_)
                k_dn.append(k_)
            h2T = [PGT[:, 11 + 2 * b_:13 + 2 * b_, :].rearrange("p a b -> p (a b)").bitcast(BF16).rearrange("p (k t) -> p k t", k=8)
                   for b_ in range(2)]
            h2T_keys = [pg_tiles[11 + 2 * b_].keys + pg_tiles[12 + 2 * b_].keys for b_ in range(2)]
            pgE = FreeList(pg_tiles[0:3])
            hpE = FreeList(halves_of(pg_tiles[3:9]) + halves_of(px_as_pg[0:0]))
            egroups = [[2 * g, 2 * g + 1] for g in range(8)] + [[16]]
            gu_i = [0]

            def pgu():
                b_ = [(0, 0), (0, 1), (1, 1)][gu_i[0] % 3]
                gu_i[0] += 1
                return bank(*b_), bk(*b_)

            e_pend = {}

            def e_load(gi_):
                for tt_, ti_ in enumerate(egroups[gi_]):
                    pxb_ = (2 * gi_ + tt_) % 4
                    DMA("sp", PXT[:, pxb_, :], x1s[128 * ti_:128 * (ti_ + 1), :], [("x1s", ti_)], pxkeys(pxb_), ("x", pxb_))

            def e_norm_front(gi_):
                e_pend[gi_] = [norm_front(x1s[128 * ti_:128 * (ti_ + 1), :], (2 * gi_ + tt_) % 4, gA[:], ["gA"], pgE,
                                          src_keys=[("x1s", ti_)], dve_sumsq=False, preloaded=True)
                               for tt_, ti_ in enumerate(egroups[gi_])]

            def e_norm_back(gi_):
                for tt_, jt2 in enumerate(e_pend.pop(gi_)):
                    norm_back(jt2, h2T[gi_ % 2], 128 * tt_, h2T_keys[gi_ % 2], pgE, (1, 0))

            e_load(0)
            e_norm_front(0)
            e_norm_back(0)
            pending = []

            def make_post(gi_, tt_, ti_):
                def post():
                    pxb = (2 * gi_ + tt_) % 4
                    qd_ = 2 + tt_
                    qkeys = bk(qd_, 0) + bk(qd_, 1)
                    ss_ap, ss_keys = sumsq_1024(Q[qd_][:, :], qkeys, pgE)
                    r_ap, r_keys = rstd_from_ss(ss_ap, ss_keys, 1024, EPS)
                    STT(tmpE, Q[qd_][:, :], r_ap, gB, ALU.mult, ALU.mult, qkeys + r_keys + gB_keys, tmpE_keys)
                    TT(PXT[:, pxb, :], PXT[:, pxb, :], tmpE, ALU.add, pxkeys(pxb) + tmpE_keys, pxkeys(pxb))
                    DMA("sp", y[128 * ti_:128 * (ti_ + 1), :], PXT[:, pxb, :], pxkeys(pxb), [], ("xs", pxb))
                return post

            for gi, tl in enumerate(egroups):
                Wg = 128 * len(tl)
                hb_i = gi % 2
                act = []
                for j in range(22):
                    if j in (1, 3) and pending:
                        pending.pop(0)()
                    if j == 4 and gi + 1 < len(egroups):
                        e_load(gi + 1)
                    if j == 10 and gi + 1 < len(egroups):
                        e_norm_front(gi + 1)
                    if j == 17 and gi + 1 < len(egroups):
                        e_norm_back(gi + 1)
                    gu_ap, g_k = pgu()
                    g_ap, u_ap, u_k = gu_ap[:, 0:256], gu_ap[:, 256:512], g_k
                    pi, po = j // 4, 128 * (j % 4)
                    MM(g_ap[:, 0:Wg], [(w_gp[pi][:, k, po:po + 128], h2T[hb_i][:, k, 0:Wg]) for k in range(8)],
                       k_gp[pi] + h2T_keys[hb_i], g_k)
                    MM(u_ap[:, 0:Wg], [(w_upp[pi][:, k, po:po + 128], h2T[hb_i][:, k, 0:Wg]) for k in range(8)],
                       k_upp[pi] + h2T_keys[hb_i], u_k)
                    sg_h = hpE.alloc()
                    ACT(sg_h.f[:, 0:Wg], g_ap[:, 0:Wg], AF.Silu, g_k, sg_h.keys)
                    if j % 2 == 0:
                        ah = hpE.alloc()
                        act.append(ah)
                    ah = act[j // 2]
                    TT(ah.b[:, 256 * (j % 2):256 * (j % 2) + Wg], sg_h.f[:, 0:Wg], u_ap[:, 0:Wg], ALU.mult, sg_h.keys + u_k, ah.keys)
                    hpE.free(sg_h)
                while pending:
                    pending.pop(0)()
                akeys_all = []
                for a_ in act:
                    akeys_all += a_.keys
                for tt, ti in enumerate(tl):
                    qd_ = 2 + tt
                    for hf in range(2):
                        MM(bank(qd_, hf),
                           [(act[j // 2].b[:, 256 * (j % 2) + 128 * tt:256 * (j % 2) + 128 * tt + 128],
                             w_dn[j // 2][:, j % 2, 512 * hf:512 * (hf + 1)]) for j in range(22)],
                           akeys_all + [k for kk in k_dn for k in kk], bk(qd_, hf))
                    pending.append(make_post(gi, tt, ti))
                for a_ in act:
                    hpE.free(a_)
            while pending:
                pending.pop(0)()

        P.emit(st)
        mark('end')
        info = dict(n_ops=len(P.ops), n_waits=P.n_waits, n_dma_sems=P.n_dma_sems, marks=marks)
    return nc, dbg_outs, info


def _consts():
    bf = ml_dtypes.bfloat16
    c = {}
    c["c_identb"] = np.eye(128, dtype=np.float32).astype(bf)
    c["c_identf"] = np.eye(128, dtype=np.float32)
    sp = np.zeros((128, 512), np.float32); sp[:, ::32] = 1.0
    ss = np.zeros((128, 128), np.float32); ss[:, ::8] = 1.0
    c["c_startp"], c["c_starts"] = sp, ss
    s = np.arange(128)[:, None]; t = np.arange(128)[None, :]
    c["c_causp"] = ((s // 32 == t // 32) & (s <= t)).astype(np.float32)
    c["c_causs"] = ((s // 8 == t // 8) & (s <= t)).astype(np.float32)
    c["c_crowp"] = (s // 32 == np.arange(4)[None, :]).astype(np.float32)
    c["c_crows"] = (s // 8 == np.arange(16)[None, :]).astype(np.float32)
    sm = (np.arange(4)[:, None] == (np.arange(32)[None, :] // 8)).astype(np.float32)
    c["c_seqm"] = np.ascontiguousarray(np.broadcast_to(sm.reshape(1, 128), (128, 128))).astype(bf)
    return c


def make_in_maps(inp):
    f = np.float32
    shared = {
        "w_in": np.ascontiguousarray(inp["w_in"][0], f),
        "w_conv_out": np.ascontiguousarray(inp["w_conv_out"][0], f),
        "w_hg_out": np.ascontiguousarray(inp["w_hg_out"][0], f),
        "w_mem_out": np.ascontiguousarray(inp["w_mem_out"][0], f),
        "w_mem_kv": np.ascontiguousarray(inp["w_mem_kv"][0], f),
        "w_out": np.ascontiguousarray(inp["w_out"][0], f),
        "w_gate": np.ascontiguousarray(inp["w_ffn_gate"][0], f),
        "w_up": np.ascontiguousarray(inp["w_ffn_up"][0], f),
        "w_down": np.ascontiguousarray(inp["w_ffn_down"][0], f),
        "vec36": np.ascontiguousarray(np.concatenate(
            [inp["conv_w"][0], inp["conv_b"], inp["conv_ln_g"], inp["conv_ln_b"], inp["hg_lb_logits"]], axis=0), f),
        "hgn": np.ascontiguousarray(inp["hg_norm_g"][0].reshape(128, 1), f),
        "gains": np.ascontiguousarray(np.concatenate(
            [inp["norm_pre_mix"], inp["norm_post_mix"], inp["norm_pre_ffn"], inp["norm_post_ffn"], inp["mem_norm_g"]], axis=0), f),
    }
    shared.update(_consts())
    maps = []
    for b in range(NCORES):
        m = dict(shared)
        m["x_all"] = np.ascontiguousarray(np.concatenate(
            [inp["x_prompt"][b], inp["x_sample"][16 * b:16 * b + 16].reshape(128, 1024)], axis=0), f)
        m["mem"] = np.ascontiguousarray(inp["mem_prompt"][b], f)
        m["sconv"] = np.ascontiguousarray(inp["state_conv"][0, 16 * b:16 * b + 16], f)
        m["shg"] = np.ascontiguousarray(inp["state_hgrn"][0, 16 * b:16 * b + 16], f)
        m["ck"] = np.ascontiguousarray(inp["cache_mem_k"][0, 16 * b:16 * b + 16].reshape(16, 256, 512), f)
        m["cv"] = np.ascontiguousarray(inp["cache_mem_v"][0, 16 * b:16 * b + 16].reshape(16, 256, 512), f)
        maps.append(m)
    return maps


_CACHE = {}


def kernel(**inp):
    if "nc" not in _CACHE:
        _CACHE["nc"] = build_program()[0]
    nc = _CACHE["nc"]
    maps = make_in_maps(inp)
    res = run_bass_kernel_spmd(nc, maps, core_ids=list(range(NCORES)))
    R = res.results
    yp = np.stack([R[b]["y"][:2048] for b in range(NCORES)], 0)
    ys = np.concatenate([R[b]["y"][2048:].reshape(16, 8, 1024) for b in range(NCORES)], 0)
    ncp = np.stack([R[b]["o_ncp"] for b in range(NCORES)], 0)[None]
    nhp = np.stack([R[b]["o_nhp"] for b in range(NCORES)], 0)[None]
    mk = np.stack([R[b]["o_mk"].reshape(256, 4, 128) for b in range(NCORES)], 0)[None]
    mv = np.stack([R[b]["o_mv"].reshape(256, 4, 128) for b in range(NCORES)], 0)[None]
    ncs = np.concatenate([R[b]["o_ncs"] for b in range(NCORES)], 0)[None]
    nhs = np.concatenate([R[b]["o_nhs"] for b in range(NCORES)], 0)[None]
    return tuple(np.ascontiguousarray(a, np.float32) for a in (yp, ys, ncp, nhp, mk, mv, ncs, nhs))
```
